# Optimizing a Trainium2 kernel written in Bass

```python
import math
import jax, jax.numpy as jnp
from jax import lax
import numpy as np

D_MODEL = 2048
BATCH = 8
SEQ = 2048
DEPTH = 2

N_EVEN = (DEPTH + 1) // 2
N_ODD = DEPTH // 2
EPS = 1e-6

HG_HEADS = 8
HG_DIM = 128
HG_WIDTH = HG_HEADS * HG_DIM
HG_CHUNK = 64

MB_HEADS = 8
MB_DIM = 128
MB_WIDTH = MB_HEADS * MB_DIM
MB_BLOCK = 256
MB_TOPK = 3
MB_Q_TILE = 8

IN0_SIZES = [HG_WIDTH, HG_WIDTH, HG_WIDTH, HG_WIDTH, MB_WIDTH, MB_WIDTH, MB_WIDTH]
IN0_COLS = sum(IN0_SIZES)
IN0_SPLITS = [int(s) for s in np.cumsum(IN0_SIZES)[:-1]]
MIX0_WIDTH = HG_WIDTH + MB_WIDTH

SSM_INNER = 2 * D_MODEL
SSM_HEADDIM = 64
SSM_HEADS = SSM_INNER // SSM_HEADDIM
SSM_STATE = 128
SSM_GROUPS = 8
SSM_HPG = SSM_HEADS // SSM_GROUPS
SSM_CONV = 4
SSM_CHUNK = 256
SSM_CONV_CH = SSM_INNER + 2 * SSM_GROUPS * SSM_STATE
IN1_COLS = SSM_INNER + SSM_CONV_CH + SSM_HEADS

D_FF = -(-8 * D_MODEL // (3 * 256)) * 256

kernel_name = "hgrn2_moba_mamba2_hybrid"


def rms_norm(x, w):
    xf = x.astype(jnp.float32)
    y = xf * lax.rsqrt(jnp.mean(xf * xf, axis=-1, keepdims=True) + EPS)
    return (y * w.astype(jnp.float32)).astype(x.dtype)


def pad_seq(a, mult, axis):
    pad = (-a.shape[axis]) % mult
    if pad == 0:
        return a
    widths = [(0, 0)] * a.ndim
    widths[axis] = (0, pad)
    return jnp.pad(a, widths)


def hgrn2_mixer(q, f_raw, i, g, lb, norm_w):
    B_, S_, _ = q.shape
    f32 = jnp.float32
    q = jax.nn.silu(q.astype(f32))
    fr = f_raw.astype(f32)
    log_f = jnp.log(lb + (1.0 - lb) * jax.nn.sigmoid(fr))
    k = (1.0 - lb) * jax.nn.sigmoid(-fr)
    v = i.astype(f32)

    def to_chunks(a):
        a = pad_seq(a, HG_CHUNK, 1)
        nc = a.shape[1] // HG_CHUNK
        return a.reshape(B_, nc, HG_CHUNK, HG_HEADS, HG_DIM).transpose(1, 0, 3, 2, 4)

    qc, kc, vc, gc = to_chunks(q), to_chunks(k), to_chunks(v), to_chunks(log_f)
    causal = jnp.tril(jnp.ones((HG_CHUNK, HG_CHUNK), dtype=bool))

    def step(state, inp):
        qb, kb, vb, gb = inp
        b = jnp.cumsum(gb, axis=2)
        diff = b[:, :, :, None, :] - b[:, :, None, :, :]
        decay = jnp.exp(jnp.where(causal[:, :, None], diff, -jnp.inf))
        attn = jnp.einsum('bhtd,bhsd,bhtsd->bhts', qb, kb, decay)
        o_intra = jnp.einsum('bhts,bhsv->bhtv', attn, vb)
        o_inter = jnp.einsum('bhtd,bhdv->bhtv', qb * jnp.exp(b), state)
        b_last = b[:, :, -1]
        k_dec = kb * jnp.exp(b_last[:, :, None, :] - b)
        new_state = jnp.exp(b_last)[..., None] * state + jnp.einsum('bhsd,bhsv->bhdv', k_dec, vb)
        return new_state, o_intra + o_inter

    s0 = jnp.zeros((B_, HG_HEADS, HG_DIM, HG_DIM), f32)
    _, o = lax.scan(step, s0, (qc, kc, vc, gc))
    o = o.transpose(1, 0, 3, 2, 4).reshape(B_, -1, HG_HEADS, HG_DIM)[:, :S_]
    o = rms_norm(o, norm_w) * jax.nn.silu(g.astype(f32)).reshape(B_, S_, HG_HEADS, HG_DIM)
    return o.reshape(B_, S_, HG_WIDTH).astype(q.dtype)


def moba_mixer(q, k, v, q_norm_w, k_norm_w):
    B_, S_, _ = q.shape
    f32 = jnp.float32
    q = rms_norm(q.astype(f32).reshape(B_, S_, MB_HEADS, MB_DIM), q_norm_w).transpose(0, 2, 1, 3)
    k = rms_norm(k.astype(f32).reshape(B_, S_, MB_HEADS, MB_DIM), k_norm_w).transpose(0, 2, 1, 3)
    v = v.astype(f32).reshape(B_, S_, MB_HEADS, MB_DIM).transpose(0, 2, 1, 3)
    k_p = pad_seq(k, MB_BLOCK, 2)
    v_p = pad_seq(v, MB_BLOCK, 2)
    nb = k_p.shape[2] // MB_BLOCK
    top_k = min(MB_TOPK, nb)
    kb = k_p.reshape(B_, MB_HEADS, nb, MB_BLOCK, MB_DIM)
    vb = v_p.reshape(B_, MB_HEADS, nb, MB_BLOCK, MB_DIM)
    k_mean = jnp.mean(kb, axis=3)
    scale = MB_DIM ** -0.5
    bi = jnp.arange(B_)[:, None, None, None]
    hi = jnp.arange(MB_HEADS)[None, :, None, None]
    blk_ids = jnp.arange(nb)

    def tile(start):
        qt = lax.dynamic_slice_in_dim(q, start, MB_Q_TILE, axis=2)
        j = start // MB_BLOCK
        gate = jnp.einsum('bhtd,bhnd->bhtn', qt, k_mean)
        gate = jnp.where(blk_ids < j, gate, -jnp.inf)
        _, idx = lax.top_k(gate, top_k)
        valid = jnp.arange(top_k) < j
        kg = kb[bi, hi, idx]
        vg = vb[bi, hi, idx]
        s_sel = jnp.einsum('bhtd,bhtkjd->bhtkj', qt, kg) * scale
        s_sel = jnp.where(valid[:, None], s_sel, -jnp.inf).reshape(B_, MB_HEADS, MB_Q_TILE, top_k * MB_BLOCK)
        k_own = lax.dynamic_slice_in_dim(k_p, j * MB_BLOCK, MB_BLOCK, axis=2)
        v_own = lax.dynamic_slice_in_dim(v_p, j * MB_BLOCK, MB_BLOCK, axis=2)
        s_own = jnp.einsum('bhtd,bhjd->bhtj', qt, k_own) * scale
        q_pos = start + jnp.arange(MB_Q_TILE)
        k_pos = j * MB_BLOCK + jnp.arange(MB_BLOCK)
        s_own = jnp.where(k_pos[None, :] <= q_pos[:, None], s_own, -jnp.inf)
        p = jax.nn.softmax(jnp.concatenate([s_sel, s_own], axis=-1), axis=-1)
        p_sel = p[..., : top_k * MB_BLOCK].reshape(B_, MB_HEADS, MB_Q_TILE, top_k, MB_BLOCK)
        p_own = p[..., top_k * MB_BLOCK:]
        return (jnp.einsum('bhtkj,bhtkjd->bhtd', p_sel, vg)
                + jnp.einsum('bhtj,bhjd->bhtd', p_own, v_own))

    starts = jnp.arange(0, S_, MB_Q_TILE)
    out = lax.map(tile, starts)
    out = out.transpose(1, 0, 3, 2, 4).reshape(B_, S_, MB_WIDTH)
    return out


def ssd_scan(x, dt, A, Bm, Cm):
    B_, S_ = x.shape[:2]
    xdt = pad_seq(x * dt[..., None], SSM_CHUNK, 1)
    a = pad_seq(dt * A, SSM_CHUNK, 1)
    Bp = pad_seq(Bm, SSM_CHUNK, 1)
    Cp = pad_seq(Cm, SSM_CHUNK, 1)
    nc = xdt.shape[1] // SSM_CHUNK
    L = SSM_CHUNK
    xdt = jnp.moveaxis(xdt.reshape(B_, nc, L, SSM_GROUPS, SSM_HPG, SSM_HEADDIM), 1, 0)
    a = jnp.moveaxis(a.reshape(B_, nc, L, SSM_GROUPS, SSM_HPG), 1, 0)
    Bp = jnp.moveaxis(Bp.reshape(B_, nc, L, SSM_GROUPS, SSM_STATE), 1, 0)
    Cp = jnp.moveaxis(Cp.reshape(B_, nc, L, SSM_GROUPS, SSM_STATE), 1, 0)
    causal = jnp.tril(jnp.ones((L, L), dtype=bool))

    def step(state, inp):
        xc, ac, Bc, Cc = inp
        a_cs = jnp.cumsum(ac.transpose(0, 2, 3, 1), axis=-1)
        seg = a_cs[..., :, None] - a_cs[..., None, :]
        Lmat = jnp.exp(jnp.where(causal, seg, -jnp.inf))
        cb = jnp.einsum('btgn,bsgn->bgts', Cc, Bc)
        y_intra = jnp.einsum('bgrts,bsgrp->btgrp', cb[:, :, None] * Lmat, xc)
        y_inter = jnp.einsum('btgn,bgrpn,bgrt->btgrp', Cc, state, jnp.exp(a_cs))
        a_last = a_cs[..., -1]
        dec = jnp.exp(a_last[..., None] - a_cs)
        new_state = (jnp.exp(a_last)[..., None, None] * state
                     + jnp.einsum('bgrs,bsgrp,bsgn->bgrpn', dec, xc, Bc))
        return new_state, y_intra + y_inter

    s0 = jnp.zeros((B_, SSM_GROUPS, SSM_HPG, SSM_HEADDIM, SSM_STATE), jnp.float32)
    _, y = lax.scan(step, s0, (xdt, a, Bp, Cp))
    y = jnp.moveaxis(y, 0, 1).reshape(B_, nc * L, SSM_HEADS, SSM_HEADDIM)[:, :S_]
    return y


def mamba2_mixer(h, w_in, conv_w, conv_b, dt_bias, a_log, d_skip, norm_w, w_out):
    B_, S_, _ = h.shape
    f32 = jnp.float32
    zxbcdt = h @ w_in
    z, xbc, dt = jnp.split(zxbcdt, [SSM_INNER, SSM_INNER + SSM_CONV_CH], axis=-1)
    xbc = lax.conv_general_dilated(
        xbc, conv_w.reshape(SSM_CONV, 1, SSM_CONV_CH), window_strides=(1,),
        padding=[(SSM_CONV - 1, 0)], dimension_numbers=('NWC', 'WIO', 'NWC'),
        feature_group_count=SSM_CONV_CH)
    xbc = jax.nn.silu((xbc + conv_b).astype(f32))
    xs, Bm, Cm = jnp.split(xbc, [SSM_INNER, SSM_INNER + SSM_GROUPS * SSM_STATE], axis=-1)
    xs = xs.reshape(B_, S_, SSM_HEADS, SSM_HEADDIM)
    Bm = Bm.reshape(B_, S_, SSM_GROUPS, SSM_STATE)
    Cm = Cm.reshape(B_, S_, SSM_GROUPS, SSM_STATE)
    dt = jax.nn.softplus(dt.astype(f32) + dt_bias.astype(f32))
    A = -jnp.exp(a_log.astype(f32))
    y = ssd_scan(xs, dt, A, Bm, Cm) + d_skip.astype(f32)[:, None] * xs
    gy = (y.reshape(B_, S_, SSM_INNER) * jax.nn.silu(z.astype(f32))).reshape(B_, S_, SSM_GROUPS, -1)
    gy = gy * lax.rsqrt(jnp.mean(gy * gy, axis=-1, keepdims=True) + EPS)
    gy = (gy.reshape(B_, S_, SSM_INNER) * norm_w.astype(f32)).astype(h.dtype)
    return gy @ w_out


def swiglu_ffn(h, w_gate, w_up, w_down):
    return (jax.nn.silu(h @ w_gate) * (h @ w_up)) @ w_down


def setup_inputs(seed: int = 0) -> dict:
    key = jax.random.key(seed)
    ks = jax.random.split(key, 24)
    f32 = jnp.float32

    def nrm(k, shape, fan_in):
        return jax.random.normal(k, shape, f32) * fan_in ** -0.5

    def gain(k, shape):
        return 1.0 + 0.02 * jax.random.normal(k, shape, f32)

    dt0 = jnp.exp(jax.random.uniform(ks[15], (N_ODD, SSM_HEADS), f32,
                                     math.log(1e-3), math.log(1e-1)))
    return {
        "x": jax.random.normal(ks[0], (BATCH, SEQ, D_MODEL), f32),
        "ln_mix": gain(ks[1], (DEPTH, D_MODEL)),
        "ln_ffn": gain(ks[2], (DEPTH, D_MODEL)),
        "w_in_even": nrm(ks[3], (N_EVEN, D_MODEL, IN0_COLS), D_MODEL),
        "w_out_even": nrm(ks[4], (N_EVEN, MIX0_WIDTH, D_MODEL), MIX0_WIDTH),
        "hgrn_lb": 0.1 * jax.random.normal(ks[5], (N_EVEN + 1, HG_WIDTH), f32),
        "hgrn_norm": gain(ks[6], (N_EVEN, HG_DIM)),
        "q_norm": gain(ks[7], (N_EVEN, MB_DIM)),
        "k_norm": gain(ks[8], (N_EVEN, MB_DIM)),
        "w_in_ssm": nrm(ks[9], (N_ODD, D_MODEL, IN1_COLS), D_MODEL),
        "conv_w": nrm(ks[10], (N_ODD, SSM_CONV, SSM_CONV_CH), SSM_CONV),
        "conv_b": 0.02 * jax.random.normal(ks[11], (N_ODD, SSM_CONV_CH), f32),
        "dt_bias": dt0 + jnp.log(-jnp.expm1(-dt0)),
        "a_log": jnp.log(jax.random.uniform(ks[12], (N_ODD, SSM_HEADS), f32, 1.0, 16.0)),
        "d_skip": gain(ks[13], (N_ODD, SSM_HEADS)),
        "ssm_norm": gain(ks[14], (N_ODD, SSM_INNER)),
        "w_out_ssm": nrm(ks[16], (N_ODD, SSM_INNER, D_MODEL), SSM_INNER),
        "w_gate": nrm(ks[17], (DEPTH, D_MODEL, D_FF), D_MODEL),
        "w_up": nrm(ks[18], (DEPTH, D_MODEL, D_FF), D_MODEL),
        "w_down": nrm(ks[19], (DEPTH, D_FF, D_MODEL), D_FF),
    }


def reference(x, ln_mix, ln_ffn, w_in_even, w_out_even, hgrn_lb, hgrn_norm, q_norm, k_norm,
              w_in_ssm, conv_w, conv_b, dt_bias, a_log, d_skip, ssm_norm, w_out_ssm,
              w_gate, w_up, w_down):
    lb_all = jnp.cumsum(jax.nn.softmax(hgrn_lb.astype(jnp.float32), axis=0), axis=0)
    for layer in range(DEPTH):
        h = rms_norm(x, ln_mix[layer])
        if layer % 2 == 0:
            e = layer // 2
            proj = h @ w_in_even[e]
            hq, hf, hi, hg, mq, mk, mv = jnp.split(proj, IN0_SPLITS, axis=-1)
            o_a = hgrn2_mixer(hq, hf, hi, hg, lb_all[e], hgrn_norm[e])
            o_b = moba_mixer(mq, mk, mv, q_norm[e], k_norm[e]).astype(h.dtype)
            x = x + jnp.concatenate([o_a, o_b], axis=-1) @ w_out_even[e]
        else:
            o = layer // 2
            x = x + mamba2_mixer(h, w_in_ssm[o], conv_w[o], conv_b[o], dt_bias[o], a_log[o],
                                 d_skip[o], ssm_norm[o], w_out_ssm[o])
        h = rms_norm(x, ln_ffn[layer])
        x = x + swiglu_ffn(h, w_gate[layer], w_up[layer], w_down[layer])
    return x
```

```python
import numpy as np
import concourse.bass as bass
import concourse.mybir as mybir
from concourse.alu_op_type import AluOpType as ALU
from concourse.bass_utils import run_bass_kernel_spmd

F32 = mybir.dt.float32
BF16 = mybir.dt.bfloat16
AF = mybir.ActivationFunctionType
AX = mybir.AxisListType

T = 2048
D = 2048
KC = 16
DFF = 5632
EPS = 1e-6
NEG = -30000.0


class Res:
    __slots__ = ("name", "w", "r", "dsem", "dcnt", "excl")

    def __init__(self, name="", excl=False):
        self.name = name
        self.excl = excl
        self.w = None
        self.r = []
        self.dsem = None
        self.dcnt = 0


class KB:
    ENGS = ("pe", "act", "dve", "pool", "sp")

    def __init__(self, nc):
        self.nc = nc
        self.q = {e: [] for e in self.ENGS}
        self.sem = {}
        self.cnt = {e: 0 for e in self.ENGS}
        self.seen = {e: {} for e in self.ENGS}
        self._ctx = []
        for e in self.ENGS:
            self.sem[e] = self._enter(nc.semaphore("es_" + e))
        self.nsem = 5
        self.dres = []
        self.ninst = {e: 0 for e in self.ENGS}

    def _enter(self, cm):
        v = cm.__enter__()
        self._ctx.append(cm)
        return v

    def mark(self):
        return len(self._ctx)

    def release(self, mark):
        while len(self._ctx) > mark:
            self._ctx.pop().__exit__(None, None, None)

    def sbuf(self, name, shape, dt):
        self.nsb = getattr(self, "nsb", 0) + 1
        return self._enter(self.nc.sbuf_tensor("%s_u%d" % (name, self.nsb), list(shape), dt))

    def psum(self, name, shape, dt):
        return self._enter(self.nc.psum_tensor(name, list(shape), dt))

    def new_sem(self, name):
        self.nsem += 1
        return self._enter(self.nc.semaphore(name))

    def close(self):
        self.release(0)

    def _deps(self, eng, reads, writes):
        deps = []
        own = self.sem[eng]
        for R in reads:
            if R.w is not None:
                deps.append(R.w)
            if R.excl:
                deps.extend(ev for ev in R.r if ev[0] is not own)
        for W in writes:
            if W.w is not None:
                deps.append(W.w)
            deps.extend(W.r)
        out = {}
        for (s, v) in deps:
            if eng == "pe" and s is own:
                continue
            k = id(s)
            if self.seen[eng].get(k, 0) >= v:
                continue
            if k not in out or out[k][1] < v:
                out[k] = (s, v)
        for k, (s, v) in out.items():
            self.seen[eng][k] = v
        return list(out.values())

    def op(self, eng, fn, reads=(), writes=()):
        waits = self._deps(eng, reads, writes)
        self.cnt[eng] += 1
        ev = (self.sem[eng], self.cnt[eng])
        sem = self.sem[eng]

        def emit(e, waits=waits, fn=fn, sem=sem):
            for (s, v) in waits:
                e.wait_ge(s, v)
            fn(e).then_inc(sem, 1)

        self.q[eng].append(emit)
        self.ninst[eng] += 1 + len(waits)
        for R in reads:
            R.r.append(ev)
        for W in writes:
            W.w = ev
            W.r = []
        return ev

    def dma(self, queue, out, in_, reads=(), writes=(), join=False, **kw):
        tgt = writes[0]
        if tgt.dsem is None:
            tgt.dsem = self.new_sem("ds_%d" % self.nsem)
            self.dres.append(tgt)
        if join:
            saved = [(W, W.w) for W in writes]
            for W in writes:
                W.w = None
        waits = self._deps(queue, reads, writes)
        if join:
            for W, w in saved:
                W.w = w
        tgt.dcnt += 16
        ev = (tgt.dsem, tgt.dcnt)
        dsem = tgt.dsem

        def emit(e, waits=waits, out=out, in_=in_, dsem=dsem, kw=kw):
            for (s, v) in waits:
                e.wait_ge(s, v)
            e.dma_start(out=out, in_=in_, **kw).then_inc(dsem, 16)

        self.q[queue].append(emit)
        self.ninst[queue] += 1 + len(waits)
        for R in reads:
            R.r.append(ev)
        for W in writes:
            W.w = ev
            if not join:
                W.r = []
        return ev

    def wait_all(self, eng, events):
        waits = []
        for ev in events:
            if ev is None:
                continue
            s, v = ev
            if self.seen[eng].get(id(s), 0) < v:
                self.seen[eng][id(s)] = v
                waits.append(ev)

        def emit(e, waits=waits):
            for (s, v) in waits:
                e.wait_ge(s, v)

        self.q[eng].append(emit)

    def barrier(self):
        evs = [(self.sem[e], self.cnt[e]) for e in self.ENGS if self.cnt[e] > 0]
        evs += [(r.dsem, r.dcnt) for r in self.dres if r.dcnt > 0]
        for e in self.ENGS:
            self.wait_all(e, [ev for ev in evs if ev[0] is not self.sem[e]])

    def finish(self):
        nc = self.nc
        q = self.q
        with nc.Block() as block:
            @block.tensor
            def _(e):
                for f in q["pe"]:
                    f(e)

            @block.scalar
            def _(e):
                for f in q["act"]:
                    f(e)

            @block.vector
            def _(e):
                for f in q["dve"]:
                    f(e)

            @block.gpsimd
            def _(e):
                for f in q["pool"]:
                    f(e)

            @block.sync
            def _(e):
                for f in q["sp"]:
                    f(e)


def _cst_layout():
    o = {}
    p = 0
    for name, n in (("ident", 128), ("ones", 128), ("bdmask", 128), ("cmaskA", 256), ("cmaskB", 256),
                    ("cmul0", 256), ("cmul1", 256), ("esel8", 1024), ("m64", 512), ("m256", 512)):
        o[name] = (p, n)
        p += n
    return o, p


CST, NCST = _cst_layout()
NCST_BF = CST["m64"][0]


def _make_cst():
    c = np.zeros((128, NCST), np.float32)
    s = np.arange(128)[:, None]
    t = np.arange(128)[None, :]
    tri = (s <= t).astype(np.float32)
    cneg = np.where(s <= t, 0.0, NEG).astype(np.float32)

    def put(name, a):
        o, n = CST[name]
        c[:, o:o + n] = a
    put("ident", np.eye(128, dtype=np.float32))
    put("ones", np.ones((128, 128), np.float32))
    put("bdmask", tri * ((s // 64) == (t // 64)))
    put("cmaskA", np.concatenate([cneg, np.zeros((128, 128), np.float32)], 1))
    put("cmaskB", np.concatenate([np.full((128, 128), NEG, np.float32), cneg], 1))
    put("cmul0", np.concatenate([tri, np.ones((128, 128), np.float32)], 1))
    put("cmul1", np.concatenate([np.zeros((128, 128), np.float32), tri], 1))
    e8 = np.zeros((128, 8, 128), np.float32)
    for n in range(8):
        e8[n, n, :] = 1.0
    put("esel8", e8.reshape(128, 1024))
    tt = np.arange(512)
    put("m64", np.broadcast_to((tt % 64 != 0).astype(np.float32)[None, :], (128, 512)))
    put("m256", np.broadcast_to((tt % 256 != 0).astype(np.float32)[None, :], (128, 512)))
    return c


def _vec_layout():
    o = {}
    p = 0
    for name, n in (("ln_mix0", 16), ("ln_ffn0", 16), ("ln_mix1", 16), ("ln_ffn1", 16), ("lb0", 8), ("lb1", 8),
                    ("hgn", 1), ("qn", 1), ("kn", 1), ("convw", 192), ("convb", 48), ("dtb", 1), ("alog", 1),
                    ("dskip", 64), ("ssmn", 32)):
        o[name] = (p, n)
        p += n
    return o, p


VEC, NVEC = _vec_layout()


def _make_vec(inp):
    v = np.zeros((128, NVEC), np.float32)

    def put(name, a):
        o, n = VEC[name]
        v[:a.shape[0], o:o + n] = a

    def col(w, n):
        return np.ascontiguousarray(w.reshape(n, 128).T)
    put("ln_mix0", col(inp["ln_mix"][0], 16))
    put("ln_ffn0", col(inp["ln_ffn"][0], 16))
    put("ln_mix1", col(inp["ln_mix"][1], 16))
    put("ln_ffn1", col(inp["ln_ffn"][1], 16))
    put("lb0", col(inp["hgrn_lb"][0], 8))
    put("lb1", col(inp["hgrn_lb"][1], 8))
    put("hgn", inp["hgrn_norm"][0].reshape(128, 1))
    put("qn", inp["q_norm"][0].reshape(128, 1))
    put("kn", inp["k_norm"][0].reshape(128, 1))
    put("convw", np.ascontiguousarray(inp["conv_w"][0].reshape(4, 48, 128).transpose(2, 1, 0)).reshape(128, 192))
    put("convb", col(inp["conv_b"][0], 48))
    put("dtb", inp["dt_bias"][0].reshape(64, 1))
    put("alog", inp["a_log"][0].reshape(64, 1))
    put("dskip", np.broadcast_to(inp["d_skip"][0].reshape(1, 64), (128, 64)))
    put("ssmn", col(inp["ssm_norm"][0], 32))
    return v


class Ctx:
    def __init__(self, nc):
        self.nc = nc
        self.kb = KB(nc)
        kb = self.kb
        self.banks = [kb.psum("bank%d" % i, [128, 512], F32) for i in range(8)]
        self.bres = [Res("bank%d" % i, excl=True) for i in range(8)]
        self.cbf = kb.sbuf("cbf", [128, NCST_BF], BF16)
        self.identf = kb.sbuf("identf", [128, 128], F32)
        self.onesf = kb.sbuf("onesf", [128, 128], F32)
        self.vec = kb.sbuf("vec_sb", [128, NVEC], F32)
        self.mscan = kb.sbuf("mscan", [128, 512], F32)
        self.r_const = Res("const")
        self.r_mscan = Res("mscan")
        self.NW = 3
        self.wbuf = None
        self.wi = 0
        self.wgen = 0

    def alloc_w(self, nelem):
        self.wgen += 1
        self.wbuf = [self.kb.sbuf("wbuf%d_%d" % (self.wgen, i), [128, nelem], BF16) for i in range(self.NW)]
        self.wres = [Res("wbuf%d" % i) for i in range(self.NW)]
        self.wi = 0

    def cb(self, name, rows=128):
        o, n = CST[name]
        return self.cbf[0:rows, o:o + n]

    def v(self, name, rows=128, c0=0, n=None):
        o, nn = VEC[name]
        if n is None:
            n = nn - c0
        return self.vec[0:rows, o + c0:o + c0 + n]

    def load_consts(self, cst, vec):
        kb = self.kb
        kb.dma("pool", self.cbf[:], cst[:, 0:NCST_BF], writes=[self.r_const])
        o, n = CST["ident"]
        kb.dma("sp", self.identf[:], cst[:, o:o + n], writes=[self.r_const], join=True)
        o, n = CST["ones"]
        kb.dma("sp", self.onesf[:], cst[:, o:o + n], writes=[self.r_const], join=True)
        kb.dma("sp", self.vec[:], vec, writes=[self.r_const], join=True)

    def load_mscan(self, cst, name):
        o, n = CST[name]
        self.kb.dma("sp", self.mscan[:], cst[:, o:o + n], writes=[self.r_mscan])

    def mm(self, out, lhsT, rhs, start, stop, reads, writes):
        return self.kb.op("pe", lambda e: e.matmul(out, lhsT, rhs, start=start, stop=stop), reads=reads, writes=writes)

    def tr(self, out, in_, ident, reads, writes):
        return self.kb.op("pe", lambda e: e.transpose(out, in_, ident), reads=reads, writes=writes)

    def act(self, out, in_, func, reads, writes, bias=None, scale=None, accum_out=None):
        kw = {}
        if bias is not None:
            kw["bias"] = bias
        if scale is not None:
            kw["scale"] = scale
        if accum_out is not None:
            kw["accum_out"] = accum_out
        return self.kb.op("act", lambda e: e.activation(out=out, in_=in_, func=func, **kw), reads=reads, writes=writes)

    def ts(self, out, in0, s1, s2, op0, op1, reads, writes, eng="dve"):
        if s2 is None:
            return self.kb.op(eng, lambda e: e.tensor_scalar(out, in0, s1, None, op0), reads=reads, writes=writes)
        return self.kb.op(eng, lambda e: e.tensor_scalar(out, in0, s1, s2, op0, op1), reads=reads, writes=writes)

    def tt(self, out, in0, in1, op, reads, writes, eng="dve"):
        return self.kb.op(eng, lambda e: e.tensor_tensor(out, in0, in1, op), reads=reads, writes=writes)

    def stt(self, out, in0, scalar, in1, op0, op1, reads, writes, eng="dve"):
        return self.kb.op(eng, lambda e: e.scalar_tensor_tensor(out, in0, scalar, in1, op0, op1), reads=reads, writes=writes)

    def copy(self, out, in_, reads, writes, eng="dve"):
        return self.kb.op(eng, lambda e: e.tensor_copy(out, in_), reads=reads, writes=writes)

    def recip(self, out, in_, reads, writes):
        return self.kb.op("dve", lambda e: e.reciprocal(out, in_), reads=reads, writes=writes)

    def memset(self, ap, val, writes, eng="dve"):
        return self.kb.op(eng, lambda e: e.memset(ap, val), writes=writes)

    def load_w(self, w, c0, ncols, K):
        kb = self.kb
        i = self.wi
        self.wi = (self.wi + 1) % self.NW
        buf, res = self.wbuf[i], self.wres[i]
        nk = K // 128
        view = buf[:, 0:nk * ncols].rearrange("p (k n) -> p k n", n=ncols)
        src = w[:, c0:c0 + ncols].rearrange("(k p) n -> p k n", p=128)
        first = True
        for k0 in range(0, nk, 4):
            k1 = min(nk, k0 + 4)
            kb.dma("pool", view[:, k0:k1, :], src[:, k0:k1, :], writes=[res], join=not first)
            first = False
        return view, res


class WStream:
    def __init__(self, c, tiles, ahead=1):
        self.c = c
        self.tiles = tiles
        self.loaded = []
        self.ahead = ahead

    def get(self, i):
        while len(self.loaded) < min(len(self.tiles), i + 1 + self.ahead):
            w, c0, n, K = self.tiles[len(self.loaded)]
            self.loaded.append(self.c.load_w(w, c0, n, K))
        return self.loaded[i]


def rmsnorm_fm(c, src, wname, hT, r_hT, t0, tn, pfx):
    kb = c.kb
    m = kb.mark()
    ng = tn // 512
    xt = [kb.sbuf(pfx + "xt%d" % i, [128, tn], F32) for i in range(2)]
    r_xt = [Res() for _ in range(2)]
    sq = [kb.sbuf(pfx + "sq%d" % i, [128, tn], BF16) for i in range(2)]
    r_sq = [Res() for _ in range(2)]
    rstd = kb.sbuf(pfx + "rstd", [128, tn], F32)
    r_rstd = Res()
    ones = c.cb("ones")
    for kc in range(KC):
        b = kc % 2
        kb.dma("sp", xt[b][:], src[kc * 128:(kc + 1) * 128, t0:t0 + tn], writes=[r_xt[b]])
        c.act(sq[b][:], xt[b][:], AF.Square, [r_xt[b]], [r_sq[b]])
        for g in range(ng):
            c.mm(c.banks[g][:, :], ones, sq[b][:, g * 512:(g + 1) * 512], kc == 0, kc == KC - 1,
                 [r_sq[b], c.r_const], [c.bres[g]])
    for g in range(ng):
        c.act(rstd[:, g * 512:(g + 1) * 512], c.banks[g][:, :], AF.Sqrt, [c.bres[g]], [r_rstd], bias=EPS, scale=1.0 / D)
    c.recip(rstd[:], rstd[:], [r_rstd], [r_rstd])
    for kc in range(KC):
        b = kc % 2
        kb.dma("sp", xt[b][:], src[kc * 128:(kc + 1) * 128, t0:t0 + tn], writes=[r_xt[b]])
        c.stt(hT[:, kc, 0:tn], xt[b][:], c.v(wname, c0=kc, n=1), rstd[:], ALU.mult, ALU.mult,
              [r_xt[b], r_rstd, c.r_const], [r_hT])
    kb.barrier()
    kb.release(m)


def ffn_phase(c, src, dst, r_dst, lname, wg, wu, wd):
    kb = c.kb
    m = kb.mark()
    TN = 1024
    hT = kb.sbuf("f_hT", [128, KC, TN], BF16)
    r_hT = Res()
    actT = kb.sbuf("f_actT", [128, 44, TN], BF16)
    r_act = Res()
    sg = [kb.sbuf("f_sg%d" % i, [128, 512], F32) for i in range(2)]
    r_sg = [Res() for _ in range(2)]
    xr = [kb.sbuf("f_xr%d" % i, [128, TN], F32) for i in range(2)]
    r_xr = [Res() for _ in range(2)]
    c.alloc_w(44 * 128)
    for th in range(T // TN):
        t0 = th * TN
        rmsnorm_fm(c, src, lname, hT, r_hT, t0, TN, "fn%d_" % th)
        tiles = []
        for ft in range(44):
            tiles.append((wg, ft * 128, 128, D))
            tiles.append((wu, ft * 128, 128, D))
        for jo in range(16):
            tiles.append((wd, jo * 128, 128, DFF))
        ws = WStream(c, tiles)
        it = 0
        for ft in range(44):
            wgt, rg = ws.get(2 * ft)
            wut, ru = ws.get(2 * ft + 1)
            for g in range(TN // 512):
                bg = (it % 2) * 2
                bu = bg + 1
                sl = slice(g * 512, (g + 1) * 512)
                for kc in range(KC):
                    c.mm(c.banks[bg][:, :], wgt[:, kc, :], hT[:, kc, sl], kc == 0, kc == KC - 1, [rg, r_hT], [c.bres[bg]])
                for kc in range(KC):
                    c.mm(c.banks[bu][:, :], wut[:, kc, :], hT[:, kc, sl], kc == 0, kc == KC - 1, [ru, r_hT], [c.bres[bu]])
                s = it % 2
                c.act(sg[s][:], c.banks[bg][:, :], AF.Silu, [c.bres[bg]], [r_sg[s]])
                c.tt(actT[:, ft, sl], sg[s][:], c.banks[bu][:, :], ALU.mult, [r_sg[s], c.bres[bu]], [r_act])
                it += 1
        for jo in range(16):
            wdt, rd = ws.get(88 + jo)
            b = jo % 2
            kb.dma("sp", xr[b][:], src[jo * 128:(jo + 1) * 128, t0:t0 + TN], writes=[r_xr[b]])
            for g in range(TN // 512):
                bk = 4 + (it % 2)
                it += 1
                sl = slice(g * 512, (g + 1) * 512)
                for ft in range(44):
                    c.mm(c.banks[bk][:, :], wdt[:, ft, :], actT[:, ft, sl], ft == 0, ft == 43, [rd, r_act], [c.bres[bk]])
                c.tt(xr[b][:, sl], xr[b][:, sl], c.banks[bk][:, :], ALU.add, [c.bres[bk]], [r_xr[b]])
            kb.dma("sp", dst[jo * 128:(jo + 1) * 128, t0:t0 + TN], xr[b][:], reads=[r_xr[b]], writes=[r_dst], join=True)
    kb.barrier()
    kb.release(m)


def transpose_tok_to_fm(c, src_tok, r_src, ntile, dst_fn, bank0=6):
    ident = c.cb("ident")
    for q in range(0, ntile, 4):
        nj = min(4, ntile - q)
        bk = bank0 + ((q // 4) % 2)
        pb = c.banks[bk][:, :].bitcast(BF16)
        for j in range(nj):
            c.tr(pb[:, j * 128:(j + 1) * 128], src_tok[:, q + j, :], ident, [r_src, c.r_const], [c.bres[bk]])
        dst_fn(q, nj, pb[:, 0:nj * 128], c.bres[bk])


def l0_mixer(c, xT, x1T, r_x1T, w_in, w_out, cst, dbg=None):
    kb = c.kb
    m = kb.mark()
    hT = kb.sbuf("a_hT", [128, KC, T], BF16)
    r_hT = Res()
    rmsnorm_fm(c, xT, "ln_mix0", hT, r_hT, 0, T, "an_")
    catT = kb.sbuf("a_catT", [128, 16, T], BF16)
    r_cat = Res()
    c.alloc_w(16 * 128)
    c.load_mscan(cst, "m64")
    m2 = kb.mark()
    hgrn_heads(c, hT, r_hT, catT, r_cat, w_in)
    kb.barrier()
    kb.release(m2)
    m2 = kb.mark()
    moba_heads(c, hT, r_hT, catT, r_cat, w_in)
    kb.barrier()
    kb.release(m2)
    if dbg is not None:
        r_dbg = Res()
        for hh in range(16):
            kb.dma("pool", dbg[hh * 128:(hh + 1) * 128, :], catT[:, hh, :], reads=[r_cat], writes=[r_dbg], join=True)
    xr = [kb.sbuf("a_xr%d" % i, [128, T], F32) for i in range(2)]
    r_xr = [Res() for _ in range(2)]
    ws = WStream(c, [(w_out, jo * 128, 128, 2048) for jo in range(16)])
    it = 0
    for jo in range(16):
        wt, rw = ws.get(jo)
        b = jo % 2
        kb.dma("sp", xr[b][:], xT[jo * 128:(jo + 1) * 128, :], writes=[r_xr[b]])
        for g in range(4):
            bk = it % 4
            it += 1
            sl = slice(g * 512, (g + 1) * 512)
            for kc in range(16):
                c.mm(c.banks[bk][:, :], wt[:, kc, :], catT[:, kc, sl], kc == 0, kc == 15, [rw, r_cat], [c.bres[bk]])
            c.tt(xr[b][:, sl], xr[b][:, sl], c.banks[bk][:, :], ALU.add, [c.bres[bk]], [r_xr[b]])
        kb.dma("sp", x1T[jo * 128:(jo + 1) * 128, :], xr[b][:], reads=[r_xr[b]], writes=[r_x1T], join=True)
    kb.barrier()
    kb.release(m)


def hgrn_heads(c, hT, r_hT, catT, r_cat, w_in):
    kb = c.kb
    G = 512
    qA = kb.sbuf("h_qA", [128, T], BF16)
    qB = kb.sbuf("h_qB", [128, T], BF16)
    kT = kb.sbuf("h_kT", [128, T], BF16)
    ebl = kb.sbuf("h_ebl", [128, 32], F32)
    vtok = kb.sbuf("h_v", [128, 16, 128], BF16)
    sgt = kb.sbuf("h_sg", [128, 16, 128], BF16)
    oall = kb.sbuf("h_oall", [128, 16, 128], F32)
    osq = kb.sbuf("h_osq", [128, 16, 128], F32)
    ofin = kb.sbuf("h_ofin", [128, 16, 128], BF16)
    ss = kb.sbuf("h_ss", [128, 16], F32)
    lbv = kb.sbuf("h_lb", [128, 8], F32)
    oml = kb.sbuf("h_oml", [128, 8], F32)
    noml = kb.sbuf("h_noml", [128, 8], F32)
    r_qA, r_qB, r_kT, r_ebl, r_v, r_sg, r_oall, r_osq, r_ofin, r_ss, r_lb = [Res() for _ in range(11)]
    tmp = {}
    for nm in ("sig", "lf", "b", "enb", "qs"):
        tmp[nm] = [kb.sbuf("h_%s%d" % (nm, i), [128, G], F32) for i in range(1)] * 2
        tmp["r_" + nm] = [Res() for _ in range(1)] * 2
    S = [kb.sbuf("h_S%d" % i, [128, 128], F32) for i in range(2)]
    Sb = [kb.sbuf("h_Sb%d" % i, [128, 128], BF16) for i in range(2)]
    r_S = [Res() for _ in range(2)]
    r_Sb = [Res() for _ in range(2)]
    stmp = kb.sbuf("h_stmp", [128, 128], F32)
    r_stmp = Res()
    ATm = [kb.sbuf("h_ATm%d" % i, [128, 128], BF16) for i in range(2)]
    r_ATm = [Res() for _ in range(2)]
    ktok = [kb.sbuf("h_ktok%d" % i, [128, 128], BF16) for i in range(2)]
    r_ktok = [Res() for _ in range(2)]

    c.tt(lbv[:], c.v("lb0"), c.v("lb1"), ALU.subtract, [c.r_const], [r_lb])
    c.act(lbv[:], lbv[:], AF.Sigmoid, [r_lb], [r_lb])
    c.ts(oml[:], lbv[:], -1.0, 1.0, ALU.mult, ALU.add, [r_lb], [r_lb])
    c.ts(noml[:], oml[:], -1.0, None, ALU.mult, None, [r_lb], [r_lb])
    c.memset(qA[:], 0.0, [r_qA])
    c.memset(qB[:], 0.0, [r_qB])

    tiles = []
    for h in range(8):
        for blk in range(4):
            tiles.append((w_in, blk * 1024 + h * 128, 128, D))
    ws = WStream(c, tiles)
    ident = c.cb("ident")
    bdm = c.cb("bdmask")
    it = 0
    for h in range(8):
        wq, rq = ws.get(4 * h)
        wf, rf = ws.get(4 * h + 1)
        lb_h = lbv[:, h:h + 1]
        oml_h = oml[:, h:h + 1]
        noml_h = noml[:, h:h + 1]
        for g in range(4):
            s = g % 2
            sl = slice(g * G, (g + 1) * G)
            bq, bf = 0 + 2 * s, 1 + 2 * s
            for kc in range(KC):
                c.mm(c.banks[bf][:, :], wf[:, kc, :], hT[:, kc, sl], kc == 0, kc == KC - 1, [rf, r_hT], [c.bres[bf]])
            for kc in range(KC):
                c.mm(c.banks[bq][:, :], wq[:, kc, :], hT[:, kc, sl], kc == 0, kc == KC - 1, [rq, r_hT], [c.bres[bq]])
            sig, lf, bb, enb, qs = (tmp[n][s] for n in ("sig", "lf", "b", "enb", "qs"))
            r_sig, r_lf, r_b, r_enb, r_qs = (tmp["r_" + n][s] for n in ("sig", "lf", "b", "enb", "qs"))
            c.act(sig[:], c.banks[bf][:, :], AF.Sigmoid, [c.bres[bf]], [r_sig])
            c.act(qs[:], c.banks[bq][:, :], AF.Silu, [c.bres[bq]], [r_qs])
            c.act(lf[:], sig[:], AF.Ln, [r_sig, r_lb], [r_lf], bias=lb_h, scale=oml_h)
            kb.op("dve", lambda e, o=bb[:], d0=c.mscan[:, 0:G], d1=lf[:]: e.tensor_tensor_scan(o, d0, d1, 0.0, ALU.mult, ALU.add),
                  reads=[r_lf, c.r_mscan], writes=[r_b])
            c.act(lf[:], bb[:], AF.Exp, [r_b], [r_lf])
            c.act(enb[:], bb[:], AF.Exp, [r_b], [r_enb], scale=-1.0)
            c.ts(sig[:], sig[:], noml_h, oml_h, ALU.mult, ALU.add, [r_sig, r_lb], [r_sig])
            c.tt(kT[:, sl], sig[:], enb[:], ALU.mult, [r_sig, r_enb], [r_kT])
            ev4 = lf[:].rearrange("p (a two c) -> p a two c", two=2, c=64)
            qs4 = qs[:].rearrange("p (a two c) -> p a two c", two=2, c=64)
            qA4 = qA[:, sl].rearrange("p (a two c) -> p a two c", two=2, c=64)
            qB4 = qB[:, sl].rearrange("p (a two c) -> p a two c", two=2, c=64)
            c.tt(qA4[:, :, 0, :], qs4[:, :, 0, :], ev4[:, :, 0, :], ALU.mult, [r_qs, r_lf], [r_qA])
            c.tt(qB4[:, :, 1, :], qs4[:, :, 1, :], ev4[:, :, 1, :], ALU.mult, [r_qs, r_lf], [r_qB])
            eb3 = lf[:].rearrange("p (a c) -> p a c", c=64)
            c.copy(ebl[:, g * 8:(g + 1) * 8], eb3[:, :, 63], [r_lf], [r_ebl])
        wi_, ri = ws.get(4 * h + 2)
        wg_, rg = ws.get(4 * h + 3)
        for j in range(16):
            bk = 4 + (j % 2)
            tsl = slice(j * 128, (j + 1) * 128)
            for kc in range(KC):
                c.mm(c.banks[bk][:, 0:128], hT[:, kc, tsl], wi_[:, kc, :], kc == 0, kc == KC - 1, [ri, r_hT], [c.bres[bk]])
            for kc in range(KC):
                c.mm(c.banks[bk][:, 128:256], hT[:, kc, tsl], wg_[:, kc, :], kc == 0, kc == KC - 1, [rg, r_hT], [c.bres[bk]])
            c.copy(vtok[:, j, :], c.banks[bk][:, 0:128], [c.bres[bk]], [r_v])
            c.act(sgt[:, j, :], c.banks[bk][:, 128:256], AF.Silu, [c.bres[bk]], [r_sg])
        c.memset(S[0][:], 0.0, [r_S[0]])
        c.memset(Sb[0][:], 0.0, [r_Sb[0]])
        cur = 0
        for j in range(16):
            a = j % 2
            p0 = j * 128
            bAT, bKT, bO, bKV = 0, 1, 2 + (j % 2), 4 + (j % 2)
            c.mm(c.banks[bAT][:, 0:64], kT[:, p0:p0 + 128], qA[:, p0:p0 + 64], True, True, [r_kT, r_qA], [c.bres[bAT]])
            c.mm(c.banks[bAT][:, 64:128], kT[:, p0:p0 + 128], qB[:, p0 + 64:p0 + 128], True, True, [r_kT, r_qB], [c.bres[bAT]])
            c.tt(ATm[a][:], c.banks[bAT][:, 0:128], bdm, ALU.mult, [c.bres[bAT], c.r_const], [r_ATm[a]])
            pk = c.banks[bKT][:, :].bitcast(BF16)
            c.tr(pk[:, 0:128], kT[:, p0:p0 + 128], ident, [r_kT, c.r_const], [c.bres[bKT]])
            c.act(ktok[a][:], pk[:, 0:128], AF.Copy, [c.bres[bKT]], [r_ktok[a]])
            ob = c.banks[bO]
            c.mm(ob[:, 0:128], ATm[a][:], vtok[:, j, :], True, False, [r_ATm[a], r_v], [c.bres[bO]])
            c.mm(ob[:, 0:128], qA[:, p0:p0 + 128], Sb[cur][:], False, False, [r_qA, r_Sb[cur]], [c.bres[bO]])
            for half in range(2):
                nxt = 1 - cur
                kvb = c.banks[bKV]
                ps = slice(half * 64, half * 64 + 64)
                c.mm(kvb[:, half * 128:(half + 1) * 128], ktok[a][ps, :], vtok[ps, j, :], True, True,
                     [r_ktok[a], r_v], [c.bres[bKV]])
                c.tt(stmp[:], kvb[:, half * 128:(half + 1) * 128], S[cur][:], ALU.add, [c.bres[bKV], r_S[cur]], [r_stmp])
                ecol = ebl[:, 2 * j + half:2 * j + half + 1]
                c.ts(S[nxt][:], stmp[:], ecol, None, ALU.mult, None, [r_stmp, r_ebl], [r_S[nxt]])
                c.act(Sb[nxt][:], stmp[:], AF.Copy, [r_stmp, r_ebl], [r_Sb[nxt]], scale=ecol)
                cur = nxt
                if half == 0:
                    c.mm(ob[:, 0:128], qB[:, p0:p0 + 128], Sb[cur][:], False, True, [r_qB, r_Sb[cur]], [c.bres[bO]])
            c.act(oall[:, j, :], ob[:, 0:128], AF.Copy, [c.bres[bO]], [r_oall])
        c.tt(osq[:], oall[:], oall[:], ALU.mult, [r_oall], [r_osq])
        kb.op("dve", lambda e, o=ss[:], i=osq[:]: e.reduce_sum(o, i, axis=AX.X), reads=[r_osq], writes=[r_ss])
        c.act(ss[:], ss[:], AF.Sqrt, [r_ss], [r_ss], bias=EPS, scale=1.0 / 128)
        c.recip(ss[:], ss[:], [r_ss], [r_ss])
        c.tt(osq[:], oall[:], ss[:].unsqueeze(2).to_broadcast([128, 16, 128]), ALU.mult, [r_oall, r_ss], [r_osq])
        c.tt(ofin[:], osq[:], sgt[:], ALU.mult, [r_osq, r_sg], [r_ofin])

        def put(q0, nj, pb, rb, h=h):
            c.act(catT[:, h, q0 * 128:(q0 + nj) * 128], pb, AF.Copy, [rb, c.r_const], [r_cat], scale=c.v("hgn"))
        transpose_tok_to_fm(c, ofin, r_ofin, 16, put)


def moba_heads(c, hT, r_hT, catT, r_cat, w_in):
    kb = c.kb
    G = 512
    qnT = kb.sbuf("m_qnT", [128, T], BF16)
    knT = kb.sbuf("m_knT", [128, T], BF16)
    qnf = kb.sbuf("m_qnf", [128, T], F32)
    knf = kb.sbuf("m_knf", [128, T], F32)
    ksum = kb.sbuf("m_ksum", [128, 8], F32)
    vext = kb.sbuf("m_vext", [128, 16, 132], BF16)
    negT = kb.sbuf("m_negT", [8, T], BF16)
    mofin = kb.sbuf("m_ofin", [128, 16, 128], BF16)
    r_qnT, r_knT, r_qnf, r_knf, r_ksum, r_vext, r_negT, r_mofin = [Res() for _ in range(8)]
    raw = [kb.sbuf("m_raw%d" % i, [128, G], F32) for i in range(2)]
    sqb = [kb.sbuf("m_sq%d" % i, [128, G], BF16) for i in range(2)]
    rsb = [kb.sbuf("m_rs%d" % i, [128, G], F32) for i in range(2)]
    r_raw = [Res() for _ in range(2)]
    r_sqb = [Res() for _ in range(2)]
    r_rsb = [Res() for _ in range(2)]
    gm = kb.sbuf("m_gm", [128, 8], F32)
    mx = kb.sbuf("m_mx", [128, 8], F32)
    negm = kb.sbuf("m_negm", [128, 8], F32)
    negs = kb.sbuf("m_negs", [8, 128], F32)
    r_gm, r_mx, r_negm, r_negs = [Res() for _ in range(4)]
    PT = [kb.sbuf("m_PT%d" % i, [128, 256], BF16) for i in range(3)]
    r_PT = [Res() for _ in range(3)]
    rinv = kb.sbuf("m_rinv", [128, 2], F32)
    r_rinv = Res()
    c.memset(vext[:], 1.0, [r_vext])
    ones = c.cb("ones")
    identb = c.cb("ident")
    tiles = []
    for h in range(8):
        for blk in range(3):
            tiles.append((w_in, 4096 + blk * 1024 + h * 128, 128, D))
    ws = WStream(c, tiles)
    scale = 128 ** -0.5
    it = 0
    for h in range(8):
        for (wi3, wn, dstb, r_dstb, dstf, r_dstf) in ((3 * h, "qn", qnT, r_qnT, qnf, r_qnf), (3 * h + 1, "kn", knT, r_knT, knf, r_knf)):
            wt, rw = ws.get(wi3)
            for g in range(4):
                s = it % 2
                it += 1
                sl = slice(g * G, (g + 1) * G)
                bp, bs = 2 * s, 2 * s + 1
                for kc in range(KC):
                    c.mm(c.banks[bp][:, :], wt[:, kc, :], hT[:, kc, sl], kc == 0, kc == KC - 1, [rw, r_hT], [c.bres[bp]])
                c.act(raw[s][:], c.banks[bp][:, :], AF.Copy, [c.bres[bp]], [r_raw[s]])
                c.act(sqb[s][:], c.banks[bp][:, :], AF.Square, [c.bres[bp]], [r_sqb[s]])
                c.mm(c.banks[bs][:, :], ones, sqb[s][:], True, True, [r_sqb[s], c.r_const], [c.bres[bs]])
                c.act(rsb[s][:], c.banks[bs][:, :], AF.Sqrt, [c.bres[bs]], [r_rsb[s]], bias=EPS, scale=1.0 / 128)
                c.recip(rsb[s][:], rsb[s][:], [r_rsb[s]], [r_rsb[s]])
                c.stt(dstf[:, sl], raw[s][:], c.v(wn), rsb[s][:], ALU.mult, ALU.mult, [r_raw[s], r_rsb[s], c.r_const], [r_dstf])
                c.copy(dstb[:, sl], dstf[:, sl], [r_dstf], [r_dstb])
        kb.op("dve", lambda e, o=ksum[:], i=knf[:].rearrange("p (n j) -> p n j", j=256): e.reduce_sum(o, i, axis=AX.X),
              reads=[r_knf], writes=[r_ksum])
        wv, rv = ws.get(3 * h + 2)
        for j in range(16):
            bk = 4 + (j % 2)
            tsl = slice(j * 128, (j + 1) * 128)
            for kc in range(KC):
                c.mm(c.banks[bk][:, 0:128], hT[:, kc, tsl], wv[:, kc, :], kc == 0, kc == KC - 1, [rv, r_hT], [c.bres[bk]])
            c.copy(vext[:, j, 0:128], c.banks[bk][:, 0:128], [c.bres[bk]], [r_vext])
        for i in range(8, 16):
            jb = i // 2
            tsl = slice(i * 128, (i + 1) * 128)
            bk = 6
            c.mm(c.banks[bk][:, 0:8], qnf[:, tsl], ksum[:], True, True, [r_qnf, r_ksum], [c.bres[bk]])
            c.memset(gm[:], -1e30, [r_gm])
            c.copy(gm[:, 0:jb], c.banks[bk][:, 0:jb], [c.bres[bk]], [r_gm])
            kb.op("dve", lambda e, o=mx[:], i_=gm[:]: e.max(o, i_), reads=[r_gm], writes=[r_mx])
            c.memset(negm[:], 0.0, [r_negm])
            c.ts(negm[:, 0:jb], gm[:, 0:jb], mx[:, 2:3], NEG, ALU.is_lt, ALU.mult, [r_gm, r_mx], [r_negm])
            c.tr(c.banks[7][0:8, 0:128], negm[:], c.identf[:], [r_negm, c.r_const], [c.bres[7]])
            c.copy(negT[:, tsl], c.banks[7][0:8, 0:128], [c.bres[7]], [r_negT])
        ip = 0
        for jb in range(8):
            qsl = slice(jb * 256, (jb + 1) * 256)
            nkt = 2 * jb + 2
            bo = [0, 1]
            for kt in range(nkt):
                n = kt // 2
                bst = 2 + (ip % 3)
                p = ip % 3
                ip += 1
                own = (n == jb)
                need_mask = (not own) and jb >= 4
                c.mm(c.banks[bst][:, 0:256], knT[:, kt * 128:(kt + 1) * 128], qnT[:, qsl], True, not (own or need_mask),
                     [r_knT, r_qnT], [c.bres[bst]])
                if need_mask:
                    o8, _ = CST["esel8"]
                    c.mm(c.banks[bst][:, 0:256], c.cbf[0:8, o8 + n * 128:o8 + (n + 1) * 128], negT[:, qsl], False, True,
                         [c.r_const, r_negT], [c.bres[bst]])
                if own:
                    cm = c.cb("cmaskA") if kt == 2 * jb else c.cb("cmaskB")
                    c.mm(c.banks[bst][:, 0:256], identb, cm, False, True, [c.r_const], [c.bres[bst]])
                c.act(PT[p][:], c.banks[bst][:, 0:256], AF.Exp, [c.bres[bst]], [r_PT[p]], scale=scale)
                for th in range(2):
                    if kt == 2 * jb + 1 and th == 0:
                        continue
                    last = (kt == 2 * jb) if th == 0 else (kt == 2 * jb + 1)
                    c.mm(c.banks[bo[th]][:, 0:129], PT[p][:, th * 128:(th + 1) * 128], vext[:, kt, 0:129], kt == 0, last,
                         [r_PT[p], r_vext], [c.bres[bo[th]]])
            for th in range(2):
                i = 2 * jb + th
                c.recip(rinv[:, th:th + 1], c.banks[bo[th]][:, 128:129], [c.bres[bo[th]]], [r_rinv])
                c.ts(mofin[:, i, :], c.banks[bo[th]][:, 0:128], rinv[:, th:th + 1], None, ALU.mult, None,
                     [c.bres[bo[th]], r_rinv], [r_mofin])

        def put(q0, nj, pb, rb, h=h):
            c.act(catT[:, 8 + h, q0 * 128:(q0 + nj) * 128], pb, AF.Copy, [rb], [r_cat])
        transpose_tok_to_fm(c, mofin, r_mofin, 16, put)


def build(stages=("l0", "ffn0", "l1", "ffn1"), debug=False):
    nc = bass.Bass("TRN2", target_bir_lowering=False)
    dbg = nc.dram_tensor("dbg", [2048, T], F32, kind="ExternalOutput").ap() if debug else None

    def din(name, shape):
        return nc.dram_tensor(name, list(shape), F32, kind="ExternalInput").ap()
    xT = din("xT", [D, T])
    cst = din("cst", [128, NCST])
    vec = din("vec", [128, NVEC])
    w_in_even = din("w_in_even", [D, 7168])
    w_out_even = din("w_out_even", [2048, D])
    w_in_ssm = din("w_in_ssm", [D, 10304])
    w_out_ssm = din("w_out_ssm", [4096, D])
    wg = [din("w_gate%d" % l, [D, DFF]) for l in range(2)]
    wu = [din("w_up%d" % l, [D, DFF]) for l in range(2)]
    wd = [din("w_down%d" % l, [DFF, D]) for l in range(2)]
    names = {"l0": "x1T", "ffn0": "x2T", "l1": "x3T", "ffn1": "yT"}
    last = stages[-1]
    bufs = {}
    for st in ("l0", "ffn0", "l1", "ffn1"):
        kind = "ExternalOutput" if st == last else "Internal"
        bufs[st] = nc.dram_tensor(names[st], [D, T], F32, kind=kind).ap()
    gyT = nc.dram_tensor("gyT", [4096, T], BF16, kind="Internal").ap()
    c = Ctx(nc)
    kb = c.kb
    c.load_consts(cst, vec)
    r_out = {st: Res(names[st]) for st in bufs}
    src = xT
    first = stages[0]
    order = ["l0", "ffn0", "l1", "ffn1"]
    for st in order[order.index(first):order.index(last) + 1]:
        if st == "l0":
            l0_mixer(c, src, bufs[st], r_out[st], w_in_even, w_out_even, cst, dbg)
        elif st == "ffn0":
            ffn_phase(c, src, bufs[st], r_out[st], "ln_ffn0", wg[0], wu[0], wd[0])
        elif st == "l1":
            from_l1 = globals().get("l1_mixer")
            from_l1(c, src, bufs[st], r_out[st], w_in_ssm, w_out_ssm, cst, gyT)
        elif st == "ffn1":
            ffn_phase(c, src, bufs[st], r_out[st], "ln_ffn1", wg[1], wu[1], wd[1])
        src = bufs[st]
    kb.barrier()
    kb.finish()
    kb.close()
    return nc, names[last]


_CACHE = {}


def run_stages(inputs, stages, xT_all, ncores=8, debug=False):
    key = tuple(stages)
    if key not in _CACHE:
        _CACHE[key] = build(stages, debug)
    nc, oname = _CACHE[key]
    cst = _make_cst()
    vec = _make_vec(inputs)
    f = lambda a: np.ascontiguousarray(a, dtype=np.float32)
    shared = {
        "cst": cst, "vec": vec,
        "w_in_even": f(inputs["w_in_even"][0]), "w_out_even": f(inputs["w_out_even"][0]),
        "w_in_ssm": f(inputs["w_in_ssm"][0]), "w_out_ssm": f(inputs["w_out_ssm"][0]),
    }
    for l in range(2):
        shared["w_gate%d" % l] = f(inputs["w_gate"][l])
        shared["w_up%d" % l] = f(inputs["w_up"][l])
        shared["w_down%d" % l] = f(inputs["w_down"][l])
    in_maps = []
    for i in range(ncores):
        d = dict(shared)
        d["xT"] = np.ascontiguousarray(xT_all[i])
        in_maps.append(d)
    res = run_bass_kernel_spmd(nc, in_maps, core_ids=list(range(ncores)))
    if debug:
        return np.stack([r[oname] for r in res.results], 0), res.results[0]["dbg"]
    return np.stack([r[oname] for r in res.results], 0)


def kernel(**inputs):
    x = np.asarray(inputs["x"], dtype=np.float32)
    xT = np.ascontiguousarray(x.transpose(0, 2, 1))
    yT = run_stages(inputs, ("l0", "ffn0", "l1", "ffn1"), xT)
    return np.ascontiguousarray(yT.transpose(0, 2, 1)).astype(np.float32)


def l1_mixer(c, xT, x3T, r_x3T, w_in, w_out, cst, gyT):
    kb = c.kb
    m = kb.mark()
    hT = kb.sbuf("b_hT", [128, KC, T], BF16)
    r_hT = Res()
    rmsnorm_fm(c, xT, "ln_mix1", hT, r_hT, 0, T, "bn_")
    c.load_mscan(cst, "m256")
    c.alloc_w(16 * 128)
    G = 512
    r_gyT = Res("gyT")
    identb = c.cb("ident")
    acsT = kb.sbuf("b_acsT", [64, T], F32)
    dt_tok = kb.sbuf("b_dttok", [128, 16, 64], F32)
    acs_tok = kb.sbuf("b_acstok", [128, 16, 64], F32)
    eacs_tok = kb.sbuf("b_eacstok", [128, 16, 64], F32)
    dec_tok = kb.sbuf("b_dectok", [128, 16, 64], F32)
    eal_b = kb.sbuf("b_ealb", [128, 8, 64], F32)
    m_dt = kb.mark()
    dtT = kb.sbuf("b_dtT", [64, T], F32)
    negA = kb.sbuf("b_negA", [64, 1], F32)
    diag = kb.sbuf("b_diag", [64, 64], F32)
    etmp = kb.sbuf("b_etmp", [64, G], F32)
    r_acsT, r_dtT, r_dttok, r_acstok, r_eacs, r_dec, r_eal, r_negA, r_diag, r_etmp = [Res() for _ in range(10)]
    c.act(negA[:], c.v("alog", rows=64), AF.Exp, [c.r_const], [r_negA])
    c.ts(negA[:], negA[:], -1.0, None, ALU.mult, None, [r_negA], [r_negA])
    wdt, rwdt = c.load_w(w_in, 10240, 64, D)
    for g in range(4):
        sl = slice(g * G, (g + 1) * G)
        bk = g % 2
        for kc in range(KC):
            c.mm(c.banks[bk][0:64, :], wdt[:, kc, :], hT[:, kc, sl], kc == 0, kc == KC - 1, [rwdt, r_hT], [c.bres[bk]])
        c.act(etmp[:], c.banks[bk][0:64, :], AF.Exp, [c.bres[bk], c.r_const], [r_etmp], bias=c.v("dtb", rows=64))
        c.act(dtT[:, sl], etmp[:], AF.Ln, [r_etmp], [r_dtT], bias=1.0)
        c.ts(etmp[:], dtT[:, sl], negA[:, 0:1], None, ALU.mult, None, [r_dtT, r_negA], [r_etmp])
        kb.op("dve", lambda e, o=acsT[:, sl], d0=c.mscan[0:64, 0:G], d1=etmp[:]: e.tensor_tensor_scan(o, d0, d1, 0.0, ALU.mult, ALU.add),
              reads=[r_etmp, c.r_mscan], writes=[r_acsT])
    for j in range(16):
        tsl = slice(j * 128, (j + 1) * 128)
        bk = 6 + (j % 2)
        c.tr(c.banks[bk][:, 0:64], dtT[0:64, tsl], c.identf[0:64, 0:64], [r_dtT, c.r_const], [c.bres[bk]])
        c.tr(c.banks[bk][:, 64:128], acsT[0:64, tsl], c.identf[0:64, 0:64], [r_acsT, c.r_const], [c.bres[bk]])
        c.copy(dt_tok[:, j, :], c.banks[bk][:, 0:64], [c.bres[bk]], [r_dttok])
        c.copy(acs_tok[:, j, :], c.banks[bk][:, 64:128], [c.bres[bk]], [r_acstok])
    c.act(eacs_tok[:], acs_tok[:], AF.Exp, [r_acstok], [r_eacs])
    for ci in range(8):
        col = ci * 256 + 255
        c.ts(diag[:], c.identf[0:64, 0:64], acsT[0:64, col:col + 1], None, ALU.mult, None, [c.r_const, r_acsT], [r_diag])
        bk = 4 + (ci % 2)
        c.mm(c.banks[bk][:, 0:64], c.onesf[0:64, :], diag[:], True, True, [c.r_const, r_diag], [c.bres[bk]])
        c.act(eal_b[:, ci, :], c.banks[bk][:, 0:64], AF.Exp, [c.bres[bk]], [r_eal])
        for i in range(2):
            c.tt(dec_tok[:, 2 * ci + i, :], c.banks[bk][:, 0:64], acs_tok[:, 2 * ci + i, :], ALU.subtract,
                 [c.bres[bk], r_acstok], [r_dec])
    c.act(dec_tok[:], dec_tok[:], AF.Exp, [r_dec], [r_dec])
    kb.barrier()
    kb.release(m_dt)

    xs_tok = kb.sbuf("b_xs", [128, 16, 512], BF16)
    sz_tok = kb.sbuf("b_sz", [128, 16, 512], BF16)
    B_tok = kb.sbuf("b_Btok", [128, 16, 128], BF16)
    BT = kb.sbuf("b_BT", [128, T], BF16)
    CT = kb.sbuf("b_CT", [128, T], BF16)
    featT = kb.sbuf("b_featT", [128, T], BF16)
    raw = kb.sbuf("b_raw", [128, T + 4], F32)
    acc = kb.sbuf("b_acc", [128, T], F32)
    stT = kb.sbuf("b_stT", [128, 512], F32)
    stTb = kb.sbuf("b_stTb", [128, 512], BF16)
    sttmp = kb.sbuf("b_sttmp", [128, 512], F32)
    r_xs, r_sz, r_Btok, r_BT, r_CT, r_featT, r_raw, r_acc, r_stT, r_stTb, r_sttmp = [Res() for _ in range(11)]
    xdt = [kb.sbuf("b_xdt%d" % i, [128, 512], BF16) for i in range(2)]
    xdd = [kb.sbuf("b_xdd%d" % i, [128, 512], BF16) for i in range(2)]
    cbm = [kb.sbuf("b_cbm%d" % i, [128, 256], BF16) for i in range(2)]
    r_xdt = [Res() for _ in range(2)]
    r_xdd = [Res() for _ in range(2)]
    r_cbm = [Res() for _ in range(2)]
    dd = [kb.sbuf("b_dd%d" % i, [128, 256], F32) for i in range(2)]
    Lb = [kb.sbuf("b_L%d" % i, [128, 256], BF16) for i in range(2)]
    Gb = [kb.sbuf("b_G%d" % i, [128, 256], BF16) for i in range(4)]
    r_dd = [Res() for _ in range(2)]
    r_L = [Res() for _ in range(2)]
    r_G = [Res() for _ in range(4)]
    t1 = [kb.sbuf("b_t1%d" % i, [128, 512], F32) for i in range(2)]
    t2 = [kb.sbuf("b_t2%d" % i, [128, 512], F32) for i in range(1)] * 2
    gyn = [kb.sbuf("b_gyn%d" % i, [128, 512], BF16) for i in range(2)]
    ssg = [kb.sbuf("b_ssg%d" % i, [128, 1], F32) for i in range(2)]
    gst = [kb.sbuf("b_gst%d" % i, [128, 4, 256], BF16) for i in range(2)]
    r_t1 = [Res() for _ in range(2)]
    r_t2 = [Res() for _ in range(1)] * 2
    r_gyn = [Res() for _ in range(2)]
    r_sqj = Res()
    r_ssg = [Res() for _ in range(2)]
    r_gst = [Res() for _ in range(2)]
    c.memset(raw[:, 0:4], 0.0, [r_raw])
    cmul = [c.cb("cmul0"), c.cb("cmul1")]

    tiles = []
    for gi in range(8):
        for ft in range(4):
            tiles.append((w_in, 4096 + gi * 512 + ft * 128, 128, D))
        tiles.append((w_in, 8192 + gi * 128, 128, D))
        tiles.append((w_in, 9216 + gi * 128, 128, D))
        for ft in range(4):
            tiles.append((w_in, gi * 512 + ft * 128, 128, D))
    ws = WStream(c, tiles)
    pit = 0
    for gi in range(8):
        hs = slice(gi * 8, (gi + 1) * 8)

        def fm_tile(widx, ch_tile, dst, r_dst):
            nonlocal pit
            wt, rw = ws.get(widx)
            for g in range(4):
                bk = pit % 2
                pit += 1
                sl = slice(g * G, (g + 1) * G)
                for kc in range(KC):
                    c.mm(c.banks[bk][:, :], wt[:, kc, :], hT[:, kc, sl], kc == 0, kc == KC - 1, [rw, r_hT], [c.bres[bk]])
                c.act(raw[:, 3 + g * G:3 + (g + 1) * G], c.banks[bk][:, :], AF.Copy, [c.bres[bk]], [r_raw])
            cw = lambda k: c.v("convw", c0=ch_tile * 4 + k, n=1)
            c.ts(acc[:], raw[:, 0:T], cw(0), c.v("convb", c0=ch_tile, n=1), ALU.mult, ALU.add, [r_raw, c.r_const], [r_acc])
            for k in range(1, 4):
                c.stt(acc[:], raw[:, k:k + T], cw(k), acc[:], ALU.mult, ALU.add, [r_raw, c.r_const, r_acc], [r_acc])
            c.act(dst[:], acc[:], AF.Silu, [r_acc], [r_dst])

        for ft in range(4):
            fm_tile(gi * 10 + ft, gi * 4 + ft, featT, r_featT)

            def putx(q0, nj, pb, rb, ft=ft):
                c.copy(xs_tok[:, q0:q0 + nj, ft * 128:(ft + 1) * 128], pb.rearrange("p (j f) -> p j f", f=128), [rb], [r_xs])
            transpose_fm_to_tok(c, featT, r_featT, putx)
        fm_tile(gi * 10 + 4, 32 + gi, BT, r_BT)

        def putb(q0, nj, pb, rb):
            c.copy(B_tok[:, q0:q0 + nj, :], pb.rearrange("p (j f) -> p j f", f=128), [rb], [r_Btok])
        transpose_fm_to_tok(c, BT, r_BT, putb)
        fm_tile(gi * 10 + 5, 40 + gi, CT, r_CT)
        for ft in range(4):
            wz, rz = ws.get(gi * 10 + 6 + ft)
            for j in range(16):
                bk = 4 + (j % 2)
                tsl = slice(j * 128, (j + 1) * 128)
                for kc in range(KC):
                    c.mm(c.banks[bk][:, 0:128], hT[:, kc, tsl], wz[:, kc, :], kc == 0, kc == KC - 1, [rz, r_hT], [c.bres[bk]])
                c.act(sz_tok[:, j, ft * 128:(ft + 1) * 128], c.banks[bk][:, 0:128], AF.Silu, [c.bres[bk]], [r_sz])
        c.memset(stT[:], 0.0, [r_stT])
        c.memset(stTb[:], 0.0, [r_stTb])
        for ci in range(8):
            csl = slice(ci * 256, (ci + 1) * 256)
            for i in range(2):
                j = 2 * ci + i
                xv = xs_tok[:, j, :].rearrange("p (h d) -> p h d", d=64)
                c.tt(xdt[i][:].rearrange("p (h d) -> p h d", d=64), xv,
                     dt_tok[:, j, hs].unsqueeze(2).to_broadcast([128, 8, 64]), ALU.mult, [r_xs, r_dttok], [r_xdt[i]])
                c.tt(xdd[i][:].rearrange("p (h d) -> p h d", d=64), xdt[i][:].rearrange("p (h d) -> p h d", d=64),
                     dec_tok[:, j, hs].unsqueeze(2).to_broadcast([128, 8, 64]), ALU.mult, [r_xdt[i], r_dec], [r_xdd[i]], eng="pool")
                c.mm(c.banks[i][:, 0:256], BT[:, ci * 256 + i * 128:ci * 256 + (i + 1) * 128], CT[:, csl], True, True,
                     [r_BT, r_CT], [c.bres[i]])
                c.tt(cbm[i][:], c.banks[i][:, 0:256], cmul[i], ALU.mult, [c.bres[i], c.r_const], [r_cbm[i]])
            for tt_ in range(2):
                c.mm(c.banks[2 + tt_][:, :], CT[:, ci * 256 + tt_ * 128:ci * 256 + (tt_ + 1) * 128], stTb[:], True, True,
                     [r_CT, r_stTb], [c.bres[2 + tt_]])
            for hh in range(8):
                H = gi * 8 + hh
                ba = 6 + (hh % 2)
                c.mm(c.banks[ba][:, 0:256], c.identf[0:64, H:H + 1].to_broadcast([64, 128]), acsT[0:64, csl], True, True,
                     [c.r_const, r_acsT], [c.bres[ba]])
                g0 = (2 * hh) % 4
                g1 = (2 * hh + 1) % 4
                c.ts(dd[0][:], c.banks[ba][:, 0:256], acs_tok[:, 2 * ci, H:H + 1], 0.0, ALU.subtract, ALU.min,
                     [c.bres[ba], r_acstok], [r_dd[0]])
                c.act(Lb[0][:], dd[0][:], AF.Exp, [r_dd[0]], [r_L[0]])
                c.tt(Gb[g0][:], Lb[0][:], cbm[0][:], ALU.mult, [r_L[0], r_cbm[0]], [r_G[g0]], eng="pool")
                c.ts(dd[1][:, 0:128], c.banks[ba][:, 128:256], acs_tok[:, 2 * ci + 1, H:H + 1], 0.0, ALU.subtract, ALU.min,
                     [c.bres[ba], r_acstok], [r_dd[1]])
                c.act(Lb[1][:, 0:128], dd[1][:, 0:128], AF.Exp, [r_dd[1]], [r_L[1]])
                c.tt(Gb[g1][:, 0:128], Lb[1][:, 0:128], cbm[1][:, 128:256], ALU.mult, [r_L[1], r_cbm[1]], [r_G[g1]], eng="pool")
                hsl = slice(hh * 64, (hh + 1) * 64)
                c.mm(c.banks[4][:, hsl], Gb[g0][:, 0:128], xdt[0][:, hsl], True, True, [r_G[g0], r_xdt[0]], [c.bres[4]])
                c.mm(c.banks[5][:, hsl], Gb[g0][:, 128:256], xdt[0][:, hsl], True, False, [r_G[g0], r_xdt[0]], [c.bres[5]])
                c.mm(c.banks[5][:, hsl], Gb[g1][:, 0:128], xdt[1][:, hsl], False, True, [r_G[g1], r_xdt[1]], [c.bres[5]])
            for tt_ in range(2):
                j = 2 * ci + tt_
                a = tt_
                v3 = lambda ap: ap.rearrange("p (h d) -> p h d", d=64)
                c.tt(v3(t1[a][:]), v3(c.banks[2 + tt_][:, :]), eacs_tok[:, j, hs].unsqueeze(2).to_broadcast([128, 8, 64]), ALU.mult,
                     [c.bres[2 + tt_], r_eacs], [r_t1[a]])
                c.tt(t1[a][:], t1[a][:], c.banks[4 + tt_][:, :], ALU.add, [c.bres[4 + tt_]], [r_t1[a]])
                c.tt(v3(t2[a][:]), v3(xs_tok[:, j, :]), c.v("dskip")[:, hs].unsqueeze(2).to_broadcast([128, 8, 64]), ALU.mult,
                     [r_xs, c.r_const], [r_t2[a]], eng="pool")
                c.tt(t1[a][:], t1[a][:], t2[a][:], ALU.add, [r_t2[a]], [r_t1[a]])
                c.tt(t1[a][:], t1[a][:], sz_tok[:, j, :], ALU.mult, [r_sz], [r_t1[a]])
                c.act(gyn[a][:], t1[a][:], AF.Square, [r_t1[a]], [r_gyn[a], r_ssg[a]], accum_out=ssg[a][:])
                c.act(ssg[a][:], ssg[a][:], AF.Sqrt, [r_ssg[a]], [r_ssg[a]], bias=EPS, scale=1.0 / 512)
                c.recip(ssg[a][:], ssg[a][:], [r_ssg[a]], [r_ssg[a]])
                c.ts(gyn[a][:], t1[a][:], ssg[a][:, 0:1], None, ALU.mult, None, [r_t1[a], r_ssg[a]], [r_gyn[a]])
                bt = tt_
                pb = c.banks[bt][:, :].bitcast(BF16)
                for ft in range(4):
                    c.tr(pb[:, ft * 128:(ft + 1) * 128], gyn[a][:, ft * 128:(ft + 1) * 128], identb, [r_gyn[a], c.r_const], [c.bres[bt]])
                gs = ci % 2
                for ft in range(4):
                    c.act(gst[gs][:, ft, tt_ * 128:(tt_ + 1) * 128], pb[:, ft * 128:(ft + 1) * 128], AF.Copy,
                          [c.bres[bt], c.r_const], [r_gst[gs]], scale=c.v("ssmn", c0=gi * 4 + ft, n=1))
            gs = ci % 2
            for ft in range(4):
                r0 = gi * 512 + ft * 128
                kb.dma("sp", gyT[r0:r0 + 128, csl], gst[gs][:, ft, :], reads=[r_gst[gs]], writes=[r_gyT], join=True)
            for i in range(2):
                c.mm(c.banks[2][:, :], B_tok[:, 2 * ci + i, :], xdd[i][:], i == 0, i == 1, [r_Btok, r_xdd[i]], [c.bres[2]])
            v3 = lambda ap: ap.rearrange("p (h d) -> p h d", d=64)
            c.tt(v3(sttmp[:]), v3(stT[:]), eal_b[:, ci, hs].unsqueeze(2).to_broadcast([128, 8, 64]), ALU.mult,
                 [r_stT, r_eal], [r_sttmp])
            c.tt(stT[:], sttmp[:], c.banks[2][:, :], ALU.add, [r_sttmp, c.bres[2]], [r_stT])
            c.act(stTb[:], stT[:], AF.Copy, [r_stT], [r_stTb])
    kb.barrier()
    kb.release(m)
    m = kb.mark()
    TN = 1024
    gyh = kb.sbuf("b_gyh", [128, 32, TN], BF16)
    r_gyh = Res()
    xr = [kb.sbuf("b_xr%d" % i, [128, TN], F32) for i in range(2)]
    r_xr = [Res() for _ in range(2)]
    c.alloc_w(32 * 128)
    it = 0
    for th in range(2):
        t0 = th * TN
        gsrc = gyT[:, t0:t0 + TN].rearrange("(k p) t -> p k t", p=128)
        for k0 in range(0, 32, 8):
            kb.dma("sp", gyh[:, k0:k0 + 8, :], gsrc[:, k0:k0 + 8, :], reads=[r_gyT], writes=[r_gyh], join=(k0 > 0))
        ws = WStream(c, [(w_out, jo * 128, 128, 4096) for jo in range(16)])
        for jo in range(16):
            wt, rw = ws.get(jo)
            b = jo % 2
            kb.dma("sp", xr[b][:], xT[jo * 128:(jo + 1) * 128, t0:t0 + TN], writes=[r_xr[b]])
            for g in range(TN // 512):
                bk = it % 4
                it += 1
                sl = slice(g * 512, (g + 1) * 512)
                for kc in range(32):
                    c.mm(c.banks[bk][:, :], wt[:, kc, :], gyh[:, kc, sl], kc == 0, kc == 31, [rw, r_gyh], [c.bres[bk]])
                c.tt(xr[b][:, sl], xr[b][:, sl], c.banks[bk][:, :], ALU.add, [c.bres[bk]], [r_xr[b]])
            kb.dma("sp", x3T[jo * 128:(jo + 1) * 128, t0:t0 + TN], xr[b][:], reads=[r_xr[b]], writes=[r_x3T], join=True)
    kb.barrier()
    kb.release(m)


def transpose_fm_to_tok(c, srcT, r_src, dst_fn):
    ident = c.cb("ident")
    for q in range(0, 16, 4):
        bk = 6 + ((q // 4) % 2)
        pb = c.banks[bk][:, :].bitcast(BF16)
        for j in range(4):
            c.tr(pb[:, j * 128:(j + 1) * 128], srcT[:, (q + j) * 128:(q + j + 1) * 128], ident, [r_src, c.r_const], [c.bres[bk]])
        dst_fn(q, 4, pb[:, 0:512], c.bres[bk])
```

```python
import numpy as np
import concourse.bass as bass
import concourse.mybir as mybir
from concourse.alu_op_type import AluOpType as ALU
from concourse.bass_utils import run_bass_kernel_spmd

F32 = mybir.dt.float32
BF16 = mybir.dt.bfloat16
AF = mybir.ActivationFunctionType
AX = mybir.AxisListType

T = 2048
D = 2048
KC = 16
DFF = 5632
EPS = 1e-6
NEG = -30000.0


class Res:
    __slots__ = ("name", "w", "r", "dsem", "dcnt", "excl")

    def __init__(self, name="", excl=False):
        self.name = name
        self.excl = excl
        self.w = None
        self.r = []
        self.dsem = None
        self.dcnt = 0


class KB:
    ENGS = ("pe", "act", "dve", "pool", "sp")

    def __init__(self, nc):
        self.nc = nc
        self.q = {e: [] for e in self.ENGS}
        self.sem = {}
        self.cnt = {e: 0 for e in self.ENGS}
        self.seen = {e: {} for e in self.ENGS}
        self._ctx = []
        for e in self.ENGS:
            self.sem[e] = self._enter(nc.semaphore("es_" + e))
        self.nsem = 5
        self.dres = []
        self.ninst = {e: 0 for e in self.ENGS}
        self.pe_pending = []
        self.pe_pend_ids = set()

    def _enter(self, cm):
        v = cm.__enter__()
        self._ctx.append(cm)
        return v

    def mark(self):
        return len(self._ctx)

    def release(self, mark):
        while len(self._ctx) > mark:
            self._ctx.pop().__exit__(None, None, None)

    def sbuf(self, name, shape, dt):
        self.nsb = getattr(self, "nsb", 0) + 1
        return self._enter(self.nc.sbuf_tensor("%s_u%d" % (name, self.nsb), list(shape), dt))

    def psum(self, name, shape, dt):
        return self._enter(self.nc.psum_tensor(name, list(shape), dt))

    def new_sem(self, name):
        self.nsem += 1
        return self._enter(self.nc.semaphore(name))

    def close(self):
        self.release(0)

    def _deps(self, eng, reads, writes):
        deps = []
        own = self.sem[eng]
        for R in reads:
            if R.w is not None:
                deps.append(R.w)
            if R.excl:
                deps.extend(ev for ev in R.r if ev[0] is not own)
        for W in writes:
            if W.w is not None:
                deps.append(W.w)
            deps.extend(W.r)
        out = {}
        for (s, v) in deps:
            if eng == "pe" and s is own:
                continue
            k = id(s)
            if self.seen[eng].get(k, 0) >= v:
                continue
            if k not in out or out[k][1] < v:
                out[k] = (s, v)
        for k, (s, v) in out.items():
            self.seen[eng][k] = v
        return list(out.values())

    def op(self, eng, fn, reads=(), writes=(), signal=True):
        if eng != "pe":
            for X in tuple(reads) + tuple(writes):
                assert id(X) not in self.pe_pend_ids, "resource %s has un-signalled PE accesses" % X.name
        waits = self._deps(eng, reads, writes)
        if eng == "pe" and not signal:
            def emit_ns(e, waits=waits, fn=fn):
                for (s, v) in waits:
                    e.wait_ge(s, v)
                fn(e)
            self.q[eng].append(emit_ns)
            self.pe_pending.append((tuple(reads), tuple(writes)))
            for X in tuple(reads) + tuple(writes):
                self.pe_pend_ids.add(id(X))
            return None
        self.cnt[eng] += 1
        ev = (self.sem[eng], self.cnt[eng])
        sem = self.sem[eng]

        def emit(e, waits=waits, fn=fn, sem=sem):
            for (s, v) in waits:
                e.wait_ge(s, v)
            fn(e).then_inc(sem, 1)

        self.q[eng].append(emit)
        self.ninst[eng] += 1 + len(waits)
        groups = [(reads, writes)]
        if eng == "pe" and self.pe_pending:
            groups = self.pe_pending + groups
            self.pe_pending = []
            self.pe_pend_ids = set()
        for rs, ws_ in groups:
            for R in rs:
                R.r.append(ev)
            for W in ws_:
                W.w = ev
                W.r = []
        return ev

    def dma(self, queue, out, in_, reads=(), writes=(), join=False, **kw):
        for X in tuple(reads) + tuple(writes):
            assert id(X) not in self.pe_pend_ids, "resource %s has un-signalled PE accesses" % X.name
        tgt = writes[0]
        if tgt.dsem is None:
            tgt.dsem = self.new_sem("ds_%d" % self.nsem)
            self.dres.append(tgt)
        if join:
            saved = [(W, W.w) for W in writes]
            for W in writes:
                W.w = None
        waits = self._deps(queue, reads, writes)
        if join:
            for W, w in saved:
                W.w = w
        tgt.dcnt += 16
        ev = (tgt.dsem, tgt.dcnt)
        dsem = tgt.dsem

        def emit(e, waits=waits, out=out, in_=in_, dsem=dsem, kw=kw):
            for (s, v) in waits:
                e.wait_ge(s, v)
            e.dma_start(out=out, in_=in_, **kw).then_inc(dsem, 16)

        self.q[queue].append(emit)
        self.ninst[queue] += 1 + len(waits)
        for R in reads:
            R.r.append(ev)
        for W in writes:
            W.w = ev
            if not join:
                W.r = []
        return ev

    def wait_all(self, eng, events):
        waits = []
        for ev in events:
            if ev is None:
                continue
            s, v = ev
            if self.seen[eng].get(id(s), 0) < v:
                self.seen[eng][id(s)] = v
                waits.append(ev)

        def emit(e, waits=waits):
            for (s, v) in waits:
                e.wait_ge(s, v)

        self.q[eng].append(emit)

    def barrier(self):
        evs = [(self.sem[e], self.cnt[e]) for e in self.ENGS if self.cnt[e] > 0]
        evs += [(r.dsem, r.dcnt) for r in self.dres if r.dcnt > 0]
        for e in self.ENGS:
            self.wait_all(e, [ev for ev in evs if ev[0] is not self.sem[e]])

    def finish(self):
        nc = self.nc
        q = self.q
        with nc.Block() as block:
            @block.tensor
            def _(e):
                for f in q["pe"]:
                    f(e)

            @block.scalar
            def _(e):
                for f in q["act"]:
                    f(e)

            @block.vector
            def _(e):
                for f in q["dve"]:
                    f(e)

            @block.gpsimd
            def _(e):
                for f in q["pool"]:
                    f(e)

            @block.sync
            def _(e):
                for f in q["sp"]:
                    f(e)


def _cst_layout():
    o = {}
    p = 0
    for name, n in (("ident", 128), ("ones", 128), ("bdmask", 128), ("cmaskA", 256), ("cmaskB", 256),
                    ("cmul0", 256), ("cmul1", 256), ("esel8", 1024), ("m64", 512), ("m256", 512)):
        o[name] = (p, n)
        p += n
    return o, p


CST, NCST = _cst_layout()
NCST_BF = CST["m64"][0]


def _make_cst():
    c = np.zeros((128, NCST), np.float32)
    s = np.arange(128)[:, None]
    t = np.arange(128)[None, :]
    tri = (s <= t).astype(np.float32)
    cneg = np.where(s <= t, 0.0, NEG).astype(np.float32)

    def put(name, a):
        o, n = CST[name]
        c[:, o:o + n] = a
    put("ident", np.eye(128, dtype=np.float32))
    put("ones", np.ones((128, 128), np.float32))
    put("bdmask", tri * ((s // 64) == (t // 64)))
    put("cmaskA", np.concatenate([cneg, np.zeros((128, 128), np.float32)], 1))
    put("cmaskB", np.concatenate([np.full((128, 128), NEG, np.float32), cneg], 1))
    put("cmul0", np.concatenate([tri, np.ones((128, 128), np.float32)], 1))
    put("cmul1", np.concatenate([np.zeros((128, 128), np.float32), tri], 1))
    e8 = np.zeros((128, 8, 128), np.float32)
    for n in range(8):
        e8[n, n, :] = 1.0
    put("esel8", e8.reshape(128, 1024))
    tt = np.arange(512)
    put("m64", np.broadcast_to((tt % 64 != 0).astype(np.float32)[None, :], (128, 512)))
    put("m256", np.broadcast_to((tt % 256 != 0).astype(np.float32)[None, :], (128, 512)))
    return c


def _vec_layout():
    o = {}
    p = 0
    for name, n in (("ln_mix0", 16), ("ln_ffn0", 16), ("ln_mix1", 16), ("ln_ffn1", 16), ("lb0", 8), ("lb1", 8),
                    ("hgn", 1), ("qn", 1), ("kn", 1), ("convw", 192), ("convb", 48), ("dtb", 1), ("alog", 1),
                    ("dskip", 64), ("ssmn", 32)):
        o[name] = (p, n)
        p += n
    return o, p


VEC, NVEC = _vec_layout()


def _make_vec(inp):
    v = np.zeros((128, NVEC), np.float32)

    def put(name, a):
        o, n = VEC[name]
        v[:a.shape[0], o:o + n] = a

    def col(w, n):
        return np.ascontiguousarray(w.reshape(n, 128).T)
    put("ln_mix0", col(inp["ln_mix"][0], 16))
    put("ln_ffn0", col(inp["ln_ffn"][0], 16))
    put("ln_mix1", col(inp["ln_mix"][1], 16))
    put("ln_ffn1", col(inp["ln_ffn"][1], 16))
    put("lb0", col(inp["hgrn_lb"][0], 8))
    put("lb1", col(inp["hgrn_lb"][1], 8))
    put("hgn", inp["hgrn_norm"][0].reshape(128, 1))
    put("qn", inp["q_norm"][0].reshape(128, 1))
    put("kn", inp["k_norm"][0].reshape(128, 1))
    put("convw", np.ascontiguousarray(inp["conv_w"][0].reshape(4, 48, 128).transpose(2, 1, 0)).reshape(128, 192))
    put("convb", col(inp["conv_b"][0], 48))
    put("dtb", inp["dt_bias"][0].reshape(64, 1))
    put("alog", inp["a_log"][0].reshape(64, 1))
    put("dskip", np.broadcast_to(inp["d_skip"][0].reshape(1, 64), (128, 64)))
    put("ssmn", col(inp["ssm_norm"][0], 32))
    return v


class Ctx:
    def __init__(self, nc):
        self.nc = nc
        self.kb = KB(nc)
        kb = self.kb
        self.banks = [kb.psum("bank%d" % i, [128, 512], F32) for i in range(8)]
        self.bres = [Res("bank%d" % i, excl=True) for i in range(8)]
        self.cbf = kb.sbuf("cbf", [128, NCST_BF], BF16)
        self.identf = kb.sbuf("identf", [128, 128], F32)
        self.onesf = kb.sbuf("onesf", [128, 128], F32)
        self.vec = kb.sbuf("vec_sb", [128, NVEC], F32)
        self.mscan = kb.sbuf("mscan", [128, 512], F32)
        self.r_const = Res("const")
        self.r_mscan = Res("mscan")
        self.NW = 3
        self.wbuf = None
        self.wi = 0
        self.wgen = 0

    def alloc_w(self, nelem):
        self.wgen += 1
        self.wbuf = [self.kb.sbuf("wbuf%d_%d" % (self.wgen, i), [128, nelem], BF16) for i in range(self.NW)]
        self.wres = [Res("wbuf%d" % i) for i in range(self.NW)]
        self.wi = 0

    def cb(self, name, rows=128):
        o, n = CST[name]
        return self.cbf[0:rows, o:o + n]

    def v(self, name, rows=128, c0=0, n=None):
        o, nn = VEC[name]
        if n is None:
            n = nn - c0
        return self.vec[0:rows, o + c0:o + c0 + n]

    def load_consts(self, cst, vec):
        kb = self.kb
        kb.dma("pool", self.cbf[:], cst[:, 0:NCST_BF], writes=[self.r_const])
        o, n = CST["ident"]
        kb.dma("sp", self.identf[:], cst[:, o:o + n], writes=[self.r_const], join=True)
        o, n = CST["ones"]
        kb.dma("sp", self.onesf[:], cst[:, o:o + n], writes=[self.r_const], join=True)
        kb.dma("sp", self.vec[:], vec, writes=[self.r_const], join=True)

    def load_mscan(self, cst, name):
        o, n = CST[name]
        self.kb.dma("sp", self.mscan[:], cst[:, o:o + n], writes=[self.r_mscan])

    def mm(self, out, lhsT, rhs, start, stop, reads, writes, signal=None):
        if signal is None:
            signal = bool(stop)
        return self.kb.op("pe", lambda e: e.matmul(out, lhsT, rhs, start=start, stop=stop), reads=reads, writes=writes,
                          signal=signal)

    def tr(self, out, in_, ident, reads, writes):
        return self.kb.op("pe", lambda e: e.transpose(out, in_, ident), reads=reads, writes=writes)

    def act(self, out, in_, func, reads, writes, bias=None, scale=None, accum_out=None):
        kw = {}
        if bias is not None:
            kw["bias"] = bias
        if scale is not None:
            kw["scale"] = scale
        if accum_out is not None:
            kw["accum_out"] = accum_out
        return self.kb.op("act", lambda e: e.activation(out=out, in_=in_, func=func, **kw), reads=reads, writes=writes)

    def ts(self, out, in0, s1, s2, op0, op1, reads, writes, eng="dve"):
        if s2 is None:
            return self.kb.op(eng, lambda e: e.tensor_scalar(out, in0, s1, None, op0), reads=reads, writes=writes)
        return self.kb.op(eng, lambda e: e.tensor_scalar(out, in0, s1, s2, op0, op1), reads=reads, writes=writes)

    def tt(self, out, in0, in1, op, reads, writes, eng="dve"):
        return self.kb.op(eng, lambda e: e.tensor_tensor(out, in0, in1, op), reads=reads, writes=writes)

    def stt(self, out, in0, scalar, in1, op0, op1, reads, writes, eng="dve"):
        return self.kb.op(eng, lambda e: e.scalar_tensor_tensor(out, in0, scalar, in1, op0, op1), reads=reads, writes=writes)

    def copy(self, out, in_, reads, writes, eng="dve"):
        return self.kb.op(eng, lambda e: e.tensor_copy(out, in_), reads=reads, writes=writes)

    def recip(self, out, in_, reads, writes):
        return self.kb.op("dve", lambda e: e.reciprocal(out, in_), reads=reads, writes=writes)

    def memset(self, ap, val, writes, eng="dve"):
        return self.kb.op(eng, lambda e: e.memset(ap, val), writes=writes)

    def load_w(self, w, c0, ncols, K):
        kb = self.kb
        i = self.wi
        self.wi = (self.wi + 1) % self.NW
        buf, res = self.wbuf[i], self.wres[i]
        nk = K // 128
        view = buf[:, 0:nk * ncols].rearrange("p (k n) -> p k n", n=ncols)
        src = w[:, c0:c0 + ncols].rearrange("(k p) n -> p k n", p=128)
        first = True
        for k0 in range(0, nk, 4):
            k1 = min(nk, k0 + 4)
            kb.dma("pool", view[:, k0:k1, :], src[:, k0:k1, :], writes=[res], join=not first)
            first = False
        return view, res


class WStream:
    def __init__(self, c, tiles, ahead=1):
        self.c = c
        self.tiles = tiles
        self.loaded = []
        self.ahead = ahead

    def get(self, i):
        while len(self.loaded) < min(len(self.tiles), i + 1 + self.ahead):
            w, c0, n, K = self.tiles[len(self.loaded)]
            self.loaded.append(self.c.load_w(w, c0, n, K))
        return self.loaded[i]


def rmsnorm_fm(c, src, wname, hT, r_hT, t0, tn, pfx):
    kb = c.kb
    m = kb.mark()
    ng = tn // 512
    xt = [kb.sbuf(pfx + "xt%d" % i, [128, tn], F32) for i in range(2)]
    r_xt = [Res() for _ in range(2)]
    sq = [kb.sbuf(pfx + "sq%d" % i, [128, tn], BF16) for i in range(2)]
    r_sq = [Res() for _ in range(2)]
    rstd = kb.sbuf(pfx + "rstd", [128, tn], F32)
    r_rstd = Res()
    ones = c.cb("ones")
    for kc in range(KC):
        b = kc % 2
        kb.dma("sp", xt[b][:], src[kc * 128:(kc + 1) * 128, t0:t0 + tn], writes=[r_xt[b]])
        c.act(sq[b][:], xt[b][:], AF.Square, [r_xt[b]], [r_sq[b]])
        for g in range(ng):
            c.mm(c.banks[g][:, :], ones, sq[b][:, g * 512:(g + 1) * 512], kc == 0, kc == KC - 1,
                 [r_sq[b], c.r_const], [c.bres[g]], signal=(g == ng - 1 or kc == KC - 1))
    for g in range(ng):
        c.act(rstd[:, g * 512:(g + 1) * 512], c.banks[g][:, :], AF.Sqrt, [c.bres[g]], [r_rstd], bias=EPS, scale=1.0 / D)
    c.recip(rstd[:], rstd[:], [r_rstd], [r_rstd])
    for kc in range(KC):
        b = kc % 2
        kb.dma("sp", xt[b][:], src[kc * 128:(kc + 1) * 128, t0:t0 + tn], writes=[r_xt[b]])
        c.stt(hT[:, kc, 0:tn], xt[b][:], c.v(wname, c0=kc, n=1), rstd[:], ALU.mult, ALU.mult,
              [r_xt[b], r_rstd, c.r_const], [r_hT])
    kb.barrier()
    kb.release(m)


def ffn_phase(c, src, dst, r_dst, lname, wg, wu, wd):
    kb = c.kb
    m = kb.mark()
    TN = 1024
    hT = kb.sbuf("f_hT", [128, KC, TN], BF16)
    r_hT = Res()
    actT = kb.sbuf("f_actT", [128, 44, TN], BF16)
    r_act = Res()
    sg = [kb.sbuf("f_sg%d" % i, [128, 512], F32) for i in range(2)]
    r_sg = [Res() for _ in range(2)]
    xr = [kb.sbuf("f_xr%d" % i, [128, TN], F32) for i in range(2)]
    r_xr = [Res() for _ in range(2)]
    c.alloc_w(44 * 128)
    for th in range(T // TN):
        t0 = th * TN
        rmsnorm_fm(c, src, lname, hT, r_hT, t0, TN, "fn%d_" % th)
        tiles = []
        for ft in range(44):
            tiles.append((wg, ft * 128, 128, D))
            tiles.append((wu, ft * 128, 128, D))
        for jo in range(16):
            tiles.append((wd, jo * 128, 128, DFF))
        ws = WStream(c, tiles)
        it = 0
        for ft in range(44):
            wgt, rg = ws.get(2 * ft)
            wut, ru = ws.get(2 * ft + 1)
            for g in range(TN // 512):
                bg = (it % 2) * 2
                bu = bg + 1
                sl = slice(g * 512, (g + 1) * 512)
                for kc in range(KC):
                    c.mm(c.banks[bg][:, :], wgt[:, kc, :], hT[:, kc, sl], kc == 0, kc == KC - 1, [rg, r_hT], [c.bres[bg]])
                for kc in range(KC):
                    c.mm(c.banks[bu][:, :], wut[:, kc, :], hT[:, kc, sl], kc == 0, kc == KC - 1, [ru, r_hT], [c.bres[bu]])
                s = it % 2
                c.act(sg[s][:], c.banks[bg][:, :], AF.Silu, [c.bres[bg]], [r_sg[s]])
                c.tt(actT[:, ft, sl], sg[s][:], c.banks[bu][:, :], ALU.mult, [r_sg[s], c.bres[bu]], [r_act])
                it += 1
        for jo in range(16):
            wdt, rd = ws.get(88 + jo)
            b = jo % 2
            kb.dma("sp", xr[b][:], src[jo * 128:(jo + 1) * 128, t0:t0 + TN], writes=[r_xr[b]])
            for g in range(TN // 512):
                bk = 4 + (it % 2)
                it += 1
                sl = slice(g * 512, (g + 1) * 512)
                for ft in range(44):
                    c.mm(c.banks[bk][:, :], wdt[:, ft, :], actT[:, ft, sl], ft == 0, ft == 43, [rd, r_act], [c.bres[bk]])
                c.tt(xr[b][:, sl], xr[b][:, sl], c.banks[bk][:, :], ALU.add, [c.bres[bk]], [r_xr[b]])
            kb.dma("sp", dst[jo * 128:(jo + 1) * 128, t0:t0 + TN], xr[b][:], reads=[r_xr[b]], writes=[r_dst], join=True)
    kb.barrier()
    kb.release(m)


def transpose_tok_to_fm(c, src_tok, r_src, ntile, dst_fn, bank0=6):
    ident = c.cb("ident")
    for q in range(0, ntile, 4):
        nj = min(4, ntile - q)
        bk = bank0 + ((q // 4) % 2)
        pb = c.banks[bk][:, :].bitcast(BF16)
        for j in range(nj):
            c.tr(pb[:, j * 128:(j + 1) * 128], src_tok[:, q + j, :], ident, [r_src, c.r_const], [c.bres[bk]])
        dst_fn(q, nj, pb[:, 0:nj * 128], c.bres[bk])


def l0_mixer(c, xT, x1T, r_x1T, w_in, w_out, cst, dbg=None):
    kb = c.kb
    m = kb.mark()
    hT = kb.sbuf("a_hT", [128, KC, T], BF16)
    r_hT = Res()
    rmsnorm_fm(c, xT, "ln_mix0", hT, r_hT, 0, T, "an_")
    catT = kb.sbuf("a_catT", [128, 16, T], BF16)
    r_cat = Res()
    c.alloc_w(16 * 128)
    c.load_mscan(cst, "m64")
    m2 = kb.mark()
    hgrn_heads(c, hT, r_hT, catT, r_cat, w_in)
    kb.barrier()
    kb.release(m2)
    m2 = kb.mark()
    moba_heads(c, hT, r_hT, catT, r_cat, w_in)
    kb.barrier()
    kb.release(m2)
    if dbg is not None:
        r_dbg = Res()
        for hh in range(16):
            kb.dma("pool", dbg[hh * 128:(hh + 1) * 128, :], catT[:, hh, :], reads=[r_cat], writes=[r_dbg], join=True)
    xr = [kb.sbuf("a_xr%d" % i, [128, T], F32) for i in range(2)]
    r_xr = [Res() for _ in range(2)]
    ws = WStream(c, [(w_out, jo * 128, 128, 2048) for jo in range(16)])
    it = 0
    for jo in range(16):
        wt, rw = ws.get(jo)
        b = jo % 2
        kb.dma("sp", xr[b][:], xT[jo * 128:(jo + 1) * 128, :], writes=[r_xr[b]])
        for g in range(4):
            bk = it % 4
            it += 1
            sl = slice(g * 512, (g + 1) * 512)
            for kc in range(16):
                c.mm(c.banks[bk][:, :], wt[:, kc, :], catT[:, kc, sl], kc == 0, kc == 15, [rw, r_cat], [c.bres[bk]])
            c.tt(xr[b][:, sl], xr[b][:, sl], c.banks[bk][:, :], ALU.add, [c.bres[bk]], [r_xr[b]])
        kb.dma("sp", x1T[jo * 128:(jo + 1) * 128, :], xr[b][:], reads=[r_xr[b]], writes=[r_x1T], join=True)
    kb.barrier()
    kb.release(m)


def hgrn_heads(c, hT, r_hT, catT, r_cat, w_in):
    kb = c.kb
    G = 512
    qA = kb.sbuf("h_qA", [128, T], BF16)
    qB = kb.sbuf("h_qB", [128, T], BF16)
    kT = kb.sbuf("h_kT", [128, T], BF16)
    ebl = kb.sbuf("h_ebl", [128, 32], F32)
    vtok = kb.sbuf("h_v", [128, 16, 128], BF16)
    sgt = kb.sbuf("h_sg", [128, 16, 128], BF16)
    oall = kb.sbuf("h_oall", [128, 16, 128], F32)
    osq = kb.sbuf("h_osq", [128, 16, 128], F32)
    ofin = kb.sbuf("h_ofin", [128, 16, 128], BF16)
    ss = kb.sbuf("h_ss", [128, 16], F32)
    lbv = kb.sbuf("h_lb", [128, 8], F32)
    oml = kb.sbuf("h_oml", [128, 8], F32)
    noml = kb.sbuf("h_noml", [128, 8], F32)
    r_qA, r_qB, r_kT, r_ebl, r_v, r_sg, r_oall, r_osq, r_ofin, r_ss, r_lb = [Res() for _ in range(11)]
    tmp = {}
    for nm in ("sig", "lf", "b", "enb", "qs"):
        tmp[nm] = [kb.sbuf("h_%s%d" % (nm, i), [128, G], F32) for i in range(1)] * 2
        tmp["r_" + nm] = [Res() for _ in range(1)] * 2
    S = [kb.sbuf("h_S%d" % i, [128, 128], F32) for i in range(2)]
    Sb = [kb.sbuf("h_Sb%d" % i, [128, 128], BF16) for i in range(2)]
    r_S = [Res() for _ in range(2)]
    r_Sb = [Res() for _ in range(2)]
    stmp = kb.sbuf("h_stmp", [128, 128], F32)
    r_stmp = Res()
    ATm = [kb.sbuf("h_ATm%d" % i, [128, 128], BF16) for i in range(2)]
    r_ATm = [Res() for _ in range(2)]
    ktok = [kb.sbuf("h_ktok%d" % i, [128, 128], BF16) for i in range(2)]
    r_ktok = [Res() for _ in range(2)]

    c.tt(lbv[:], c.v("lb0"), c.v("lb1"), ALU.subtract, [c.r_const], [r_lb])
    c.act(lbv[:], lbv[:], AF.Sigmoid, [r_lb], [r_lb])
    c.ts(oml[:], lbv[:], -1.0, 1.0, ALU.mult, ALU.add, [r_lb], [r_lb])
    c.ts(noml[:], oml[:], -1.0, None, ALU.mult, None, [r_lb], [r_lb])
    c.memset(qA[:], 0.0, [r_qA])
    c.memset(qB[:], 0.0, [r_qB])

    tiles = []
    for h in range(8):
        for blk in range(4):
            tiles.append((w_in, blk * 1024 + h * 128, 128, D))
    ws = WStream(c, tiles)
    ident = c.cb("ident")
    bdm = c.cb("bdmask")
    it = 0
    for h in range(8):
        wq, rq = ws.get(4 * h)
        wf, rf = ws.get(4 * h + 1)
        lb_h = lbv[:, h:h + 1]
        oml_h = oml[:, h:h + 1]
        noml_h = noml[:, h:h + 1]
        for g in range(4):
            s = g % 2
            sl = slice(g * G, (g + 1) * G)
            bq, bf = 0 + 2 * s, 1 + 2 * s
            for kc in range(KC):
                c.mm(c.banks[bf][:, :], wf[:, kc, :], hT[:, kc, sl], kc == 0, kc == KC - 1, [rf, r_hT], [c.bres[bf]])
            for kc in range(KC):
                c.mm(c.banks[bq][:, :], wq[:, kc, :], hT[:, kc, sl], kc == 0, kc == KC - 1, [rq, r_hT], [c.bres[bq]])
            sig, lf, bb, enb, qs = (tmp[n][s] for n in ("sig", "lf", "b", "enb", "qs"))
            r_sig, r_lf, r_b, r_enb, r_qs = (tmp["r_" + n][s] for n in ("sig", "lf", "b", "enb", "qs"))
            c.act(sig[:], c.banks[bf][:, :], AF.Sigmoid, [c.bres[bf]], [r_sig])
            c.act(qs[:], c.banks[bq][:, :], AF.Silu, [c.bres[bq]], [r_qs])
            c.act(lf[:], sig[:], AF.Ln, [r_sig, r_lb], [r_lf], bias=lb_h, scale=oml_h)
            kb.op("dve", lambda e, o=bb[:], d0=c.mscan[:, 0:G], d1=lf[:]: e.tensor_tensor_scan(o, d0, d1, 0.0, ALU.mult, ALU.add),
                  reads=[r_lf, c.r_mscan], writes=[r_b])
            c.act(lf[:], bb[:], AF.Exp, [r_b], [r_lf])
            c.act(enb[:], bb[:], AF.Exp, [r_b], [r_enb], scale=-1.0)
            c.ts(sig[:], sig[:], noml_h, oml_h, ALU.mult, ALU.add, [r_sig, r_lb], [r_sig])
            c.tt(kT[:, sl], sig[:], enb[:], ALU.mult, [r_sig, r_enb], [r_kT])
            ev4 = lf[:].rearrange("p (a two c) -> p a two c", two=2, c=64)
            qs4 = qs[:].rearrange("p (a two c) -> p a two c", two=2, c=64)
            qA4 = qA[:, sl].rearrange("p (a two c) -> p a two c", two=2, c=64)
            qB4 = qB[:, sl].rearrange("p (a two c) -> p a two c", two=2, c=64)
            c.tt(qA4[:, :, 0, :], qs4[:, :, 0, :], ev4[:, :, 0, :], ALU.mult, [r_qs, r_lf], [r_qA])
            c.tt(qB4[:, :, 1, :], qs4[:, :, 1, :], ev4[:, :, 1, :], ALU.mult, [r_qs, r_lf], [r_qB])
            eb3 = lf[:].rearrange("p (a c) -> p a c", c=64)
            c.copy(ebl[:, g * 8:(g + 1) * 8], eb3[:, :, 63], [r_lf], [r_ebl])
        wi_, ri = ws.get(4 * h + 2)
        wg_, rg = ws.get(4 * h + 3)
        for j in range(16):
            bk = 4 + (j % 2)
            tsl = slice(j * 128, (j + 1) * 128)
            for kc in range(KC):
                c.mm(c.banks[bk][:, 0:128], hT[:, kc, tsl], wi_[:, kc, :], kc == 0, kc == KC - 1, [ri, r_hT], [c.bres[bk]])
            for kc in range(KC):
                c.mm(c.banks[bk][:, 128:256], hT[:, kc, tsl], wg_[:, kc, :], kc == 0, kc == KC - 1, [rg, r_hT], [c.bres[bk]])
            c.copy(vtok[:, j, :], c.banks[bk][:, 0:128], [c.bres[bk]], [r_v])
            c.act(sgt[:, j, :], c.banks[bk][:, 128:256], AF.Silu, [c.bres[bk]], [r_sg])
        c.memset(S[0][:], 0.0, [r_S[0]])
        c.memset(Sb[0][:], 0.0, [r_Sb[0]])
        cur = 0
        for j in range(16):
            a = j % 2
            p0 = j * 128
            bAT, bKT, bO, bKV = 0, 1, 2 + (j % 2), 4 + (j % 2)
            c.mm(c.banks[bAT][:, 0:64], kT[:, p0:p0 + 128], qA[:, p0:p0 + 64], True, True, [r_kT, r_qA], [c.bres[bAT]])
            c.mm(c.banks[bAT][:, 64:128], kT[:, p0:p0 + 128], qB[:, p0 + 64:p0 + 128], True, True, [r_kT, r_qB], [c.bres[bAT]])
            c.tt(ATm[a][:], c.banks[bAT][:, 0:128], bdm, ALU.mult, [c.bres[bAT], c.r_const], [r_ATm[a]])
            pk = c.banks[bKT][:, :].bitcast(BF16)
            c.tr(pk[:, 0:128], kT[:, p0:p0 + 128], ident, [r_kT, c.r_const], [c.bres[bKT]])
            c.act(ktok[a][:], pk[:, 0:128], AF.Copy, [c.bres[bKT]], [r_ktok[a]])
            ob = c.banks[bO]
            c.mm(ob[:, 0:128], ATm[a][:], vtok[:, j, :], True, False, [r_ATm[a], r_v], [c.bres[bO]])
            c.mm(ob[:, 0:128], qA[:, p0:p0 + 128], Sb[cur][:], False, False, [r_qA, r_Sb[cur]], [c.bres[bO]])
            for half in range(2):
                nxt = 1 - cur
                kvb = c.banks[bKV]
                ps = slice(half * 64, half * 64 + 64)
                c.mm(kvb[:, half * 128:(half + 1) * 128], ktok[a][ps, :], vtok[ps, j, :], True, True,
                     [r_ktok[a], r_v], [c.bres[bKV]])
                c.tt(stmp[:], kvb[:, half * 128:(half + 1) * 128], S[cur][:], ALU.add, [c.bres[bKV], r_S[cur]], [r_stmp])
                ecol = ebl[:, 2 * j + half:2 * j + half + 1]
                c.ts(S[nxt][:], stmp[:], ecol, None, ALU.mult, None, [r_stmp, r_ebl], [r_S[nxt]])
                c.act(Sb[nxt][:], stmp[:], AF.Copy, [r_stmp, r_ebl], [r_Sb[nxt]], scale=ecol)
                cur = nxt
                if half == 0:
                    c.mm(ob[:, 0:128], qB[:, p0:p0 + 128], Sb[cur][:], False, True, [r_qB, r_Sb[cur]], [c.bres[bO]])
            c.act(oall[:, j, :], ob[:, 0:128], AF.Copy, [c.bres[bO]], [r_oall])
        c.tt(osq[:], oall[:], oall[:], ALU.mult, [r_oall], [r_osq])
        kb.op("dve", lambda e, o=ss[:], i=osq[:]: e.reduce_sum(o, i, axis=AX.X), reads=[r_osq], writes=[r_ss])
        c.act(ss[:], ss[:], AF.Sqrt, [r_ss], [r_ss], bias=EPS, scale=1.0 / 128)
        c.recip(ss[:], ss[:], [r_ss], [r_ss])
        c.tt(osq[:], oall[:], ss[:].unsqueeze(2).to_broadcast([128, 16, 128]), ALU.mult, [r_oall, r_ss], [r_osq])
        c.tt(ofin[:], osq[:], sgt[:], ALU.mult, [r_osq, r_sg], [r_ofin])

        def put(q0, nj, pb, rb, h=h):
            c.act(catT[:, h, q0 * 128:(q0 + nj) * 128], pb, AF.Copy, [rb, c.r_const], [r_cat], scale=c.v("hgn"))
        transpose_tok_to_fm(c, ofin, r_ofin, 16, put)


def moba_heads(c, hT, r_hT, catT, r_cat, w_in):
    kb = c.kb
    G = 512
    qnT = kb.sbuf("m_qnT", [128, T], BF16)
    knT = kb.sbuf("m_knT", [128, T], BF16)
    qnf = kb.sbuf("m_qnf", [128, T], F32)
    knf = kb.sbuf("m_knf", [128, T], F32)
    ksum = kb.sbuf("m_ksum", [128, 8], F32)
    vext = kb.sbuf("m_vext", [128, 16, 132], BF16)
    negT = kb.sbuf("m_negT", [8, T], BF16)
    mofin = kb.sbuf("m_ofin", [128, 16, 128], BF16)
    r_qnT, r_knT, r_qnf, r_knf, r_ksum, r_vext, r_negT, r_mofin = [Res() for _ in range(8)]
    raw = [kb.sbuf("m_raw%d" % i, [128, G], F32) for i in range(2)]
    sqb = [kb.sbuf("m_sq%d" % i, [128, G], BF16) for i in range(2)]
    rsb = [kb.sbuf("m_rs%d" % i, [128, G], F32) for i in range(2)]
    r_raw = [Res() for _ in range(2)]
    r_sqb = [Res() for _ in range(2)]
    r_rsb = [Res() for _ in range(2)]
    gm = kb.sbuf("m_gm", [128, 8], F32)
    mx = kb.sbuf("m_mx", [128, 8], F32)
    negm = kb.sbuf("m_negm", [128, 8], F32)
    negs = kb.sbuf("m_negs", [8, 128], F32)
    r_gm, r_mx, r_negm, r_negs = [Res() for _ in range(4)]
    PT = [kb.sbuf("m_PT%d" % i, [128, 256], BF16) for i in range(3)]
    r_PT = [Res() for _ in range(3)]
    rinv = kb.sbuf("m_rinv", [128, 2], F32)
    r_rinv = Res()
    c.memset(vext[:], 1.0, [r_vext])
    ones = c.cb("ones")
    identb = c.cb("ident")
    tiles = []
    for h in range(8):
        for blk in range(3):
            tiles.append((w_in, 4096 + blk * 1024 + h * 128, 128, D))
    ws = WStream(c, tiles)
    scale = 128 ** -0.5
    it = 0
    for h in range(8):
        for (wi3, wn, dstb, r_dstb, dstf, r_dstf) in ((3 * h, "qn", qnT, r_qnT, qnf, r_qnf), (3 * h + 1, "kn", knT, r_knT, knf, r_knf)):
            wt, rw = ws.get(wi3)
            for g in range(4):
                s = it % 2
                it += 1
                sl = slice(g * G, (g + 1) * G)
                bp, bs = 2 * s, 2 * s + 1
                for kc in range(KC):
                    c.mm(c.banks[bp][:, :], wt[:, kc, :], hT[:, kc, sl], kc == 0, kc == KC - 1, [rw, r_hT], [c.bres[bp]])
                c.act(raw[s][:], c.banks[bp][:, :], AF.Copy, [c.bres[bp]], [r_raw[s]])
                c.act(sqb[s][:], c.banks[bp][:, :], AF.Square, [c.bres[bp]], [r_sqb[s]])
                c.mm(c.banks[bs][:, :], ones, sqb[s][:], True, True, [r_sqb[s], c.r_const], [c.bres[bs]])
                c.act(rsb[s][:], c.banks[bs][:, :], AF.Sqrt, [c.bres[bs]], [r_rsb[s]], bias=EPS, scale=1.0 / 128)
                c.recip(rsb[s][:], rsb[s][:], [r_rsb[s]], [r_rsb[s]])
                c.stt(dstf[:, sl], raw[s][:], c.v(wn), rsb[s][:], ALU.mult, ALU.mult, [r_raw[s], r_rsb[s], c.r_const], [r_dstf])
                c.copy(dstb[:, sl], dstf[:, sl], [r_dstf], [r_dstb])
        kb.op("dve", lambda e, o=ksum[:], i=knf[:].rearrange("p (n j) -> p n j", j=256): e.reduce_sum(o, i, axis=AX.X),
              reads=[r_knf], writes=[r_ksum])
        wv, rv = ws.get(3 * h + 2)
        for j in range(16):
            bk = 4 + (j % 2)
            tsl = slice(j * 128, (j + 1) * 128)
            for kc in range(KC):
                c.mm(c.banks[bk][:, 0:128], hT[:, kc, tsl], wv[:, kc, :], kc == 0, kc == KC - 1, [rv, r_hT], [c.bres[bk]])
            c.copy(vext[:, j, 0:128], c.banks[bk][:, 0:128], [c.bres[bk]], [r_vext])
        for i in range(8, 16):
            jb = i // 2
            tsl = slice(i * 128, (i + 1) * 128)
            bk = 6
            c.mm(c.banks[bk][:, 0:8], qnf[:, tsl], ksum[:], True, True, [r_qnf, r_ksum], [c.bres[bk]])
            c.memset(gm[:], -1e30, [r_gm])
            c.copy(gm[:, 0:jb], c.banks[bk][:, 0:jb], [c.bres[bk]], [r_gm])
            kb.op("dve", lambda e, o=mx[:], i_=gm[:]: e.max(o, i_), reads=[r_gm], writes=[r_mx])
            c.memset(negm[:], 0.0, [r_negm])
            c.ts(negm[:, 0:jb], gm[:, 0:jb], mx[:, 2:3], NEG, ALU.is_lt, ALU.mult, [r_gm, r_mx], [r_negm])
            c.tr(c.banks[7][0:8, 0:128], negm[:], c.identf[:], [r_negm, c.r_const], [c.bres[7]])
            c.copy(negT[:, tsl], c.banks[7][0:8, 0:128], [c.bres[7]], [r_negT])
        ip = 0
        for jb in range(8):
            qsl = slice(jb * 256, (jb + 1) * 256)
            nkt = 2 * jb + 2
            bo = [0, 1]
            for kt in range(nkt):
                n = kt // 2
                bst = 2 + (ip % 3)
                p = ip % 3
                ip += 1
                own = (n == jb)
                need_mask = (not own) and jb >= 4
                c.mm(c.banks[bst][:, 0:256], knT[:, kt * 128:(kt + 1) * 128], qnT[:, qsl], True, not (own or need_mask),
                     [r_knT, r_qnT], [c.bres[bst]])
                if need_mask:
                    o8, _ = CST["esel8"]
                    c.mm(c.banks[bst][:, 0:256], c.cbf[0:8, o8 + n * 128:o8 + (n + 1) * 128], negT[:, qsl], False, True,
                         [c.r_const, r_negT], [c.bres[bst]])
                if own:
                    cm = c.cb("cmaskA") if kt == 2 * jb else c.cb("cmaskB")
                    c.mm(c.banks[bst][:, 0:256], identb, cm, False, True, [c.r_const], [c.bres[bst]])
                c.act(PT[p][:], c.banks[bst][:, 0:256], AF.Exp, [c.bres[bst]], [r_PT[p]], scale=scale)
                for th in range(2):
                    if kt == 2 * jb + 1 and th == 0:
                        continue
                    last = (kt == 2 * jb) if th == 0 else (kt == 2 * jb + 1)
                    c.mm(c.banks[bo[th]][:, 0:129], PT[p][:, th * 128:(th + 1) * 128], vext[:, kt, 0:129], kt == 0, last,
                         [r_PT[p], r_vext], [c.bres[bo[th]]], signal=True)
            for th in range(2):
                i = 2 * jb + th
                c.recip(rinv[:, th:th + 1], c.banks[bo[th]][:, 128:129], [c.bres[bo[th]]], [r_rinv])
                c.ts(mofin[:, i, :], c.banks[bo[th]][:, 0:128], rinv[:, th:th + 1], None, ALU.mult, None,
                     [c.bres[bo[th]], r_rinv], [r_mofin])

        def put(q0, nj, pb, rb, h=h):
            c.act(catT[:, 8 + h, q0 * 128:(q0 + nj) * 128], pb, AF.Copy, [rb], [r_cat])
        transpose_tok_to_fm(c, mofin, r_mofin, 16, put)


def build(stages=("l0", "ffn0", "l1", "ffn1"), debug=False):
    nc = bass.Bass("TRN2", target_bir_lowering=False)
    dbg = nc.dram_tensor("dbg", [2048, T], F32, kind="ExternalOutput").ap() if debug else None

    def din(name, shape):
        return nc.dram_tensor(name, list(shape), F32, kind="ExternalInput").ap()
    xT = din("xT", [D, T])
    cst = din("cst", [128, NCST])
    vec = din("vec", [128, NVEC])
    w_in_even = din("w_in_even", [D, 7168])
    w_out_even = din("w_out_even", [2048, D])
    w_in_ssm = din("w_in_ssm", [D, 10304])
    w_out_ssm = din("w_out_ssm", [4096, D])
    wg = [din("w_gate%d" % l, [D, DFF]) for l in range(2)]
    wu = [din("w_up%d" % l, [D, DFF]) for l in range(2)]
    wd = [din("w_down%d" % l, [DFF, D]) for l in range(2)]
    names = {"l0": "x1T", "ffn0": "x2T", "l1": "x3T", "ffn1": "yT"}
    last = stages[-1]
    bufs = {}
    for st in ("l0", "ffn0", "l1", "ffn1"):
        kind = "ExternalOutput" if st == last else "Internal"
        bufs[st] = nc.dram_tensor(names[st], [D, T], F32, kind=kind).ap()
    gyT = nc.dram_tensor("gyT", [4096, T], BF16, kind="Internal").ap()
    c = Ctx(nc)
    kb = c.kb
    c.load_consts(cst, vec)
    r_out = {st: Res(names[st]) for st in bufs}
    src = xT
    first = stages[0]
    order = ["l0", "ffn0", "l1", "ffn1"]
    for st in order[order.index(first):order.index(last) + 1]:
        if st == "l0":
            l0_mixer(c, src, bufs[st], r_out[st], w_in_even, w_out_even, cst, dbg)
        elif st == "ffn0":
            ffn_phase(c, src, bufs[st], r_out[st], "ln_ffn0", wg[0], wu[0], wd[0])
        elif st == "l1":
            from_l1 = globals().get("l1_mixer")
            from_l1(c, src, bufs[st], r_out[st], w_in_ssm, w_out_ssm, cst, gyT)
        elif st == "ffn1":
            ffn_phase(c, src, bufs[st], r_out[st], "ln_ffn1", wg[1], wu[1], wd[1])
        src = bufs[st]
    kb.barrier()
    kb.finish()
    kb.close()
    return nc, names[last]


_CACHE = {}


def run_stages(inputs, stages, xT_all, ncores=8, debug=False):
    key = tuple(stages)
    if key not in _CACHE:
        _CACHE[key] = build(stages, debug)
    nc, oname = _CACHE[key]
    cst = _make_cst()
    vec = _make_vec(inputs)
    f = lambda a: np.ascontiguousarray(a, dtype=np.float32)
    shared = {
        "cst": cst, "vec": vec,
        "w_in_even": f(inputs["w_in_even"][0]), "w_out_even": f(inputs["w_out_even"][0]),
        "w_in_ssm": f(inputs["w_in_ssm"][0]), "w_out_ssm": f(inputs["w_out_ssm"][0]),
    }
    for l in range(2):
        shared["w_gate%d" % l] = f(inputs["w_gate"][l])
        shared["w_up%d" % l] = f(inputs["w_up"][l])
        shared["w_down%d" % l] = f(inputs["w_down"][l])
    in_maps = []
    for i in range(ncores):
        d = dict(shared)
        d["xT"] = np.ascontiguousarray(xT_all[i])
        in_maps.append(d)
    res = run_bass_kernel_spmd(nc, in_maps, core_ids=list(range(ncores)))
    if debug:
        return np.stack([r[oname] for r in res.results], 0), res.results[0]["dbg"]
    return np.stack([r[oname] for r in res.results], 0)


def kernel(**inputs):
    x = np.asarray(inputs["x"], dtype=np.float32)
    xT = np.ascontiguousarray(x.transpose(0, 2, 1))
    yT = run_stages(inputs, ("l0", "ffn0", "l1", "ffn1"), xT)
    return np.ascontiguousarray(yT.transpose(0, 2, 1)).astype(np.float32)


def l1_mixer(c, xT, x3T, r_x3T, w_in, w_out, cst, gyT):
    kb = c.kb
    m = kb.mark()
    hT = kb.sbuf("b_hT", [128, KC, T], BF16)
    r_hT = Res()
    rmsnorm_fm(c, xT, "ln_mix1", hT, r_hT, 0, T, "bn_")
    c.load_mscan(cst, "m256")
    c.alloc_w(16 * 128)
    G = 512
    r_gyT = Res("gyT")
    identb = c.cb("ident")
    acsT = kb.sbuf("b_acsT", [64, T], F32)
    dt_tok = kb.sbuf("b_dttok", [128, 16, 64], F32)
    acs_tok = kb.sbuf("b_acstok", [128, 16, 64], F32)
    eacs_tok = kb.sbuf("b_eacstok", [128, 16, 64], F32)
    dec_tok = kb.sbuf("b_dectok", [128, 16, 64], F32)
    eal_b = kb.sbuf("b_ealb", [128, 8, 64], F32)
    m_dt = kb.mark()
    dtT = kb.sbuf("b_dtT", [64, T], F32)
    negA = kb.sbuf("b_negA", [64, 1], F32)
    diag = kb.sbuf("b_diag", [64, 64], F32)
    etmp = kb.sbuf("b_etmp", [64, G], F32)
    r_acsT, r_dtT, r_dttok, r_acstok, r_eacs, r_dec, r_eal, r_negA, r_diag, r_etmp = [Res() for _ in range(10)]
    c.act(negA[:], c.v("alog", rows=64), AF.Exp, [c.r_const], [r_negA])
    c.ts(negA[:], negA[:], -1.0, None, ALU.mult, None, [r_negA], [r_negA])
    wdt, rwdt = c.load_w(w_in, 10240, 64, D)
    for g in range(4):
        sl = slice(g * G, (g + 1) * G)
        bk = g % 2
        for kc in range(KC):
            c.mm(c.banks[bk][0:64, :], wdt[:, kc, :], hT[:, kc, sl], kc == 0, kc == KC - 1, [rwdt, r_hT], [c.bres[bk]])
        c.act(etmp[:], c.banks[bk][0:64, :], AF.Exp, [c.bres[bk], c.r_const], [r_etmp], bias=c.v("dtb", rows=64))
        c.act(dtT[:, sl], etmp[:], AF.Ln, [r_etmp], [r_dtT], bias=1.0)
        c.ts(etmp[:], dtT[:, sl], negA[:, 0:1], None, ALU.mult, None, [r_dtT, r_negA], [r_etmp])
        kb.op("dve", lambda e, o=acsT[:, sl], d0=c.mscan[0:64, 0:G], d1=etmp[:]: e.tensor_tensor_scan(o, d0, d1, 0.0, ALU.mult, ALU.add),
              reads=[r_etmp, c.r_mscan], writes=[r_acsT])
    for j in range(16):
        tsl = slice(j * 128, (j + 1) * 128)
        bk = 6 + (j % 2)
        c.tr(c.banks[bk][:, 0:64], dtT[0:64, tsl], c.identf[0:64, 0:64], [r_dtT, c.r_const], [c.bres[bk]])
        c.tr(c.banks[bk][:, 64:128], acsT[0:64, tsl], c.identf[0:64, 0:64], [r_acsT, c.r_const], [c.bres[bk]])
        c.copy(dt_tok[:, j, :], c.banks[bk][:, 0:64], [c.bres[bk]], [r_dttok])
        c.copy(acs_tok[:, j, :], c.banks[bk][:, 64:128], [c.bres[bk]], [r_acstok])
    c.act(eacs_tok[:], acs_tok[:], AF.Exp, [r_acstok], [r_eacs])
    for ci in range(8):
        col = ci * 256 + 255
        c.ts(diag[:], c.identf[0:64, 0:64], acsT[0:64, col:col + 1], None, ALU.mult, None, [c.r_const, r_acsT], [r_diag])
        bk = 4 + (ci % 2)
        c.mm(c.banks[bk][:, 0:64], c.onesf[0:64, :], diag[:], True, True, [c.r_const, r_diag], [c.bres[bk]])
        c.act(eal_b[:, ci, :], c.banks[bk][:, 0:64], AF.Exp, [c.bres[bk]], [r_eal])
        for i in range(2):
            c.tt(dec_tok[:, 2 * ci + i, :], c.banks[bk][:, 0:64], acs_tok[:, 2 * ci + i, :], ALU.subtract,
                 [c.bres[bk], r_acstok], [r_dec])
    c.act(dec_tok[:], dec_tok[:], AF.Exp, [r_dec], [r_dec])
    kb.barrier()
    kb.release(m_dt)

    xs_tok = kb.sbuf("b_xs", [128, 16, 512], BF16)
    sz_tok = kb.sbuf("b_sz", [128, 16, 512], BF16)
    B_tok = kb.sbuf("b_Btok", [128, 16, 128], BF16)
    BT = kb.sbuf("b_BT", [128, T], BF16)
    CT = kb.sbuf("b_CT", [128, T], BF16)
    featT = kb.sbuf("b_featT", [128, T], BF16)
    raw = kb.sbuf("b_raw", [128, T + 4], F32)
    acc = [kb.sbuf("b_acc%d" % i, [128, 512], F32) for i in range(2)]
    stT = kb.sbuf("b_stT", [128, 512], F32)
    stTb = kb.sbuf("b_stTb", [128, 512], BF16)
    sttmp = kb.sbuf("b_sttmp", [128, 512], F32)
    r_xs, r_sz, r_Btok, r_BT, r_CT, r_stT, r_stTb, r_sttmp, r_rawpad = [Res() for _ in range(9)]
    r_featT = [Res() for _ in range(4)]
    r_raw = [Res() for _ in range(4)]
    r_acc = [Res() for _ in range(2)]
    xdt = [kb.sbuf("b_xdt%d" % i, [128, 512], BF16) for i in range(2)]
    xdd = [kb.sbuf("b_xdd%d" % i, [128, 512], BF16) for i in range(2)]
    cbm = [kb.sbuf("b_cbm%d" % i, [128, 256], BF16) for i in range(2)]
    r_xdt = [Res() for _ in range(2)]
    r_xdd = [Res() for _ in range(2)]
    r_cbm = [Res() for _ in range(2)]
    dd = [kb.sbuf("b_dd%d" % i, [128, 256], F32) for i in range(2)]
    Lb = [kb.sbuf("b_L%d" % i, [128, 256], BF16) for i in range(2)]
    Gb = [kb.sbuf("b_G%d" % i, [128, 256], BF16) for i in range(4)]
    r_dd = [Res() for _ in range(2)]
    r_L = [Res() for _ in range(2)]
    r_G = [Res() for _ in range(4)]
    t1 = [kb.sbuf("b_t1%d" % i, [128, 512], F32) for i in range(2)]
    t2 = [kb.sbuf("b_t2%d" % i, [128, 512], F32) for i in range(1)] * 2
    gyn = [kb.sbuf("b_gyn%d" % i, [128, 512], BF16) for i in range(2)]
    ssg = [kb.sbuf("b_ssg%d" % i, [128, 1], F32) for i in range(2)]
    gst = [kb.sbuf("b_gst%d" % i, [128, 4, 256], BF16) for i in range(2)]
    r_t1 = [Res() for _ in range(2)]
    r_t2 = [Res() for _ in range(1)] * 2
    r_gyn = [Res() for _ in range(2)]
    r_sqj = Res()
    r_ssg = [Res() for _ in range(2)]
    r_gst = [Res() for _ in range(2)]
    c.memset(raw[:, 0:3], 0.0, [r_rawpad])
    cmul = [c.cb("cmul0"), c.cb("cmul1")]

    tiles = []
    for gi in range(8):
        for ft in range(4):
            tiles.append((w_in, 4096 + gi * 512 + ft * 128, 128, D))
        tiles.append((w_in, 8192 + gi * 128, 128, D))
        tiles.append((w_in, 9216 + gi * 128, 128, D))
        for ft in range(4):
            tiles.append((w_in, gi * 512 + ft * 128, 128, D))
    ws = WStream(c, tiles)
    pit = 0
    for gi in range(8):
        hs = slice(gi * 8, (gi + 1) * 8)

        def fm_tile(widx, ch_tile, dst, r_dst4):
            nonlocal pit
            wt, rw = ws.get(widx)
            cw = lambda k: c.v("convw", c0=ch_tile * 4 + k, n=1)
            for g in range(4):
                bk = pit % 2
                ab = pit % 2
                pit += 1
                sl = slice(g * G, (g + 1) * G)
                for kc in range(KC):
                    c.mm(c.banks[bk][:, :], wt[:, kc, :], hT[:, kc, sl], kc == 0, kc == KC - 1, [rw, r_hT], [c.bres[bk]])
                c.act(raw[:, 3 + g * G:3 + (g + 1) * G], c.banks[bk][:, :], AF.Copy, [c.bres[bk]], [r_raw[g]])
                rr = [r_raw[g], c.r_const] + ([r_raw[g - 1]] if g > 0 else [r_rawpad])
                c.ts(acc[ab][:], raw[:, g * G:(g + 1) * G], cw(0), c.v("convb", c0=ch_tile, n=1), ALU.mult, ALU.add, rr, [r_acc[ab]])
                for k in range(1, 4):
                    c.stt(acc[ab][:], raw[:, g * G + k:(g + 1) * G + k], cw(k), acc[ab][:], ALU.mult, ALU.add,
                          rr + [r_acc[ab]], [r_acc[ab]])
                c.act(dst[:, sl], acc[ab][:], AF.Silu, [r_acc[ab]], [r_dst4[g]])

        for ft in range(4):
            fm_tile(gi * 10 + ft, gi * 4 + ft, featT, r_featT)

            def putx(q0, nj, pb, rb, ft=ft):
                c.copy(xs_tok[:, q0:q0 + nj, ft * 128:(ft + 1) * 128], pb.rearrange("p (j f) -> p j f", f=128), [rb], [r_xs])
            transpose_fm_to_tok(c, featT, r_featT, putx)
        fm_tile(gi * 10 + 4, 32 + gi, BT, [r_BT] * 4)

        def putb(q0, nj, pb, rb):
            c.copy(B_tok[:, q0:q0 + nj, :], pb.rearrange("p (j f) -> p j f", f=128), [rb], [r_Btok])
        transpose_fm_to_tok(c, BT, [r_BT] * 4, putb)
        fm_tile(gi * 10 + 5, 40 + gi, CT, [r_CT] * 4)
        for ft in range(4):
            wz, rz = ws.get(gi * 10 + 6 + ft)
            for j in range(16):
                bk = 4 + (j % 2)
                tsl = slice(j * 128, (j + 1) * 128)
                for kc in range(KC):
                    c.mm(c.banks[bk][:, 0:128], hT[:, kc, tsl], wz[:, kc, :], kc == 0, kc == KC - 1, [rz, r_hT], [c.bres[bk]])
                c.act(sz_tok[:, j, ft * 128:(ft + 1) * 128], c.banks[bk][:, 0:128], AF.Silu, [c.bres[bk]], [r_sz])
        c.memset(stT[:], 0.0, [r_stT])
        c.memset(stTb[:], 0.0, [r_stTb])
        for ci in range(8):
            csl = slice(ci * 256, (ci + 1) * 256)
            for i in range(2):
                j = 2 * ci + i
                xv = xs_tok[:, j, :].rearrange("p (h d) -> p h d", d=64)
                c.tt(xdt[i][:].rearrange("p (h d) -> p h d", d=64), xv,
                     dt_tok[:, j, hs].unsqueeze(2).to_broadcast([128, 8, 64]), ALU.mult, [r_xs, r_dttok], [r_xdt[i]])
                c.tt(xdd[i][:].rearrange("p (h d) -> p h d", d=64), xdt[i][:].rearrange("p (h d) -> p h d", d=64),
                     dec_tok[:, j, hs].unsqueeze(2).to_broadcast([128, 8, 64]), ALU.mult, [r_xdt[i], r_dec], [r_xdd[i]], eng="pool")
                c.mm(c.banks[i][:, 0:256], BT[:, ci * 256 + i * 128:ci * 256 + (i + 1) * 128], CT[:, csl], True, True,
                     [r_BT, r_CT], [c.bres[i]])
                c.tt(cbm[i][:], c.banks[i][:, 0:256], cmul[i], ALU.mult, [c.bres[i], c.r_const], [r_cbm[i]])
            for tt_ in range(2):
                c.mm(c.banks[2 + tt_][:, :], CT[:, ci * 256 + tt_ * 128:ci * 256 + (tt_ + 1) * 128], stTb[:], True, True,
                     [r_CT, r_stTb], [c.bres[2 + tt_]])
            def emit_arow(hh_):
                H_ = gi * 8 + hh_
                ba_ = 6 + (hh_ % 2)
                c.mm(c.banks[ba_][:, 0:256], c.identf[0:64, H_:H_ + 1].to_broadcast([64, 128]), acsT[0:64, csl], True, True,
                     [c.r_const, r_acsT], [c.bres[ba_]])
            emit_arow(0)
            for hh in range(8):
                H = gi * 8 + hh
                ba = 6 + (hh % 2)
                if hh + 1 < 8:
                    emit_arow(hh + 1)
                g0 = (2 * hh) % 4
                g1 = (2 * hh + 1) % 4
                c.ts(dd[0][:], c.banks[ba][:, 0:256], acs_tok[:, 2 * ci, H:H + 1], 0.0, ALU.subtract, ALU.min,
                     [c.bres[ba], r_acstok], [r_dd[0]])
                c.act(Lb[0][:], dd[0][:], AF.Exp, [r_dd[0]], [r_L[0]])
                c.tt(Gb[g0][:], Lb[0][:], cbm[0][:], ALU.mult, [r_L[0], r_cbm[0]], [r_G[g0]], eng="pool")
                c.ts(dd[1][:, 0:128], c.banks[ba][:, 128:256], acs_tok[:, 2 * ci + 1, H:H + 1], 0.0, ALU.subtract, ALU.min,
                     [c.bres[ba], r_acstok], [r_dd[1]])
                c.act(Lb[1][:, 0:128], dd[1][:, 0:128], AF.Exp, [r_dd[1]], [r_L[1]])
                c.tt(Gb[g1][:, 0:128], Lb[1][:, 0:128], cbm[1][:, 128:256], ALU.mult, [r_L[1], r_cbm[1]], [r_G[g1]], eng="pool")
                hsl = slice(hh * 64, (hh + 1) * 64)
                c.mm(c.banks[4][:, hsl], Gb[g0][:, 0:128], xdt[0][:, hsl], True, True, [r_G[g0], r_xdt[0]], [c.bres[4]])
                c.mm(c.banks[5][:, hsl], Gb[g0][:, 128:256], xdt[0][:, hsl], True, False, [r_G[g0], r_xdt[0]], [c.bres[5]])
                c.mm(c.banks[5][:, hsl], Gb[g1][:, 0:128], xdt[1][:, hsl], False, True, [r_G[g1], r_xdt[1]], [c.bres[5]])
            for tt_ in range(2):
                j = 2 * ci + tt_
                a = tt_
                v3 = lambda ap: ap.rearrange("p (h d) -> p h d", d=64)
                c.tt(v3(t1[a][:]), v3(c.banks[2 + tt_][:, :]), eacs_tok[:, j, hs].unsqueeze(2).to_broadcast([128, 8, 64]), ALU.mult,
                     [c.bres[2 + tt_], r_eacs], [r_t1[a]])
                c.tt(t1[a][:], t1[a][:], c.banks[4 + tt_][:, :], ALU.add, [c.bres[4 + tt_]], [r_t1[a]])
                c.tt(v3(t2[a][:]), v3(xs_tok[:, j, :]), c.v("dskip")[:, hs].unsqueeze(2).to_broadcast([128, 8, 64]), ALU.mult,
                     [r_xs, c.r_const], [r_t2[a]], eng="pool")
                c.tt(t1[a][:], t1[a][:], t2[a][:], ALU.add, [r_t2[a]], [r_t1[a]])
                c.tt(t1[a][:], t1[a][:], sz_tok[:, j, :], ALU.mult, [r_sz], [r_t1[a]])
                c.act(gyn[a][:], t1[a][:], AF.Square, [r_t1[a]], [r_gyn[a], r_ssg[a]], accum_out=ssg[a][:])
                c.act(ssg[a][:], ssg[a][:], AF.Sqrt, [r_ssg[a]], [r_ssg[a]], bias=EPS, scale=1.0 / 512)
                c.recip(ssg[a][:], ssg[a][:], [r_ssg[a]], [r_ssg[a]])
                c.ts(gyn[a][:], t1[a][:], ssg[a][:, 0:1], None, ALU.mult, None, [r_t1[a], r_ssg[a]], [r_gyn[a]])
                bt = tt_
                pb = c.banks[bt][:, :].bitcast(BF16)
                for ft in range(4):
                    c.tr(pb[:, ft * 128:(ft + 1) * 128], gyn[a][:, ft * 128:(ft + 1) * 128], identb, [r_gyn[a], c.r_const], [c.bres[bt]])
                gs = ci % 2
                for ft in range(4):
                    c.act(gst[gs][:, ft, tt_ * 128:(tt_ + 1) * 128], pb[:, ft * 128:(ft + 1) * 128], AF.Copy,
                          [c.bres[bt], c.r_const], [r_gst[gs]], scale=c.v("ssmn", c0=gi * 4 + ft, n=1))
            gs = ci % 2
            for ft in range(4):
                r0 = gi * 512 + ft * 128
                kb.dma("sp", gyT[r0:r0 + 128, csl], gst[gs][:, ft, :], reads=[r_gst[gs]], writes=[r_gyT], join=True)
            for i in range(2):
                c.mm(c.banks[2][:, :], B_tok[:, 2 * ci + i, :], xdd[i][:], i == 0, i == 1, [r_Btok, r_xdd[i]], [c.bres[2]])
            v3 = lambda ap: ap.rearrange("p (h d) -> p h d", d=64)
            c.tt(v3(sttmp[:]), v3(stT[:]), eal_b[:, ci, hs].unsqueeze(2).to_broadcast([128, 8, 64]), ALU.mult,
                 [r_stT, r_eal], [r_sttmp])
            c.tt(stT[:], sttmp[:], c.banks[2][:, :], ALU.add, [r_sttmp, c.bres[2]], [r_stT])
            c.act(stTb[:], stT[:], AF.Copy, [r_stT], [r_stTb])
    kb.barrier()
    kb.release(m)
    m = kb.mark()
    TN = 1024
    gyh = kb.sbuf("b_gyh", [128, 32, TN], BF16)
    r_gyh = Res()
    xr = [kb.sbuf("b_xr%d" % i, [128, TN], F32) for i in range(2)]
    r_xr = [Res() for _ in range(2)]
    c.alloc_w(32 * 128)
    it = 0
    for th in range(2):
        t0 = th * TN
        gsrc = gyT[:, t0:t0 + TN].rearrange("(k p) t -> p k t", p=128)
        for k0 in range(0, 32, 8):
            kb.dma("sp", gyh[:, k0:k0 + 8, :], gsrc[:, k0:k0 + 8, :], reads=[r_gyT], writes=[r_gyh], join=(k0 > 0))
        ws = WStream(c, [(w_out, jo * 128, 128, 4096) for jo in range(16)])
        for jo in range(16):
            wt, rw = ws.get(jo)
            b = jo % 2
            kb.dma("sp", xr[b][:], xT[jo * 128:(jo + 1) * 128, t0:t0 + TN], writes=[r_xr[b]])
            for g in range(TN // 512):
                bk = it % 4
                it += 1
                sl = slice(g * 512, (g + 1) * 512)
                for kc in range(32):
                    c.mm(c.banks[bk][:, :], wt[:, kc, :], gyh[:, kc, sl], kc == 0, kc == 31, [rw, r_gyh], [c.bres[bk]])
                c.tt(xr[b][:, sl], xr[b][:, sl], c.banks[bk][:, :], ALU.add, [c.bres[bk]], [r_xr[b]])
            kb.dma("sp", x3T[jo * 128:(jo + 1) * 128, t0:t0 + TN], xr[b][:], reads=[r_xr[b]], writes=[r_x3T], join=True)
    kb.barrier()
    kb.release(m)


def transpose_fm_to_tok(c, srcT, r_src, dst_fn):
    ident = c.cb("ident")
    for q in range(0, 16, 4):
        bk = 6 + ((q // 4) % 2)
        pb = c.banks[bk][:, :].bitcast(BF16)
        for j in range(4):
            c.tr(pb[:, j * 128:(j + 1) * 128], srcT[:, (q + j) * 128:(q + j + 1) * 128], ident, [r_src[q // 4], c.r_const], [c.bres[bk]])
        dst_fn(q, 4, pb[:, 0:512], c.bres[bk])
```

```python
import numpy as np
import concourse.bass as bass
import concourse.mybir as mybir
from concourse.alu_op_type import AluOpType as ALU
from concourse.bass_utils import run_bass_kernel_spmd

F32 = mybir.dt.float32
BF16 = mybir.dt.bfloat16
AF = mybir.ActivationFunctionType
AX = mybir.AxisListType

T = 2048
D = 2048
KC = 16
DFF = 5632
EPS = 1e-6
NEG = -30000.0


class Res:
    __slots__ = ("name", "w", "r", "dsem", "dcnt", "excl")

    def __init__(self, name="", excl=False):
        self.name = name
        self.excl = excl
        self.w = None
        self.r = []
        self.dsem = None
        self.dcnt = 0


class KB:
    ENGS = ("pe", "act", "dve", "pool", "sp")

    def __init__(self, nc):
        self.nc = nc
        self.q = {e: [] for e in self.ENGS}
        self.sem = {}
        self.cnt = {e: 0 for e in self.ENGS}
        self.seen = {e: {} for e in self.ENGS}
        self._ctx = []
        for e in self.ENGS:
            self.sem[e] = self._enter(nc.semaphore("es_" + e))
        self.nsem = 5
        self.dres = []
        self.ninst = {e: 0 for e in self.ENGS}
        self.pe_pending = []
        self.pe_pend_ids = set()

    def _enter(self, cm):
        v = cm.__enter__()
        self._ctx.append(cm)
        return v

    def mark(self):
        return len(self._ctx)

    def release(self, mark):
        while len(self._ctx) > mark:
            self._ctx.pop().__exit__(None, None, None)

    def sbuf(self, name, shape, dt):
        self.nsb = getattr(self, "nsb", 0) + 1
        return self._enter(self.nc.sbuf_tensor("%s_u%d" % (name, self.nsb), list(shape), dt))

    def psum(self, name, shape, dt):
        return self._enter(self.nc.psum_tensor(name, list(shape), dt))

    def new_sem(self, name):
        self.nsem += 1
        return self._enter(self.nc.semaphore(name))

    def close(self):
        self.release(0)

    def _deps(self, eng, reads, writes):
        deps = []
        own = self.sem[eng]
        for R in reads:
            if R.w is not None:
                deps.append(R.w)
            if R.excl:
                deps.extend(ev for ev in R.r if ev[0] is not own)
        for W in writes:
            if W.w is not None:
                deps.append(W.w)
            deps.extend(W.r)
        out = {}
        for (s, v) in deps:
            if eng == "pe" and s is own:
                continue
            k = id(s)
            if self.seen[eng].get(k, 0) >= v:
                continue
            if k not in out or out[k][1] < v:
                out[k] = (s, v)
        for k, (s, v) in out.items():
            self.seen[eng][k] = v
        return list(out.values())

    def op(self, eng, fn, reads=(), writes=(), signal=True):
        if eng != "pe":
            for X in tuple(reads) + tuple(writes):
                assert id(X) not in self.pe_pend_ids, "resource %s has un-signalled PE accesses" % X.name
        waits = self._deps(eng, reads, writes)
        if eng == "pe" and not signal:
            def emit_ns(e, waits=waits, fn=fn):
                for (s, v) in waits:
                    e.wait_ge(s, v)
                fn(e)
            self.q[eng].append(emit_ns)
            self.pe_pending.append((tuple(reads), tuple(writes)))
            for X in tuple(reads) + tuple(writes):
                self.pe_pend_ids.add(id(X))
            return None
        self.cnt[eng] += 1
        ev = (self.sem[eng], self.cnt[eng])
        sem = self.sem[eng]

        def emit(e, waits=waits, fn=fn, sem=sem):
            for (s, v) in waits:
                e.wait_ge(s, v)
            fn(e).then_inc(sem, 1)

        self.q[eng].append(emit)
        self.ninst[eng] += 1 + len(waits)
        groups = [(reads, writes)]
        if eng == "pe" and self.pe_pending:
            groups = self.pe_pending + groups
            self.pe_pending = []
            self.pe_pend_ids = set()
        for rs, ws_ in groups:
            for R in rs:
                R.r.append(ev)
            for W in ws_:
                W.w = ev
                W.r = []
        return ev

    def dma(self, queue, out, in_, reads=(), writes=(), join=False, **kw):
        for X in tuple(reads) + tuple(writes):
            assert id(X) not in self.pe_pend_ids, "resource %s has un-signalled PE accesses" % X.name
        tgt = writes[0]
        if tgt.dsem is None:
            tgt.dsem = self.new_sem("ds_%d" % self.nsem)
            self.dres.append(tgt)
        if join:
            saved = [(W, W.w) for W in writes]
            for W in writes:
                W.w = None
        waits = self._deps(queue, reads, writes)
        if join:
            for W, w in saved:
                W.w = w
        tgt.dcnt += 16
        ev = (tgt.dsem, tgt.dcnt)
        dsem = tgt.dsem

        def emit(e, waits=waits, out=out, in_=in_, dsem=dsem, kw=kw):
            for (s, v) in waits:
                e.wait_ge(s, v)
            e.dma_start(out=out, in_=in_, **kw).then_inc(dsem, 16)

        self.q[queue].append(emit)
        self.ninst[queue] += 1 + len(waits)
        for R in reads:
            R.r.append(ev)
        for W in writes:
            W.w = ev
            if not join:
                W.r = []
        return ev

    def wait_all(self, eng, events):
        waits = []
        for ev in events:
            if ev is None:
                continue
            s, v = ev
            if self.seen[eng].get(id(s), 0) < v:
                self.seen[eng][id(s)] = v
                waits.append(ev)

        def emit(e, waits=waits):
            for (s, v) in waits:
                e.wait_ge(s, v)

        self.q[eng].append(emit)

    def barrier(self):
        evs = [(self.sem[e], self.cnt[e]) for e in self.ENGS if self.cnt[e] > 0]
        evs += [(r.dsem, r.dcnt) for r in self.dres if r.dcnt > 0]
        for e in self.ENGS:
            self.wait_all(e, [ev for ev in evs if ev[0] is not self.sem[e]])

    def finish(self):
        nc = self.nc
        q = self.q
        with nc.Block() as block:
            @block.tensor
            def _(e):
                for f in q["pe"]:
                    f(e)

            @block.scalar
            def _(e):
                for f in q["act"]:
                    f(e)

            @block.vector
            def _(e):
                for f in q["dve"]:
                    f(e)

            @block.gpsimd
            def _(e):
                for f in q["pool"]:
                    f(e)

            @block.sync
            def _(e):
                for f in q["sp"]:
                    f(e)


def _cst_layout():
    o = {}
    p = 0
    for name, n in (("ident", 128), ("ones", 128), ("bdmask", 128), ("cmaskA", 256), ("cmaskB", 256),
                    ("cmul0", 256), ("cmul1", 256), ("esel8", 1024), ("m64", 512), ("m256", 512)):
        o[name] = (p, n)
        p += n
    return o, p


CST, NCST = _cst_layout()
NCST_BF = CST["m64"][0]


def _make_cst():
    c = np.zeros((128, NCST), np.float32)
    s = np.arange(128)[:, None]
    t = np.arange(128)[None, :]
    tri = (s <= t).astype(np.float32)
    cneg = np.where(s <= t, 0.0, NEG).astype(np.float32)

    def put(name, a):
        o, n = CST[name]
        c[:, o:o + n] = a
    put("ident", np.eye(128, dtype=np.float32))
    put("ones", np.ones((128, 128), np.float32))
    put("bdmask", tri * ((s // 64) == (t // 64)))
    put("cmaskA", np.concatenate([cneg, np.zeros((128, 128), np.float32)], 1))
    put("cmaskB", np.concatenate([np.full((128, 128), NEG, np.float32), cneg], 1))
    put("cmul0", np.concatenate([tri, np.ones((128, 128), np.float32)], 1))
    put("cmul1", np.concatenate([np.zeros((128, 128), np.float32), tri], 1))
    e8 = np.zeros((128, 8, 128), np.float32)
    for n in range(8):
        e8[n, n, :] = 1.0
    put("esel8", e8.reshape(128, 1024))
    tt = np.arange(512)
    put("m64", np.broadcast_to((tt % 64 != 0).astype(np.float32)[None, :], (128, 512)))
    put("m256", np.broadcast_to((tt % 256 != 0).astype(np.float32)[None, :], (128, 512)))
    return c


def _vec_layout():
    o = {}
    p = 0
    for name, n in (("ln_mix0", 16), ("ln_ffn0", 16), ("ln_mix1", 16), ("ln_ffn1", 16), ("lb0", 8), ("lb1", 8),
                    ("hgn", 1), ("qn", 1), ("kn", 1), ("convw", 192), ("convb", 48), ("dtb", 1), ("alog", 1),
                    ("dskip", 64), ("ssmn", 32)):
        o[name] = (p, n)
        p += n
    return o, p


VEC, NVEC = _vec_layout()


def _make_vec(inp):
    v = np.zeros((128, NVEC), np.float32)

    def put(name, a):
        o, n = VEC[name]
        v[:a.shape[0], o:o + n] = a

    def col(w, n):
        return np.ascontiguousarray(w.reshape(n, 128).T)
    put("ln_mix0", col(inp["ln_mix"][0], 16))
    put("ln_ffn0", col(inp["ln_ffn"][0], 16))
    put("ln_mix1", col(inp["ln_mix"][1], 16))
    put("ln_ffn1", col(inp["ln_ffn"][1], 16))
    put("lb0", col(inp["hgrn_lb"][0], 8))
    put("lb1", col(inp["hgrn_lb"][1], 8))
    put("hgn", inp["hgrn_norm"][0].reshape(128, 1))
    put("qn", inp["q_norm"][0].reshape(128, 1))
    put("kn", inp["k_norm"][0].reshape(128, 1))
    put("convw", np.ascontiguousarray(inp["conv_w"][0].reshape(4, 48, 128).transpose(2, 1, 0)).reshape(128, 192))
    put("convb", col(inp["conv_b"][0], 48))
    put("dtb", inp["dt_bias"][0].reshape(64, 1))
    put("alog", inp["a_log"][0].reshape(64, 1))
    put("dskip", np.broadcast_to(inp["d_skip"][0].reshape(1, 64), (128, 64)))
    put("ssmn", col(inp["ssm_norm"][0], 32))
    return v


class Ctx:
    def __init__(self, nc):
        self.nc = nc
        self.kb = KB(nc)
        kb = self.kb
        self.banks = [kb.psum("bank%d" % i, [128, 512], F32) for i in range(8)]
        self.bres = [Res("bank%d" % i, excl=True) for i in range(8)]
        self.cbf = kb.sbuf("cbf", [128, NCST_BF], BF16)
        self.identf = kb.sbuf("identf", [128, 128], F32)
        self.onesf = kb.sbuf("onesf", [128, 128], F32)
        self.vec = kb.sbuf("vec_sb", [128, NVEC], F32)
        self.mscan = kb.sbuf("mscan", [128, 512], F32)
        self.r_const = Res("const")
        self.r_mscan = Res("mscan")
        self.NW = 3
        self.wbuf = None
        self.wi = 0
        self.wgen = 0

    def alloc_w(self, nelem):
        self.wgen += 1
        self.wbuf = [self.kb.sbuf("wbuf%d_%d" % (self.wgen, i), [128, nelem], BF16) for i in range(self.NW)]
        self.wres = [Res("wbuf%d" % i) for i in range(self.NW)]
        self.wi = 0

    def cb(self, name, rows=128):
        o, n = CST[name]
        return self.cbf[0:rows, o:o + n]

    def v(self, name, rows=128, c0=0, n=None):
        o, nn = VEC[name]
        if n is None:
            n = nn - c0
        return self.vec[0:rows, o + c0:o + c0 + n]

    def load_consts(self, cst, vec):
        kb = self.kb
        kb.dma("pool", self.cbf[:], cst[:, 0:NCST_BF], writes=[self.r_const])
        o, n = CST["ident"]
        kb.dma("sp", self.identf[:], cst[:, o:o + n], writes=[self.r_const], join=True)
        o, n = CST["ones"]
        kb.dma("sp", self.onesf[:], cst[:, o:o + n], writes=[self.r_const], join=True)
        kb.dma("sp", self.vec[:], vec, writes=[self.r_const], join=True)

    def load_mscan(self, cst, name):
        o, n = CST[name]
        self.kb.dma("sp", self.mscan[:], cst[:, o:o + n], writes=[self.r_mscan])

    def mm(self, out, lhsT, rhs, start, stop, reads, writes, signal=None):
        if signal is None:
            signal = bool(stop)
        return self.kb.op("pe", lambda e: e.matmul(out, lhsT, rhs, start=start, stop=stop), reads=reads, writes=writes,
                          signal=signal)

    def tr(self, out, in_, ident, reads, writes):
        return self.kb.op("pe", lambda e: e.transpose(out, in_, ident), reads=reads, writes=writes)

    def act(self, out, in_, func, reads, writes, bias=None, scale=None, accum_out=None):
        kw = {}
        if bias is not None:
            kw["bias"] = bias
        if scale is not None:
            kw["scale"] = scale
        if accum_out is not None:
            kw["accum_out"] = accum_out
        return self.kb.op("act", lambda e: e.activation(out=out, in_=in_, func=func, **kw), reads=reads, writes=writes)

    def ts(self, out, in0, s1, s2, op0, op1, reads, writes, eng="dve"):
        if s2 is None:
            return self.kb.op(eng, lambda e: e.tensor_scalar(out, in0, s1, None, op0), reads=reads, writes=writes)
        return self.kb.op(eng, lambda e: e.tensor_scalar(out, in0, s1, s2, op0, op1), reads=reads, writes=writes)

    def tt(self, out, in0, in1, op, reads, writes, eng="dve"):
        return self.kb.op(eng, lambda e: e.tensor_tensor(out, in0, in1, op), reads=reads, writes=writes)

    def stt(self, out, in0, scalar, in1, op0, op1, reads, writes, eng="dve"):
        return self.kb.op(eng, lambda e: e.scalar_tensor_tensor(out, in0, scalar, in1, op0, op1), reads=reads, writes=writes)

    def copy(self, out, in_, reads, writes, eng="dve"):
        return self.kb.op(eng, lambda e: e.tensor_copy(out, in_), reads=reads, writes=writes)

    def recip(self, out, in_, reads, writes):
        return self.kb.op("dve", lambda e: e.reciprocal(out, in_), reads=reads, writes=writes)

    def memset(self, ap, val, writes, eng="dve"):
        return self.kb.op(eng, lambda e: e.memset(ap, val), writes=writes)

    def load_w(self, w, c0, ncols, K):
        kb = self.kb
        i = self.wi
        self.wi = (self.wi + 1) % self.NW
        buf, res = self.wbuf[i], self.wres[i]
        nk = K // 128
        view = buf[:, 0:nk * ncols].rearrange("p (k n) -> p k n", n=ncols)
        src = w[:, c0:c0 + ncols].rearrange("(k p) n -> p k n", p=128)
        first = True
        for k0 in range(0, nk, 4):
            k1 = min(nk, k0 + 4)
            kb.dma("pool", view[:, k0:k1, :], src[:, k0:k1, :], writes=[res], join=not first)
            first = False
        return view, res


class WStream:
    def __init__(self, c, tiles, ahead=1):
        self.c = c
        self.tiles = tiles
        self.loaded = []
        self.ahead = ahead

    def get(self, i):
        while len(self.loaded) < min(len(self.tiles), i + 1 + self.ahead):
            w, c0, n, K = self.tiles[len(self.loaded)]
            self.loaded.append(self.c.load_w(w, c0, n, K))
        return self.loaded[i]


def rmsnorm_fm(c, src, wname, hT, r_hT, t0, tn, pfx):
    kb = c.kb
    m = kb.mark()
    ng = tn // 512
    xt = [kb.sbuf(pfx + "xt%d" % i, [128, tn], F32) for i in range(2)]
    r_xt = [Res() for _ in range(2)]
    sq = [kb.sbuf(pfx + "sq%d" % i, [128, tn], BF16) for i in range(2)]
    r_sq = [Res() for _ in range(2)]
    rstd = kb.sbuf(pfx + "rstd", [128, tn], F32)
    r_rstd = Res()
    ones = c.cb("ones")
    for kc in range(KC):
        b = kc % 2
        kb.dma("sp", xt[b][:], src[kc * 128:(kc + 1) * 128, t0:t0 + tn], writes=[r_xt[b]])
        c.act(sq[b][:], xt[b][:], AF.Square, [r_xt[b]], [r_sq[b]])
        for g in range(ng):
            c.mm(c.banks[g][:, :], ones, sq[b][:, g * 512:(g + 1) * 512], kc == 0, kc == KC - 1,
                 [r_sq[b], c.r_const], [c.bres[g]], signal=(g == ng - 1 or kc == KC - 1))
    for g in range(ng):
        c.act(rstd[:, g * 512:(g + 1) * 512], c.banks[g][:, :], AF.Sqrt, [c.bres[g]], [r_rstd], bias=EPS, scale=1.0 / D)
    c.recip(rstd[:], rstd[:], [r_rstd], [r_rstd])
    for kc in range(KC):
        b = kc % 2
        kb.dma("sp", xt[b][:], src[kc * 128:(kc + 1) * 128, t0:t0 + tn], writes=[r_xt[b]])
        c.stt(hT[:, kc, 0:tn], xt[b][:], c.v(wname, c0=kc, n=1), rstd[:], ALU.mult, ALU.mult,
              [r_xt[b], r_rstd, c.r_const], [r_hT])
    kb.barrier()
    kb.release(m)


def ffn_phase(c, src, dst, r_dst, lname, wg, wu, wd):
    kb = c.kb
    m = kb.mark()
    TN = 1024
    hT = kb.sbuf("f_hT", [128, KC, TN], BF16)
    r_hT = Res()
    actT = kb.sbuf("f_actT", [128, 44, TN], BF16)
    r_act = Res()
    sg = [kb.sbuf("f_sg%d" % i, [128, 512], F32) for i in range(2)]
    r_sg = [Res() for _ in range(2)]
    xr = [kb.sbuf("f_xr%d" % i, [128, TN], F32) for i in range(2)]
    r_xr = [Res() for _ in range(2)]
    c.alloc_w(44 * 128)
    for th in range(T // TN):
        t0 = th * TN
        rmsnorm_fm(c, src, lname, hT, r_hT, t0, TN, "fn%d_" % th)
        tiles = []
        for ft in range(44):
            tiles.append((wg, ft * 128, 128, D))
            tiles.append((wu, ft * 128, 128, D))
        for jo in range(16):
            tiles.append((wd, jo * 128, 128, DFF))
        ws = WStream(c, tiles)
        it = 0
        for ft in range(44):
            wgt, rg = ws.get(2 * ft)
            wut, ru = ws.get(2 * ft + 1)
            for g in range(TN // 512):
                bg = (it % 2) * 2
                bu = bg + 1
                sl = slice(g * 512, (g + 1) * 512)
                for kc in range(KC):
                    c.mm(c.banks[bg][:, :], wgt[:, kc, :], hT[:, kc, sl], kc == 0, kc == KC - 1, [rg, r_hT], [c.bres[bg]])
                for kc in range(KC):
                    c.mm(c.banks[bu][:, :], wut[:, kc, :], hT[:, kc, sl], kc == 0, kc == KC - 1, [ru, r_hT], [c.bres[bu]])
                s = it % 2
                c.act(sg[s][:], c.banks[bg][:, :], AF.Silu, [c.bres[bg]], [r_sg[s]])
                c.tt(actT[:, ft, sl], sg[s][:], c.banks[bu][:, :], ALU.mult, [r_sg[s], c.bres[bu]], [r_act])
                it += 1
        for jo in range(16):
            wdt, rd = ws.get(88 + jo)
            b = jo % 2
            kb.dma("sp", xr[b][:], src[jo * 128:(jo + 1) * 128, t0:t0 + TN], writes=[r_xr[b]])
            for g in range(TN // 512):
                bk = 4 + (it % 2)
                it += 1
                sl = slice(g * 512, (g + 1) * 512)
                for ft in range(44):
                    c.mm(c.banks[bk][:, :], wdt[:, ft, :], actT[:, ft, sl], ft == 0, ft == 43, [rd, r_act], [c.bres[bk]])
                c.tt(xr[b][:, sl], xr[b][:, sl], c.banks[bk][:, :], ALU.add, [c.bres[bk]], [r_xr[b]])
            kb.dma("sp", dst[jo * 128:(jo + 1) * 128, t0:t0 + TN], xr[b][:], reads=[r_xr[b]], writes=[r_dst], join=True)
    kb.barrier()
    kb.release(m)


def transpose_tok_to_fm(c, src_tok, r_src, ntile, dst_fn, bank0=6):
    ident = c.cb("ident")
    for q in range(0, ntile, 4):
        nj = min(4, ntile - q)
        bk = bank0 + ((q // 4) % 2)
        pb = c.banks[bk][:, :].bitcast(BF16)
        for j in range(nj):
            c.tr(pb[:, j * 128:(j + 1) * 128], src_tok[:, q + j, :], ident, [r_src, c.r_const], [c.bres[bk]])
        dst_fn(q, nj, pb[:, 0:nj * 128], c.bres[bk])


def l0_mixer(c, xT, x1T, r_x1T, w_in, w_out, cst, dbg=None):
    kb = c.kb
    m = kb.mark()
    hT = kb.sbuf("a_hT", [128, KC, T], BF16)
    r_hT = Res()
    rmsnorm_fm(c, xT, "ln_mix0", hT, r_hT, 0, T, "an_")
    catT = kb.sbuf("a_catT", [128, 16, T], BF16)
    r_cat = Res()
    c.alloc_w(16 * 128)
    c.load_mscan(cst, "m64")
    import os
    m2 = kb.mark()
    if not os.environ.get("K_SKIP_HGRN"):
        hgrn_heads(c, hT, r_hT, catT, r_cat, w_in)
    kb.barrier()
    kb.release(m2)
    m2 = kb.mark()
    if not os.environ.get("K_SKIP_MOBA"):
        moba_heads(c, hT, r_hT, catT, r_cat, w_in)
    kb.barrier()
    kb.release(m2)
    if dbg is not None:
        r_dbg = Res()
        for hh in range(16):
            kb.dma("pool", dbg[hh * 128:(hh + 1) * 128, :], catT[:, hh, :], reads=[r_cat], writes=[r_dbg], join=True)
    xr = [kb.sbuf("a_xr%d" % i, [128, T], F32) for i in range(2)]
    r_xr = [Res() for _ in range(2)]
    ws = WStream(c, [(w_out, jo * 128, 128, 2048) for jo in range(16)])
    it = 0
    for jo in range(16):
        wt, rw = ws.get(jo)
        b = jo % 2
        kb.dma("sp", xr[b][:], xT[jo * 128:(jo + 1) * 128, :], writes=[r_xr[b]])
        for g in range(4):
            bk = it % 4
            it += 1
            sl = slice(g * 512, (g + 1) * 512)
            for kc in range(16):
                c.mm(c.banks[bk][:, :], wt[:, kc, :], catT[:, kc, sl], kc == 0, kc == 15, [rw, r_cat], [c.bres[bk]])
            c.tt(xr[b][:, sl], xr[b][:, sl], c.banks[bk][:, :], ALU.add, [c.bres[bk]], [r_xr[b]])
        kb.dma("sp", x1T[jo * 128:(jo + 1) * 128, :], xr[b][:], reads=[r_xr[b]], writes=[r_x1T], join=True)
    kb.barrier()
    kb.release(m)


def hgrn_heads(c, hT, r_hT, catT, r_cat, w_in):
    kb = c.kb
    G = 512
    qA = kb.sbuf("h_qA", [128, T], BF16)
    qB = kb.sbuf("h_qB", [128, T], BF16)
    kT = kb.sbuf("h_kT", [128, T], BF16)
    ebl = kb.sbuf("h_ebl", [128, 32], F32)
    vtok = kb.sbuf("h_v", [128, 16, 128], BF16)
    sgt = kb.sbuf("h_sg", [128, 16, 128], BF16)
    oall = kb.sbuf("h_oall", [128, 16, 128], F32)
    osq = kb.sbuf("h_osq", [128, 16, 128], F32)
    ofin = kb.sbuf("h_ofin", [128, 16, 128], BF16)
    ss = kb.sbuf("h_ss", [128, 16], F32)
    lbv = kb.sbuf("h_lb", [128, 8], F32)
    oml = kb.sbuf("h_oml", [128, 8], F32)
    noml = kb.sbuf("h_noml", [128, 8], F32)
    r_qA, r_qB, r_kT, r_ebl, r_v, r_sg, r_oall, r_osq, r_ofin, r_ss, r_lb = [Res() for _ in range(11)]
    tmp = {}
    for nm in ("sig", "lf", "b", "enb", "qs"):
        tmp[nm] = [kb.sbuf("h_%s%d" % (nm, i), [128, G], F32) for i in range(1)] * 2
        tmp["r_" + nm] = [Res() for _ in range(1)] * 2
    S = [kb.sbuf("h_S%d" % i, [128, 128], F32) for i in range(2)]
    Sb = [kb.sbuf("h_Sb%d" % i, [128, 128], BF16) for i in range(2)]
    r_S = [Res() for _ in range(2)]
    r_Sb = [Res() for _ in range(2)]
    stmp = kb.sbuf("h_stmp", [128, 128], F32)
    r_stmp = Res()
    ATm = [kb.sbuf("h_ATm%d" % i, [128, 128], BF16) for i in range(2)]
    r_ATm = [Res() for _ in range(2)]
    ktok = [kb.sbuf("h_ktok%d" % i, [128, 128], BF16) for i in range(2)]
    r_ktok = [Res() for _ in range(2)]

    c.tt(lbv[:], c.v("lb0"), c.v("lb1"), ALU.subtract, [c.r_const], [r_lb])
    c.act(lbv[:], lbv[:], AF.Sigmoid, [r_lb], [r_lb])
    c.ts(oml[:], lbv[:], -1.0, 1.0, ALU.mult, ALU.add, [r_lb], [r_lb])
    c.ts(noml[:], oml[:], -1.0, None, ALU.mult, None, [r_lb], [r_lb])
    c.memset(qA[:], 0.0, [r_qA])
    c.memset(qB[:], 0.0, [r_qB])

    tiles = []
    for h in range(8):
        for blk in range(4):
            tiles.append((w_in, blk * 1024 + h * 128, 128, D))
    ws = WStream(c, tiles)
    ident = c.cb("ident")
    bdm = c.cb("bdmask")
    it = 0
    for h in range(8):
        wq, rq = ws.get(4 * h)
        wf, rf = ws.get(4 * h + 1)
        lb_h = lbv[:, h:h + 1]
        oml_h = oml[:, h:h + 1]
        noml_h = noml[:, h:h + 1]
        for g in range(4):
            s = g % 2
            sl = slice(g * G, (g + 1) * G)
            bq, bf = 0 + 2 * s, 1 + 2 * s
            for kc in range(KC):
                c.mm(c.banks[bf][:, :], wf[:, kc, :], hT[:, kc, sl], kc == 0, kc == KC - 1, [rf, r_hT], [c.bres[bf]])
            for kc in range(KC):
                c.mm(c.banks[bq][:, :], wq[:, kc, :], hT[:, kc, sl], kc == 0, kc == KC - 1, [rq, r_hT], [c.bres[bq]])
            sig, lf, bb, enb, qs = (tmp[n][s] for n in ("sig", "lf", "b", "enb", "qs"))
            r_sig, r_lf, r_b, r_enb, r_qs = (tmp["r_" + n][s] for n in ("sig", "lf", "b", "enb", "qs"))
            c.act(sig[:], c.banks[bf][:, :], AF.Sigmoid, [c.bres[bf]], [r_sig])
            c.act(qs[:], c.banks[bq][:, :], AF.Silu, [c.bres[bq]], [r_qs])
            c.act(lf[:], sig[:], AF.Ln, [r_sig, r_lb], [r_lf], bias=lb_h, scale=oml_h)
            kb.op("dve", lambda e, o=bb[:], d0=c.mscan[:, 0:G], d1=lf[:]: e.tensor_tensor_scan(o, d0, d1, 0.0, ALU.mult, ALU.add),
                  reads=[r_lf, c.r_mscan], writes=[r_b])
            c.act(lf[:], bb[:], AF.Exp, [r_b], [r_lf])
            c.act(enb[:], bb[:], AF.Exp, [r_b], [r_enb], scale=-1.0)
            c.ts(sig[:], sig[:], noml_h, oml_h, ALU.mult, ALU.add, [r_sig, r_lb], [r_sig])
            c.tt(kT[:, sl], sig[:], enb[:], ALU.mult, [r_sig, r_enb], [r_kT])
            ev4 = lf[:].rearrange("p (a two c) -> p a two c", two=2, c=64)
            qs4 = qs[:].rearrange("p (a two c) -> p a two c", two=2, c=64)
            qA4 = qA[:, sl].rearrange("p (a two c) -> p a two c", two=2, c=64)
            qB4 = qB[:, sl].rearrange("p (a two c) -> p a two c", two=2, c=64)
            c.tt(qA4[:, :, 0, :], qs4[:, :, 0, :], ev4[:, :, 0, :], ALU.mult, [r_qs, r_lf], [r_qA])
            c.tt(qB4[:, :, 1, :], qs4[:, :, 1, :], ev4[:, :, 1, :], ALU.mult, [r_qs, r_lf], [r_qB])
            eb3 = lf[:].rearrange("p (a c) -> p a c", c=64)
            c.copy(ebl[:, g * 8:(g + 1) * 8], eb3[:, :, 63], [r_lf], [r_ebl])
        wi_, ri = ws.get(4 * h + 2)
        wg_, rg = ws.get(4 * h + 3)
        for j in range(16):
            bk = 4 + (j % 2)
            tsl = slice(j * 128, (j + 1) * 128)
            for kc in range(KC):
                c.mm(c.banks[bk][:, 0:128], hT[:, kc, tsl], wi_[:, kc, :], kc == 0, kc == KC - 1, [ri, r_hT], [c.bres[bk]])
            for kc in range(KC):
                c.mm(c.banks[bk][:, 128:256], hT[:, kc, tsl], wg_[:, kc, :], kc == 0, kc == KC - 1, [rg, r_hT], [c.bres[bk]])
            c.copy(vtok[:, j, :], c.banks[bk][:, 0:128], [c.bres[bk]], [r_v])
            c.act(sgt[:, j, :], c.banks[bk][:, 128:256], AF.Silu, [c.bres[bk]], [r_sg])
        c.memset(S[0][:], 0.0, [r_S[0]])
        c.memset(Sb[0][:], 0.0, [r_Sb[0]])
        cur = 0
        for j in range(16):
            a = j % 2
            p0 = j * 128
            bAT, bKT, bO, bKV = 0, 1, 2 + (j % 2), 4 + (j % 2)
            c.mm(c.banks[bAT][:, 0:64], kT[:, p0:p0 + 128], qA[:, p0:p0 + 64], True, True, [r_kT, r_qA], [c.bres[bAT]])
            c.mm(c.banks[bAT][:, 64:128], kT[:, p0:p0 + 128], qB[:, p0 + 64:p0 + 128], True, True, [r_kT, r_qB], [c.bres[bAT]])
            c.tt(ATm[a][:], c.banks[bAT][:, 0:128], bdm, ALU.mult, [c.bres[bAT], c.r_const], [r_ATm[a]])
            pk = c.banks[bKT][:, :].bitcast(BF16)
            c.tr(pk[:, 0:128], kT[:, p0:p0 + 128], ident, [r_kT, c.r_const], [c.bres[bKT]])
            c.act(ktok[a][:], pk[:, 0:128], AF.Copy, [c.bres[bKT]], [r_ktok[a]])
            ob = c.banks[bO]
            c.mm(ob[:, 0:128], ATm[a][:], vtok[:, j, :], True, False, [r_ATm[a], r_v], [c.bres[bO]])
            c.mm(ob[:, 0:128], qA[:, p0:p0 + 128], Sb[cur][:], False, False, [r_qA, r_Sb[cur]], [c.bres[bO]])
            for half in range(2):
                nxt = 1 - cur
                kvb = c.banks[bKV]
                ps = slice(half * 64, half * 64 + 64)
                c.mm(kvb[:, half * 128:(half + 1) * 128], ktok[a][ps, :], vtok[ps, j, :], True, True,
                     [r_ktok[a], r_v], [c.bres[bKV]])
                c.tt(stmp[:], kvb[:, half * 128:(half + 1) * 128], S[cur][:], ALU.add, [c.bres[bKV], r_S[cur]], [r_stmp])
                ecol = ebl[:, 2 * j + half:2 * j + half + 1]
                c.ts(S[nxt][:], stmp[:], ecol, None, ALU.mult, None, [r_stmp, r_ebl], [r_S[nxt]])
                c.act(Sb[nxt][:], stmp[:], AF.Copy, [r_stmp, r_ebl], [r_Sb[nxt]], scale=ecol)
                cur = nxt
                if half == 0:
                    c.mm(ob[:, 0:128], qB[:, p0:p0 + 128], Sb[cur][:], False, True, [r_qB, r_Sb[cur]], [c.bres[bO]])
            c.act(oall[:, j, :], ob[:, 0:128], AF.Copy, [c.bres[bO]], [r_oall])
        c.tt(osq[:], oall[:], oall[:], ALU.mult, [r_oall], [r_osq])
        kb.op("dve", lambda e, o=ss[:], i=osq[:]: e.reduce_sum(o, i, axis=AX.X), reads=[r_osq], writes=[r_ss])
        c.act(ss[:], ss[:], AF.Sqrt, [r_ss], [r_ss], bias=EPS, scale=1.0 / 128)
        c.recip(ss[:], ss[:], [r_ss], [r_ss])
        c.tt(osq[:], oall[:], ss[:].unsqueeze(2).to_broadcast([128, 16, 128]), ALU.mult, [r_oall, r_ss], [r_osq])
        c.tt(ofin[:], osq[:], sgt[:], ALU.mult, [r_osq, r_sg], [r_ofin])

        def put(q0, nj, pb, rb, h=h):
            c.act(catT[:, h, q0 * 128:(q0 + nj) * 128], pb, AF.Copy, [rb, c.r_const], [r_cat], scale=c.v("hgn"))
        transpose_tok_to_fm(c, ofin, r_ofin, 16, put)


def moba_heads(c, hT, r_hT, catT, r_cat, w_in):
    kb = c.kb
    G = 512
    qnT = kb.sbuf("m_qnT", [128, T], BF16)
    knT = kb.sbuf("m_knT", [128, T], BF16)
    qnf = kb.sbuf("m_qnf", [128, T], F32)
    knf = kb.sbuf("m_knf", [128, T], F32)
    ksum = kb.sbuf("m_ksum", [128, 8], F32)
    vext = kb.sbuf("m_vext", [128, 16, 132], BF16)
    negT = kb.sbuf("m_negT", [8, T], BF16)
    mofin = kb.sbuf("m_ofin", [128, 16, 128], BF16)
    r_qnT, r_knT, r_qnf, r_knf, r_ksum, r_vext, r_negT, r_mofin = [Res() for _ in range(8)]
    raw = [kb.sbuf("m_raw%d" % i, [128, G], F32) for i in range(2)]
    sqb = [kb.sbuf("m_sq%d" % i, [128, G], BF16) for i in range(2)]
    rsb = [kb.sbuf("m_rs%d" % i, [128, G], F32) for i in range(2)]
    r_raw = [Res() for _ in range(2)]
    r_sqb = [Res() for _ in range(2)]
    r_rsb = [Res() for _ in range(2)]
    gm = kb.sbuf("m_gm", [128, 8], F32)
    mx = kb.sbuf("m_mx", [128, 8], F32)
    negm = kb.sbuf("m_negm", [128, 8], F32)
    negs = kb.sbuf("m_negs", [8, 128], F32)
    r_gm, r_mx, r_negm, r_negs = [Res() for _ in range(4)]
    PT = [kb.sbuf("m_PT%d" % i, [128, 256], BF16) for i in range(3)]
    r_PT = [Res() for _ in range(3)]
    rinv = kb.sbuf("m_rinv", [128, 2], F32)
    r_rinv = Res()
    c.memset(vext[:], 1.0, [r_vext])
    ones = c.cb("ones")
    identb = c.cb("ident")
    tiles = []
    for h in range(8):
        for blk in range(3):
            tiles.append((w_in, 4096 + blk * 1024 + h * 128, 128, D))
    ws = WStream(c, tiles)
    scale = 128 ** -0.5
    it = 0
    for h in range(8):
        for (wi3, wn, dstb, r_dstb, dstf, r_dstf) in ((3 * h, "qn", qnT, r_qnT, qnf, r_qnf), (3 * h + 1, "kn", knT, r_knT, knf, r_knf)):
            wt, rw = ws.get(wi3)
            for g in range(4):
                s = it % 2
                it += 1
                sl = slice(g * G, (g + 1) * G)
                bp, bs = 2 * s, 2 * s + 1
                for kc in range(KC):
                    c.mm(c.banks[bp][:, :], wt[:, kc, :], hT[:, kc, sl], kc == 0, kc == KC - 1, [rw, r_hT], [c.bres[bp]])
                c.act(raw[s][:], c.banks[bp][:, :], AF.Copy, [c.bres[bp]], [r_raw[s]])
                c.act(sqb[s][:], c.banks[bp][:, :], AF.Square, [c.bres[bp]], [r_sqb[s]])
                c.mm(c.banks[bs][:, :], ones, sqb[s][:], True, True, [r_sqb[s], c.r_const], [c.bres[bs]])
                c.act(rsb[s][:], c.banks[bs][:, :], AF.Sqrt, [c.bres[bs]], [r_rsb[s]], bias=EPS, scale=1.0 / 128)
                c.recip(rsb[s][:], rsb[s][:], [r_rsb[s]], [r_rsb[s]])
                c.stt(dstf[:, sl], raw[s][:], c.v(wn), rsb[s][:], ALU.mult, ALU.mult, [r_raw[s], r_rsb[s], c.r_const], [r_dstf])
                c.copy(dstb[:, sl], dstf[:, sl], [r_dstf], [r_dstb])
        kb.op("dve", lambda e, o=ksum[:], i=knf[:].rearrange("p (n j) -> p n j", j=256): e.reduce_sum(o, i, axis=AX.X),
              reads=[r_knf], writes=[r_ksum])
        wv, rv = ws.get(3 * h + 2)
        for j in range(16):
            bk = 4 + (j % 2)
            tsl = slice(j * 128, (j + 1) * 128)
            for kc in range(KC):
                c.mm(c.banks[bk][:, 0:128], hT[:, kc, tsl], wv[:, kc, :], kc == 0, kc == KC - 1, [rv, r_hT], [c.bres[bk]])
            c.copy(vext[:, j, 0:128], c.banks[bk][:, 0:128], [c.bres[bk]], [r_vext])
        for i in range(8, 16):
            jb = i // 2
            tsl = slice(i * 128, (i + 1) * 128)
            bk = 6
            c.mm(c.banks[bk][:, 0:8], qnf[:, tsl], ksum[:], True, True, [r_qnf, r_ksum], [c.bres[bk]])
            c.memset(gm[:], -1e30, [r_gm])
            c.copy(gm[:, 0:jb], c.banks[bk][:, 0:jb], [c.bres[bk]], [r_gm])
            kb.op("dve", lambda e, o=mx[:], i_=gm[:]: e.max(o, i_), reads=[r_gm], writes=[r_mx])
            c.memset(negm[:], 0.0, [r_negm])
            c.ts(negm[:, 0:jb], gm[:, 0:jb], mx[:, 2:3], NEG, ALU.is_lt, ALU.mult, [r_gm, r_mx], [r_negm])
            c.tr(c.banks[7][0:8, 0:128], negm[:], c.identf[:], [r_negm, c.r_const], [c.bres[7]])
            c.copy(negT[:, tsl], c.banks[7][0:8, 0:128], [c.bres[7]], [r_negT])
        ip = 0
        for jb in range(8):
            qsl = slice(jb * 256, (jb + 1) * 256)
            nkt = 2 * jb + 2
            bo = [0, 1]
            for kt in range(nkt):
                n = kt // 2
                bst = 2 + (ip % 3)
                p = ip % 3
                ip += 1
                own = (n == jb)
                need_mask = (not own) and jb >= 4
                c.mm(c.banks[bst][:, 0:256], knT[:, kt * 128:(kt + 1) * 128], qnT[:, qsl], True, not (own or need_mask),
                     [r_knT, r_qnT], [c.bres[bst]])
                if need_mask:
                    o8, _ = CST["esel8"]
                    c.mm(c.banks[bst][:, 0:256], c.cbf[0:8, o8 + n * 128:o8 + (n + 1) * 128], negT[:, qsl], False, True,
                         [c.r_const, r_negT], [c.bres[bst]])
                if own:
                    cm = c.cb("cmaskA") if kt == 2 * jb else c.cb("cmaskB")
                    c.mm(c.banks[bst][:, 0:256], identb, cm, False, True, [c.r_const], [c.bres[bst]])
                c.act(PT[p][:], c.banks[bst][:, 0:256], AF.Exp, [c.bres[bst]], [r_PT[p]], scale=scale)
                for th in range(2):
                    if kt == 2 * jb + 1 and th == 0:
                        continue
                    last = (kt == 2 * jb) if th == 0 else (kt == 2 * jb + 1)
                    c.mm(c.banks[bo[th]][:, 0:129], PT[p][:, th * 128:(th + 1) * 128], vext[:, kt, 0:129], kt == 0, last,
                         [r_PT[p], r_vext], [c.bres[bo[th]]], signal=True)
            for th in range(2):
                i = 2 * jb + th
                c.recip(rinv[:, th:th + 1], c.banks[bo[th]][:, 128:129], [c.bres[bo[th]]], [r_rinv])
                c.ts(mofin[:, i, :], c.banks[bo[th]][:, 0:128], rinv[:, th:th + 1], None, ALU.mult, None,
                     [c.bres[bo[th]], r_rinv], [r_mofin])

        def put(q0, nj, pb, rb, h=h):
            c.act(catT[:, 8 + h, q0 * 128:(q0 + nj) * 128], pb, AF.Copy, [rb], [r_cat])
        transpose_tok_to_fm(c, mofin, r_mofin, 16, put)


def build(stages=("l0", "ffn0", "l1", "ffn1"), debug=False):
    nc = bass.Bass("TRN2", target_bir_lowering=False)
    dbg = nc.dram_tensor("dbg", [2048, T], F32, kind="ExternalOutput").ap() if debug else None

    def din(name, shape):
        return nc.dram_tensor(name, list(shape), F32, kind="ExternalInput").ap()
    xT = din("xT", [D, T])
    cst = din("cst", [128, NCST])
    vec = din("vec", [128, NVEC])
    ssmn_rep = din("ssmn_rep", [128, 4096])
    w_in_even = din("w_in_even", [D, 7168])
    w_out_even = din("w_out_even", [2048, D])
    w_in_ssm = din("w_in_ssm", [D, 10304])
    w_out_ssm = din("w_out_ssm", [4096, D])
    wg = [din("w_gate%d" % l, [D, DFF]) for l in range(2)]
    wu = [din("w_up%d" % l, [D, DFF]) for l in range(2)]
    wd = [din("w_down%d" % l, [DFF, D]) for l in range(2)]
    names = {"l0": "x1T", "ffn0": "x2T", "l1": "x3T", "ffn1": "yT"}
    last = stages[-1]
    bufs = {}
    for st in ("l0", "ffn0", "l1", "ffn1"):
        kind = "ExternalOutput" if st == last else "Internal"
        bufs[st] = nc.dram_tensor(names[st], [D, T], F32, kind=kind).ap()
    gyT = nc.dram_tensor("gyT", [4096, T], BF16, kind="Internal").ap()
    c = Ctx(nc)
    kb = c.kb
    c.load_consts(cst, vec)
    r_out = {st: Res(names[st]) for st in bufs}
    src = xT
    first = stages[0]
    order = ["l0", "ffn0", "l1", "ffn1"]
    for st in order[order.index(first):order.index(last) + 1]:
        if st == "l0":
            l0_mixer(c, src, bufs[st], r_out[st], w_in_even, w_out_even, cst, dbg)
        elif st == "ffn0":
            ffn_phase(c, src, bufs[st], r_out[st], "ln_ffn0", wg[0], wu[0], wd[0])
        elif st == "l1":
            from_l1 = globals().get("l1_mixer")
            from_l1(c, src, bufs[st], r_out[st], w_in_ssm, w_out_ssm, cst, gyT, ssmn_rep)
        elif st == "ffn1":
            ffn_phase(c, src, bufs[st], r_out[st], "ln_ffn1", wg[1], wu[1], wd[1])
        src = bufs[st]
    kb.barrier()
    kb.finish()
    kb.close()
    return nc, names[last]


_CACHE = {}


def run_stages(inputs, stages, xT_all, ncores=8, debug=False):
    key = tuple(stages)
    if key not in _CACHE:
        _CACHE[key] = build(stages, debug)
    nc, oname = _CACHE[key]
    cst = _make_cst()
    vec = _make_vec(inputs)
    f = lambda a: np.ascontiguousarray(a, dtype=np.float32)
    shared = {
        "cst": cst, "vec": vec,
        "ssmn_rep": np.ascontiguousarray(np.broadcast_to(f(inputs["ssm_norm"][0]).reshape(1, 4096), (128, 4096))),
        "w_in_even": f(inputs["w_in_even"][0]), "w_out_even": f(inputs["w_out_even"][0]),
        "w_in_ssm": f(inputs["w_in_ssm"][0]), "w_out_ssm": f(inputs["w_out_ssm"][0]),
    }
    for l in range(2):
        shared["w_gate%d" % l] = f(inputs["w_gate"][l])
        shared["w_up%d" % l] = f(inputs["w_up"][l])
        shared["w_down%d" % l] = f(inputs["w_down"][l])
    in_maps = []
    for i in range(ncores):
        d = dict(shared)
        d["xT"] = np.ascontiguousarray(xT_all[i])
        in_maps.append(d)
    import os
    if os.environ.get("K_TRACE"):
        res = run_bass_kernel_spmd(nc, in_maps, core_ids=list(range(ncores)), trace=True)
        print("EXEC_TIME_NS", res.exec_time_ns)
    else:
        res = run_bass_kernel_spmd(nc, in_maps, core_ids=list(range(ncores)))
    if debug:
        return np.stack([r[oname] for r in res.results], 0), res.results[0]["dbg"]
    return np.stack([r[oname] for r in res.results], 0)


def kernel(**inputs):
    x = np.asarray(inputs["x"], dtype=np.float32)
    xT = np.ascontiguousarray(x.transpose(0, 2, 1))
    yT = run_stages(inputs, ("l0", "ffn0", "l1", "ffn1"), xT)
    return np.ascontiguousarray(yT.transpose(0, 2, 1)).astype(np.float32)


def l1_mixer(c, xT, x3T, r_x3T, w_in, w_out, cst, gyT, ssmn_rep):
    kb = c.kb
    m = kb.mark()
    hT = kb.sbuf("b_hT", [128, KC, T], BF16)
    r_hT = Res()
    rmsnorm_fm(c, xT, "ln_mix1", hT, r_hT, 0, T, "bn_")
    c.load_mscan(cst, "m256")
    c.alloc_w(16 * 128)
    G = 512
    r_gyT = Res("gyT")
    identb = c.cb("ident")
    acs_h = kb.sbuf("b_acsh", [64, T], BF16)
    acs_l = kb.sbuf("b_acsl", [64, T], BF16)
    r_acshl = Res()
    dt_tok = kb.sbuf("b_dttok", [128, 16, 64], F32)
    nacs_tok = kb.sbuf("b_nacstok", [128, 16, 64], F32)
    eacs_tok = kb.sbuf("b_eacstok", [128, 16, 64], F32)
    dec_tok = kb.sbuf("b_dectok", [128, 16, 64], F32)
    eal_b = kb.sbuf("b_ealb", [128, 8, 64], F32)
    m_dt = kb.mark()
    dtT = kb.sbuf("b_dtT", [64, T], F32)
    acsT = kb.sbuf("b_acsT", [64, T], F32)
    negA = kb.sbuf("b_negA", [64, 1], F32)
    diag = kb.sbuf("b_diag", [64, 64], F32)
    etmp = kb.sbuf("b_etmp", [64, G], F32)
    r_acsT, r_dtT, r_dttok, r_acstok, r_eacs, r_dec, r_eal, r_negA, r_diag, r_etmp = [Res() for _ in range(10)]
    c.act(negA[:], c.v("alog", rows=64), AF.Exp, [c.r_const], [r_negA])
    c.ts(negA[:], negA[:], -1.0, None, ALU.mult, None, [r_negA], [r_negA])
    wdt, rwdt = c.load_w(w_in, 10240, 64, D)
    for g in range(4):
        sl = slice(g * G, (g + 1) * G)
        bk = g % 2
        for kc in range(KC):
            c.mm(c.banks[bk][0:64, :], wdt[:, kc, :], hT[:, kc, sl], kc == 0, kc == KC - 1, [rwdt, r_hT], [c.bres[bk]])
        c.act(etmp[:], c.banks[bk][0:64, :], AF.Exp, [c.bres[bk], c.r_const], [r_etmp], bias=c.v("dtb", rows=64))
        c.act(dtT[:, sl], etmp[:], AF.Ln, [r_etmp], [r_dtT], bias=1.0)
        c.ts(etmp[:], dtT[:, sl], negA[:, 0:1], None, ALU.mult, None, [r_dtT, r_negA], [r_etmp])
        kb.op("dve", lambda e, o=acsT[:, sl], d0=c.mscan[0:64, 0:G], d1=etmp[:]: e.tensor_tensor_scan(o, d0, d1, 0.0, ALU.mult, ALU.add),
              reads=[r_etmp, c.r_mscan], writes=[r_acsT])
        c.copy(acs_h[:, sl], acsT[:, sl], [r_acsT], [r_acshl])
        c.tt(etmp[:], acsT[:, sl], acs_h[:, sl], ALU.subtract, [r_acsT, r_acshl], [r_etmp])
        c.copy(acs_l[:, sl], etmp[:], [r_etmp], [r_acshl])
    for j in range(16):
        tsl = slice(j * 128, (j + 1) * 128)
        bk = 6 + (j % 2)
        c.tr(c.banks[bk][:, 0:64], dtT[0:64, tsl], c.identf[0:64, 0:64], [r_dtT, c.r_const], [c.bres[bk]])
        c.tr(c.banks[bk][:, 64:128], acsT[0:64, tsl], c.identf[0:64, 0:64], [r_acsT, c.r_const], [c.bres[bk]])
        c.copy(dt_tok[:, j, :], c.banks[bk][:, 0:64], [c.bres[bk]], [r_dttok])
        c.ts(nacs_tok[:, j, :], c.banks[bk][:, 64:128], -1.0, None, ALU.mult, None, [c.bres[bk]], [r_acstok])
    c.act(eacs_tok[:], nacs_tok[:], AF.Exp, [r_acstok], [r_eacs], scale=-1.0)
    for ci in range(8):
        col = ci * 256 + 255
        c.ts(diag[:], c.identf[0:64, 0:64], acsT[0:64, col:col + 1], None, ALU.mult, None, [c.r_const, r_acsT], [r_diag])
        bk = 4 + (ci % 2)
        c.mm(c.banks[bk][:, 0:64], c.onesf[0:64, :], diag[:], True, True, [c.r_const, r_diag], [c.bres[bk]])
        c.act(eal_b[:, ci, :], c.banks[bk][:, 0:64], AF.Exp, [c.bres[bk]], [r_eal])
        for i in range(2):
            c.tt(dec_tok[:, 2 * ci + i, :], c.banks[bk][:, 0:64], nacs_tok[:, 2 * ci + i, :], ALU.add,
                 [c.bres[bk], r_acstok], [r_dec])
    c.act(dec_tok[:], dec_tok[:], AF.Exp, [r_dec], [r_dec])
    kb.barrier()
    kb.release(m_dt)

    xs_tok = kb.sbuf("b_xs", [128, 16, 512], BF16)
    sz_tok = kb.sbuf("b_sz", [128, 16, 512], BF16)
    ssmn_b = kb.sbuf("b_ssmnb", [128, 512], F32)
    r_ssmnb = Res()
    B_tok = kb.sbuf("b_Btok", [128, 16, 128], BF16)
    BT = kb.sbuf("b_BT", [128, T], BF16)
    CT = kb.sbuf("b_CT", [128, T], BF16)
    stT = kb.sbuf("b_stT", [128, 512], F32)
    stTb = kb.sbuf("b_stTb", [128, 512], BF16)
    sttmp = kb.sbuf("b_sttmp", [128, 512], F32)
    r_xs, r_sz, r_Btok, r_BT, r_CT, r_stT, r_stTb, r_sttmp = [Res() for _ in range(8)]
    cmul = [c.cb("cmul0"), c.cb("cmul1")]

    tiles = []
    for gi in range(8):
        for ft in range(4):
            tiles.append((w_in, 4096 + gi * 512 + ft * 128, 128, D))
        tiles.append((w_in, 8192 + gi * 128, 128, D))
        tiles.append((w_in, 9216 + gi * 128, 128, D))
    ws = WStream(c, tiles)
    pit = 0
    for gi in range(8):
        hs = slice(gi * 8, (gi + 1) * 8)
        mA = kb.mark()
        wz_slab = kb.sbuf("b_wz", [128, 16, 512], BF16)
        featT = kb.sbuf("b_featT", [128, T], BF16)
        raw = kb.sbuf("b_raw", [128, T + 4], F32)
        acc = [kb.sbuf("b_acc%d" % i, [128, 512], F32) for i in range(2)]
        r_wz, r_rawpad = Res(), Res()
        r_featT = [Res() for _ in range(4)]
        r_raw = [Res() for _ in range(4)]
        r_acc = [Res() for _ in range(2)]
        c.memset(raw[:, 0:3], 0.0, [r_rawpad])
        zsrc = w_in[:, gi * 512:(gi + 1) * 512].rearrange("(k p) n -> p k n", p=128)
        for k0 in range(0, 16, 4):
            kb.dma("pool", wz_slab[:, k0:k0 + 4, :], zsrc[:, k0:k0 + 4, :], writes=[r_wz], join=(k0 > 0))

        kb.dma("sp", ssmn_b[:], ssmn_rep[:, gi * 512:(gi + 1) * 512], writes=[r_ssmnb])

        def emit_z(j):
            bk = 4 + (j % 2)
            for kc in range(KC):
                c.mm(c.banks[bk][:, :], hT[:, kc, j * 128:(j + 1) * 128], wz_slab[:, kc, :], kc == 0, kc == KC - 1,
                     [r_wz, r_hT], [c.bres[bk]])
            c.act(sz_tok[:, j, :], c.banks[bk][:, :], AF.Silu, [c.bres[bk]], [r_sz])

        def fm_tile(widx, ch_tile, dst, r_dst4):
            nonlocal pit
            wt, rw = ws.get(widx)
            cw = lambda k: c.v("convw", c0=ch_tile * 4 + k, n=1)
            for g in range(4):
                bk = pit % 2
                ab = pit % 2
                pit += 1
                sl = slice(g * G, (g + 1) * G)
                for kc in range(KC):
                    c.mm(c.banks[bk][:, :], wt[:, kc, :], hT[:, kc, sl], kc == 0, kc == KC - 1, [rw, r_hT], [c.bres[bk]])
                c.act(raw[:, 3 + g * G:3 + (g + 1) * G], c.banks[bk][:, :], AF.Copy, [c.bres[bk]], [r_raw[g]])
                rr = [r_raw[g], c.r_const] + ([r_raw[g - 1]] if g > 0 else [r_rawpad])
                c.ts(acc[ab][:], raw[:, g * G:(g + 1) * G], cw(0), c.v("convb", c0=ch_tile, n=1), ALU.mult, ALU.add, rr, [r_acc[ab]])
                for k in range(1, 4):
                    c.stt(acc[ab][:], raw[:, g * G + k:(g + 1) * G + k], cw(k), acc[ab][:], ALU.mult, ALU.add,
                          rr + [r_acc[ab]], [r_acc[ab]])
                c.act(dst[:, sl], acc[ab][:], AF.Silu, [r_acc[ab]], [r_dst4[g]])

        for ft in range(4):
            fm_tile(gi * 6 + ft, gi * 4 + ft, featT, r_featT)

            def putx(q0, nj, pb, rb, ft=ft):
                c.copy(xs_tok[:, q0:q0 + nj, ft * 128:(ft + 1) * 128], pb.rearrange("p (j f) -> p j f", f=128), [rb], [r_xs])
            transpose_fm_to_tok(c, featT, r_featT, putx)
        fm_tile(gi * 6 + 4, 32 + gi, BT, [r_BT] * 4)

        def putb(q0, nj, pb, rb):
            c.copy(B_tok[:, q0:q0 + nj, :], pb.rearrange("p (j f) -> p j f", f=128), [rb], [r_Btok])
        transpose_fm_to_tok(c, BT, [r_BT] * 4, putb)
        fm_tile(gi * 6 + 5, 40 + gi, CT, [r_CT] * 4)
        for j in range(16):
            emit_z(j)
        kb.barrier()
        kb.release(mA)
        mB = kb.mark()
        xD = [kb.sbuf("b_xD%d" % i, [128, 512], BF16) for i in range(2)]
        xdt = [kb.sbuf("b_xdt%d" % i, [128, 512], BF16) for i in range(2)]
        xdd = [kb.sbuf("b_xdd%d" % i, [128, 512], BF16) for i in range(2)]
        cbm = [kb.sbuf("b_cbm%d" % i, [128, 256], BF16) for i in range(2)]
        Lb = [kb.sbuf("b_L%d" % i, [128, 256], BF16) for i in range(2)]
        Gb = [kb.sbuf("b_G%d" % i, [128, 256], BF16) for i in range(4)]
        t1 = [kb.sbuf("b_t1%d" % i, [128, 512], F32) for i in range(2)]
        gyn = [kb.sbuf("b_gyn%d" % i, [128, 512], BF16) for i in range(2)]
        ssg = [kb.sbuf("b_ssg%d" % i, [128, 1], F32) for i in range(2)]
        gst = [kb.sbuf("b_gst%d" % i, [128, 4, 256], BF16) for i in range(2)]
        r_xD = [Res() for _ in range(2)]
        r_xdt = [Res() for _ in range(2)]
        r_xdd = [Res() for _ in range(2)]
        r_cbm = [Res() for _ in range(2)]
        r_L = [Res() for _ in range(2)]
        r_G = [Res() for _ in range(4)]
        r_t1 = [Res() for _ in range(2)]
        r_gyn = [Res() for _ in range(2)]
        r_ssg = [Res() for _ in range(2)]
        r_gst = [Res() for _ in range(2)]
        if gi == 0:
            print('MAMBA sbuf remaining (loop scope)', c.nc.sbuf_bytes_remaining)
        c.memset(stT[:], 0.0, [r_stT])
        c.memset(stTb[:], 0.0, [r_stTb])
        for ci in range(8):
            csl = slice(ci * 256, (ci + 1) * 256)
            for i in range(2):
                j = 2 * ci + i
                xv = xs_tok[:, j, :].rearrange("p (h d) -> p h d", d=64)
                c.tt(xdt[i][:].rearrange("p (h d) -> p h d", d=64), xv,
                     dt_tok[:, j, hs].unsqueeze(2).to_broadcast([128, 8, 64]), ALU.mult, [r_xs, r_dttok], [r_xdt[i]])
                c.tt(xdd[i][:].rearrange("p (h d) -> p h d", d=64), xdt[i][:].rearrange("p (h d) -> p h d", d=64),
                     dec_tok[:, j, hs].unsqueeze(2).to_broadcast([128, 8, 64]), ALU.mult, [r_xdt[i], r_dec], [r_xdd[i]], eng="pool")
                c.mm(c.banks[0][:, i * 256:(i + 1) * 256], BT[:, ci * 256 + i * 128:ci * 256 + (i + 1) * 128], CT[:, csl], True, True,
                     [r_BT, r_CT], [c.bres[0]])
                c.tt(cbm[i][:], c.banks[0][:, i * 256:(i + 1) * 256], cmul[i], ALU.mult, [c.bres[0], c.r_const], [r_cbm[i]])
            v3 = lambda ap: ap.rearrange("p (h d) -> p h d", d=64)
            for tt_ in range(2):
                j = 2 * ci + tt_
                c.tt(v3(xD[tt_][:]), v3(xs_tok[:, j, :]), c.v("dskip")[:, hs].unsqueeze(2).to_broadcast([128, 8, 64]), ALU.mult,
                     [r_xs, c.r_const], [r_xD[tt_]], eng="pool")
                c.mm(c.banks[2 + tt_][:, :], CT[:, ci * 256 + tt_ * 128:ci * 256 + (tt_ + 1) * 128], stTb[:], True, True,
                     [r_CT, r_stTb], [c.bres[2 + tt_]])
                c.tt(v3(t1[tt_][:]), v3(c.banks[2 + tt_][:, :]), eacs_tok[:, j, hs].unsqueeze(2).to_broadcast([128, 8, 64]), ALU.mult,
                     [c.bres[2 + tt_], r_eacs], [r_t1[tt_]])
                c.mm(c.banks[4 + tt_][:, :], identb, xD[tt_][:], True, False, [c.r_const, r_xD[tt_]], [c.bres[4 + tt_]], signal=True)
            def emit_arow(hh_):
                H_ = gi * 8 + hh_
                ba_ = 6 + (hh_ % 2)
                sel = identb[0:64, H_:H_ + 1].to_broadcast([64, 128])
                c.mm(c.banks[ba_][:, 0:256], sel, acs_h[0:64, csl], True, False, [c.r_const, r_acshl], [c.bres[ba_]])
                c.mm(c.banks[ba_][:, 0:256], sel, acs_l[0:64, csl], False, True, [c.r_const, r_acshl], [c.bres[ba_]])
            emit_arow(0)
            for hh in range(8):
                H = gi * 8 + hh
                ba = 6 + (hh % 2)
                if hh + 1 < 8:
                    emit_arow(hh + 1)
                g0 = (2 * hh) % 4
                g1 = (2 * hh + 1) % 4
                c.act(Lb[0][:], c.banks[ba][:, 0:256], AF.Exp, [c.bres[ba], r_acstok], [r_L[0]], bias=nacs_tok[:, 2 * ci, H:H + 1])
                c.stt(Gb[g0][:], Lb[0][:], 1.0, cbm[0][:], ALU.min, ALU.mult, [r_L[0], r_cbm[0]], [r_G[g0]])
                c.act(Lb[1][:, 0:128], c.banks[ba][:, 128:256], AF.Exp, [c.bres[ba], r_acstok], [r_L[1]],
                      bias=nacs_tok[:, 2 * ci + 1, H:H + 1])
                c.stt(Gb[g1][:, 0:128], Lb[1][:, 0:128], 1.0, cbm[1][:, 128:256], ALU.min, ALU.mult, [r_L[1], r_cbm[1]], [r_G[g1]])
                hsl = slice(hh * 64, (hh + 1) * 64)
                lasth = (hh == 7)
                c.mm(c.banks[4][:, hsl], Gb[g0][:, 0:128], xdt[0][:, hsl], False, lasth, [r_G[g0], r_xdt[0]], [c.bres[4]], signal=True)
                c.mm(c.banks[5][:, hsl], Gb[g0][:, 128:256], xdt[0][:, hsl], False, False, [r_G[g0], r_xdt[0]], [c.bres[5]])
                c.mm(c.banks[5][:, hsl], Gb[g1][:, 0:128], xdt[1][:, hsl], False, lasth, [r_G[g1], r_xdt[1]], [c.bres[5]], signal=True)
            for tt_ in range(2):
                j = 2 * ci + tt_
                a = tt_
                c.tt(t1[a][:], t1[a][:], c.banks[4 + tt_][:, :], ALU.add, [c.bres[4 + tt_]], [r_t1[a]])
                c.tt(t1[a][:], t1[a][:], sz_tok[:, j, :], ALU.mult, [r_sz], [r_t1[a]])
                c.act(gyn[a][:], t1[a][:], AF.Square, [r_t1[a]], [r_gyn[a], r_ssg[a]], accum_out=ssg[a][:])
                c.act(ssg[a][:], ssg[a][:], AF.Ln, [r_ssg[a]], [r_ssg[a]], bias=EPS, scale=1.0 / 512)
                c.act(ssg[a][:], ssg[a][:], AF.Exp, [r_ssg[a]], [r_ssg[a]], scale=-0.5)
                c.stt(gyn[a][:], t1[a][:], ssg[a][:, 0:1], ssmn_b[:], ALU.mult, ALU.mult, [r_t1[a], r_ssg[a], r_ssmnb], [r_gyn[a]])
                bt = 0
                pb = c.banks[bt][:, :].bitcast(BF16)
                for ft in range(4):
                    c.tr(pb[:, ft * 128:(ft + 1) * 128], gyn[a][:, ft * 128:(ft + 1) * 128], identb, [r_gyn[a], c.r_const], [c.bres[bt]])
                gs = ci % 2
                c.act(gst[gs][:, :, tt_ * 128:(tt_ + 1) * 128], pb[:, 0:512].rearrange("p (f t) -> p f t", t=128), AF.Copy,
                      [c.bres[bt]], [r_gst[gs]])
            gs = ci % 2
            for ft in range(4):
                r0 = gi * 512 + ft * 128
                kb.dma("sp", gyT[r0:r0 + 128, csl], gst[gs][:, ft, :], reads=[r_gst[gs]], writes=[r_gyT], join=True)
            for i in range(2):
                c.mm(c.banks[2][:, :], B_tok[:, 2 * ci + i, :], xdd[i][:], i == 0, i == 1, [r_Btok, r_xdd[i]], [c.bres[2]])
            v3 = lambda ap: ap.rearrange("p (h d) -> p h d", d=64)
            c.tt(v3(sttmp[:]), v3(stT[:]), eal_b[:, ci, hs].unsqueeze(2).to_broadcast([128, 8, 64]), ALU.mult,
                 [r_stT, r_eal], [r_sttmp])
            c.tt(stT[:], sttmp[:], c.banks[2][:, :], ALU.add, [r_sttmp, c.bres[2]], [r_stT])
            c.act(stTb[:], stT[:], AF.Copy, [r_stT], [r_stTb])
        kb.barrier()
        kb.release(mB)
    kb.barrier()
    kb.release(m)
    m = kb.mark()
    TN = 1024
    gyh = kb.sbuf("b_gyh", [128, 32, TN], BF16)
    r_gyh = Res()
    xr = [kb.sbuf("b_xr%d" % i, [128, TN], F32) for i in range(2)]
    r_xr = [Res() for _ in range(2)]
    c.alloc_w(32 * 128)
    it = 0
    for th in range(2):
        t0 = th * TN
        gsrc = gyT[:, t0:t0 + TN].rearrange("(k p) t -> p k t", p=128)
        for k0 in range(0, 32, 8):
            kb.dma("sp", gyh[:, k0:k0 + 8, :], gsrc[:, k0:k0 + 8, :], reads=[r_gyT], writes=[r_gyh], join=(k0 > 0))
        ws = WStream(c, [(w_out, jo * 128, 128, 4096) for jo in range(16)])
        for jo in range(16):
            wt, rw = ws.get(jo)
            b = jo % 2
            kb.dma("sp", xr[b][:], xT[jo * 128:(jo + 1) * 128, t0:t0 + TN], writes=[r_xr[b]])
            for g in range(TN // 512):
                bk = it % 4
                it += 1
                sl = slice(g * 512, (g + 1) * 512)
                for kc in range(32):
                    c.mm(c.banks[bk][:, :], wt[:, kc, :], gyh[:, kc, sl], kc == 0, kc == 31, [rw, r_gyh], [c.bres[bk]])
                c.tt(xr[b][:, sl], xr[b][:, sl], c.banks[bk][:, :], ALU.add, [c.bres[bk]], [r_xr[b]])
            kb.dma("sp", x3T[jo * 128:(jo + 1) * 128, t0:t0 + TN], xr[b][:], reads=[r_xr[b]], writes=[r_x3T], join=True)
    kb.barrier()
    kb.release(m)


def transpose_fm_to_tok(c, srcT, r_src, dst_fn):
    ident = c.cb("ident")
    for q in range(0, 16, 4):
        bk = 6 + ((q // 4) % 2)
        pb = c.banks[bk][:, :].bitcast(BF16)
        for j in range(4):
            c.tr(pb[:, j * 128:(j + 1) * 128], srcT[:, (q + j) * 128:(q + j + 1) * 128], ident, [r_src[q // 4], c.r_const], [c.bres[bk]])
        dst_fn(q, 4, pb[:, 0:512], c.bres[bk])
```

```python
import numpy as np
import concourse.bass as bass
import concourse.mybir as mybir
from concourse.alu_op_type import AluOpType as ALU
from concourse.bass_utils import run_bass_kernel_spmd

F32 = mybir.dt.float32
BF16 = mybir.dt.bfloat16
AF = mybir.ActivationFunctionType
AX = mybir.AxisListType

T = 2048
D = 2048
KC = 16
DFF = 5632
EPS = 1e-6
NEG = -30000.0


_CUR_KB = [None]


class Res:
    __slots__ = ("name", "w", "r", "slot", "excl", "level")

    def __init__(self, name="", excl=False):
        self.name = name
        self.excl = excl
        self.w = None
        self.r = []
        self.slot = None
        kb_ = _CUR_KB[0]
        self.level = len(kb_._ctx) if kb_ is not None else 0


class KB:
    ENGS = ("pe", "act", "dve", "pool", "sp")

    def __init__(self, nc):
        self.nc = nc
        self.q = {e: [] for e in self.ENGS}
        self.sem = {}
        self.cnt = {e: 0 for e in self.ENGS}
        self.seen = {e: {} for e in self.ENGS}
        self._ctx = []
        for e in self.ENGS:
            self.sem[e] = self._enter(nc.semaphore("es_" + e))
        self.nsem = 5
        self.all_slots = []
        self.free_slots = []
        self.live_slots = []
        self._gctx = []
        _CUR_KB[0] = self
        self.ninst = {e: 0 for e in self.ENGS}
        self.pe_pending = []
        self.pe_pend_ids = set()
        self.pe_pend_w = set()

    def _enter(self, cm):
        v = cm.__enter__()
        self._ctx.append(cm)
        return v

    def mark(self):
        return len(self._ctx)

    def release(self, mark):
        while len(self._ctx) > mark:
            self._ctx.pop().__exit__(None, None, None)
        keep = []
        for lvl, slot in self.live_slots:
            if lvl > mark:
                self.free_slots.append(slot)
            else:
                keep.append((lvl, slot))
        self.live_slots = keep

    def _slot(self, res):
        if res.slot is None:
            if self.free_slots:
                slot = self.free_slots.pop()
            else:
                self.nsem += 1
                cm = self.nc.semaphore("ds_%d" % self.nsem)
                slot = [cm.__enter__(), 0]
                self._gctx.append(cm)
                self.all_slots.append(slot)
            res.slot = slot
            self.live_slots.append((res.level, slot))
        return res.slot

    def sbuf(self, name, shape, dt):
        self.nsb = getattr(self, "nsb", 0) + 1
        return self._enter(self.nc.sbuf_tensor("%s_u%d" % (name, self.nsb), list(shape), dt))

    def psum(self, name, shape, dt):
        return self._enter(self.nc.psum_tensor(name, list(shape), dt))

    def new_sem(self, name):
        self.nsem += 1
        return self._enter(self.nc.semaphore(name))

    def close(self):
        self.release(0)
        for cm in reversed(self._gctx):
            cm.__exit__(None, None, None)
        self._gctx = []

    def _deps(self, eng, reads, writes):
        deps = []
        own = self.sem[eng]
        for R in reads:
            if R.w is not None:
                deps.append(R.w)
            if R.excl:
                deps.extend(ev for ev in R.r if ev[0] is not own)
        for W in writes:
            if W.w is not None:
                deps.append(W.w)
            deps.extend(W.r)
        out = {}
        for (s, v) in deps:
            if eng == "pe" and s is own:
                continue
            k = id(s)
            if self.seen[eng].get(k, 0) >= v:
                continue
            if k not in out or out[k][1] < v:
                out[k] = (s, v)
        for k, (s, v) in out.items():
            self.seen[eng][k] = v
        return list(out.values())

    def op(self, eng, fn, reads=(), writes=(), signal=True):
        if eng != "pe":
            for X in reads:
                assert id(X) not in self.pe_pend_w, "resource %s has un-signalled PE writes" % X.name
            for X in writes:
                assert id(X) not in self.pe_pend_w and id(X) not in self.pe_pend_ids, \
                    "resource %s has un-signalled PE accesses" % X.name
        waits = self._deps(eng, reads, writes)
        if eng == "pe" and not signal:
            def emit_ns(e, waits=waits, fn=fn):
                for (s, v) in waits:
                    e.wait_ge(s, v)
                fn(e)
            self.q[eng].append(emit_ns)
            self.pe_pending.append((tuple(reads), tuple(writes)))
            for X in tuple(reads) + tuple(writes):
                self.pe_pend_ids.add(id(X))
            for X in writes:
                self.pe_pend_w.add(id(X))
            return None
        self.cnt[eng] += 1
        ev = (self.sem[eng], self.cnt[eng])
        sem = self.sem[eng]

        def emit(e, waits=waits, fn=fn, sem=sem):
            for (s, v) in waits:
                e.wait_ge(s, v)
            fn(e).then_inc(sem, 1)

        self.q[eng].append(emit)
        self.ninst[eng] += 1 + len(waits)
        groups = [(reads, writes)]
        if eng == "pe" and self.pe_pending:
            groups = self.pe_pending + groups
            self.pe_pending = []
            self.pe_pend_ids = set()
            self.pe_pend_w = set()
        for rs, ws_ in groups:
            for R in rs:
                R.r.append(ev)
            for W in ws_:
                W.w = ev
                W.r = []
        return ev

    def dma(self, queue, out, in_, reads=(), writes=(), join=False, **kw):
        for X in reads:
            assert id(X) not in self.pe_pend_w, "resource %s has un-signalled PE writes" % X.name
        for X in writes:
            assert id(X) not in self.pe_pend_ids, "resource %s has un-signalled PE accesses" % X.name
        tgt = writes[0]
        slot = self._slot(tgt)
        if join:
            saved = [(W, W.w) for W in writes]
            for W in writes:
                W.w = None
        waits = self._deps(queue, reads, writes)
        if join:
            for W, w in saved:
                W.w = w
        slot[1] += 16
        ev = (slot[0], slot[1])
        dsem = slot[0]

        def emit(e, waits=waits, out=out, in_=in_, dsem=dsem, kw=kw):
            for (s, v) in waits:
                e.wait_ge(s, v)
            e.dma_start(out=out, in_=in_, **kw).then_inc(dsem, 16)

        self.q[queue].append(emit)
        self.ninst[queue] += 1 + len(waits)
        for R in reads:
            R.r.append(ev)
        for W in writes:
            W.w = ev
            if not join:
                W.r = []
        return ev

    def wait_all(self, eng, events):
        waits = []
        for ev in events:
            if ev is None:
                continue
            s, v = ev
            if self.seen[eng].get(id(s), 0) < v:
                self.seen[eng][id(s)] = v
                waits.append(ev)

        def emit(e, waits=waits):
            for (s, v) in waits:
                e.wait_ge(s, v)

        self.q[eng].append(emit)

    def barrier(self):
        evs = [(self.sem[e], self.cnt[e]) for e in self.ENGS if self.cnt[e] > 0]
        evs += [(sl[0], sl[1]) for sl in self.all_slots if sl[1] > 0]
        for e in self.ENGS:
            self.wait_all(e, [ev for ev in evs if ev[0] is not self.sem[e]])

    def finish(self):
        nc = self.nc
        q = self.q
        with nc.Block() as block:
            @block.tensor
            def _(e):
                for f in q["pe"]:
                    f(e)

            @block.scalar
            def _(e):
                for f in q["act"]:
                    f(e)

            @block.vector
            def _(e):
                for f in q["dve"]:
                    f(e)

            @block.gpsimd
            def _(e):
                for f in q["pool"]:
                    f(e)

            @block.sync
            def _(e):
                for f in q["sp"]:
                    f(e)


def _cst_layout():
    o = {}
    p = 0
    for name, n in (("ident", 128), ("ones", 128), ("bdmask", 128), ("cmaskA", 256), ("cmaskB", 256),
                    ("cmul0", 256), ("cmul1", 256), ("esel8", 1024), ("m64", 512), ("m256", 512)):
        o[name] = (p, n)
        p += n
    return o, p


CST, NCST = _cst_layout()
NCST_BF = CST["m64"][0]


def _make_cst():
    c = np.zeros((128, NCST), np.float32)
    s = np.arange(128)[:, None]
    t = np.arange(128)[None, :]
    tri = (s <= t).astype(np.float32)
    cneg = np.where(s <= t, 0.0, NEG).astype(np.float32)

    def put(name, a):
        o, n = CST[name]
        c[:, o:o + n] = a
    put("ident", np.eye(128, dtype=np.float32))
    put("ones", np.ones((128, 128), np.float32))
    put("bdmask", tri * ((s // 64) == (t // 64)))
    put("cmaskA", np.concatenate([cneg, np.zeros((128, 128), np.float32)], 1))
    put("cmaskB", np.concatenate([np.full((128, 128), NEG, np.float32), cneg], 1))
    put("cmul0", np.concatenate([tri, np.ones((128, 128), np.float32)], 1))
    put("cmul1", np.concatenate([np.zeros((128, 128), np.float32), tri], 1))
    e8 = np.zeros((128, 8, 128), np.float32)
    for n in range(8):
        e8[n, n, :] = 1.0
    put("esel8", e8.reshape(128, 1024))
    tt = np.arange(512)
    put("m64", np.broadcast_to((tt % 64 != 0).astype(np.float32)[None, :], (128, 512)))
    put("m256", np.broadcast_to((tt % 256 != 0).astype(np.float32)[None, :], (128, 512)))
    return c


def _vec_layout():
    o = {}
    p = 0
    for name, n in (("ln_mix0", 16), ("ln_ffn0", 16), ("ln_mix1", 16), ("ln_ffn1", 16), ("lb0", 8), ("lb1", 8),
                    ("hgn", 1), ("qn", 1), ("kn", 1), ("convw", 192), ("convb", 48), ("dtb", 1), ("alog", 1),
                    ("dskip", 64), ("ssmn", 32)):
        o[name] = (p, n)
        p += n
    return o, p


VEC, NVEC = _vec_layout()


def _make_vec(inp):
    v = np.zeros((128, NVEC), np.float32)

    def put(name, a):
        o, n = VEC[name]
        v[:a.shape[0], o:o + n] = a

    def col(w, n):
        return np.ascontiguousarray(w.reshape(n, 128).T)
    put("ln_mix0", col(inp["ln_mix"][0], 16))
    put("ln_ffn0", col(inp["ln_ffn"][0], 16))
    put("ln_mix1", col(inp["ln_mix"][1], 16))
    put("ln_ffn1", col(inp["ln_ffn"][1], 16))
    put("lb0", col(inp["hgrn_lb"][0], 8))
    put("lb1", col(inp["hgrn_lb"][1], 8))
    put("hgn", inp["hgrn_norm"][0].reshape(128, 1))
    put("qn", inp["q_norm"][0].reshape(128, 1))
    put("kn", inp["k_norm"][0].reshape(128, 1))
    put("convw", np.ascontiguousarray(inp["conv_w"][0].reshape(4, 48, 128).transpose(2, 1, 0)).reshape(128, 192))
    put("convb", col(inp["conv_b"][0], 48))
    put("dtb", inp["dt_bias"][0].reshape(64, 1))
    put("alog", inp["a_log"][0].reshape(64, 1))
    put("dskip", np.broadcast_to(inp["d_skip"][0].reshape(1, 64), (128, 64)))
    put("ssmn", col(inp["ssm_norm"][0], 32))
    return v


class Ctx:
    def __init__(self, nc):
        self.nc = nc
        self.kb = KB(nc)
        kb = self.kb
        self.banks = [kb.psum("bank%d" % i, [128, 512], F32) for i in range(8)]
        self.bres = [Res("bank%d" % i, excl=True) for i in range(8)]
        self.cbf = kb.sbuf("cbf", [128, NCST_BF], BF16)
        self.identf = kb.sbuf("identf", [128, 128], F32)
        self.onesf = kb.sbuf("onesf", [128, 128], F32)
        self.vec = kb.sbuf("vec_sb", [128, NVEC], F32)
        self.mscan = kb.sbuf("mscan", [128, 512], F32)
        self.r_const = Res("const")
        self.r_mscan = Res("mscan")
        self.NW = 3
        self.wbuf = None
        self.wi = 0
        self.wgen = 0

    def alloc_w(self, nelem):
        self.wpool = self.new_wpool(nelem)

    def new_wpool(self, nelem):
        self.wgen += 1
        return {"buf": [self.kb.sbuf("wbuf%d_%d" % (self.wgen, i), [128, nelem], BF16) for i in range(self.NW)],
                "res": [Res("wbuf%d" % i) for i in range(self.NW)], "i": 0}

    def cb(self, name, rows=128):
        o, n = CST[name]
        return self.cbf[0:rows, o:o + n]

    def v(self, name, rows=128, c0=0, n=None):
        o, nn = VEC[name]
        if n is None:
            n = nn - c0
        return self.vec[0:rows, o + c0:o + c0 + n]

    def load_consts(self, cst, vec):
        kb = self.kb
        kb.dma("pool", self.cbf[:], cst[:, 0:NCST_BF], writes=[self.r_const])
        o, n = CST["ident"]
        kb.dma("sp", self.identf[:], cst[:, o:o + n], writes=[self.r_const], join=True)
        o, n = CST["ones"]
        kb.dma("sp", self.onesf[:], cst[:, o:o + n], writes=[self.r_const], join=True)
        kb.dma("sp", self.vec[:], vec, writes=[self.r_const], join=True)

    def load_mscan(self, cst, name):
        o, n = CST[name]
        self.kb.dma("sp", self.mscan[:], cst[:, o:o + n], writes=[self.r_mscan])

    def mm(self, out, lhsT, rhs, start, stop, reads, writes, signal=None):
        if signal is None:
            signal = bool(stop)
        return self.kb.op("pe", lambda e: e.matmul(out, lhsT, rhs, start=start, stop=stop), reads=reads, writes=writes,
                          signal=signal)

    def tr(self, out, in_, ident, reads, writes):
        return self.kb.op("pe", lambda e: e.transpose(out, in_, ident), reads=reads, writes=writes)

    def act(self, out, in_, func, reads, writes, bias=None, scale=None, accum_out=None):
        kw = {}
        if bias is not None:
            kw["bias"] = bias
        if scale is not None:
            kw["scale"] = scale
        if accum_out is not None:
            kw["accum_out"] = accum_out
        return self.kb.op("act", lambda e: e.activation(out=out, in_=in_, func=func, **kw), reads=reads, writes=writes)

    def ts(self, out, in0, s1, s2, op0, op1, reads, writes, eng="dve"):
        if s2 is None:
            return self.kb.op(eng, lambda e: e.tensor_scalar(out, in0, s1, None, op0), reads=reads, writes=writes)
        return self.kb.op(eng, lambda e: e.tensor_scalar(out, in0, s1, s2, op0, op1), reads=reads, writes=writes)

    def tt(self, out, in0, in1, op, reads, writes, eng="dve"):
        return self.kb.op(eng, lambda e: e.tensor_tensor(out, in0, in1, op), reads=reads, writes=writes)

    def stt(self, out, in0, scalar, in1, op0, op1, reads, writes, eng="dve"):
        return self.kb.op(eng, lambda e: e.scalar_tensor_tensor(out, in0, scalar, in1, op0, op1), reads=reads, writes=writes)

    def copy(self, out, in_, reads, writes, eng="dve"):
        return self.kb.op(eng, lambda e: e.tensor_copy(out, in_), reads=reads, writes=writes)

    def recip(self, out, in_, reads, writes):
        return self.kb.op("dve", lambda e: e.reciprocal(out, in_), reads=reads, writes=writes)

    def memset(self, ap, val, writes, eng="dve"):
        return self.kb.op(eng, lambda e: e.memset(ap, val), writes=writes)

    def load_w(self, w, c0, ncols, K, pool=None):
        kb = self.kb
        pool = pool if pool is not None else self.wpool
        i = pool["i"]
        pool["i"] = (i + 1) % self.NW
        buf, res = pool["buf"][i], pool["res"][i]
        nk = K // 128
        view = buf[:, 0:nk * ncols].rearrange("p (k n) -> p k n", n=ncols)
        src = w[:, c0:c0 + ncols].rearrange("(k p) n -> p k n", p=128)
        first = True
        for k0 in range(0, nk, 4):
            k1 = min(nk, k0 + 4)
            kb.dma("pool", view[:, k0:k1, :], src[:, k0:k1, :], writes=[res], join=not first)
            first = False
        return view, res


class WStream:
    def __init__(self, c, tiles, ahead=1, pool=None):
        self.c = c
        self.tiles = tiles
        self.loaded = []
        self.ahead = ahead
        self.pool = pool

    def get(self, i):
        while len(self.loaded) < min(len(self.tiles), i + 1 + self.ahead):
            w, c0, n, K = self.tiles[len(self.loaded)]
            self.loaded.append(self.c.load_w(w, c0, n, K, self.pool))
        return self.loaded[i]


def rmsnorm_fm(c, src, wname, hT, r_hT, t0, tn, pfx):
    kb = c.kb
    m = kb.mark()
    ng = tn // 512
    xt = [kb.sbuf(pfx + "xt%d" % i, [128, tn], F32) for i in range(2)]
    r_xt = [Res() for _ in range(2)]
    sq = [kb.sbuf(pfx + "sq%d" % i, [128, tn], BF16) for i in range(2)]
    r_sq = [Res() for _ in range(2)]
    rstd = kb.sbuf(pfx + "rstd", [128, tn], F32)
    r_rstd = Res()
    ones = c.cb("ones")
    for kc in range(KC):
        b = kc % 2
        kb.dma("sp", xt[b][:], src[kc * 128:(kc + 1) * 128, t0:t0 + tn], writes=[r_xt[b]])
        c.act(sq[b][:], xt[b][:], AF.Square, [r_xt[b]], [r_sq[b]])
        for g in range(ng):
            c.mm(c.banks[g][:, :], ones, sq[b][:, g * 512:(g + 1) * 512], kc == 0, kc == KC - 1,
                 [r_sq[b], c.r_const], [c.bres[g]], signal=(g == ng - 1 or kc == KC - 1))
    for g in range(ng):
        c.act(rstd[:, g * 512:(g + 1) * 512], c.banks[g][:, :], AF.Sqrt, [c.bres[g]], [r_rstd], bias=EPS, scale=1.0 / D)
    c.recip(rstd[:], rstd[:], [r_rstd], [r_rstd])
    for kc in range(KC):
        b = kc % 2
        kb.dma("sp", xt[b][:], src[kc * 128:(kc + 1) * 128, t0:t0 + tn], writes=[r_xt[b]])
        c.stt(hT[:, kc, 0:tn], xt[b][:], c.v(wname, c0=kc, n=1), rstd[:], ALU.mult, ALU.mult,
              [r_xt[b], r_rstd, c.r_const], [r_hT])
    kb.barrier()
    kb.release(m)


def ffn_phase(c, src, dst, r_dst, lname, wg, wu, wd):
    kb = c.kb
    m = kb.mark()
    TN = 1024
    hT = kb.sbuf("f_hT", [128, KC, TN], BF16)
    r_hT = Res()
    actT = kb.sbuf("f_actT", [128, 44, TN], BF16)
    r_act = Res()
    sg = [kb.sbuf("f_sg%d" % i, [128, 512], F32) for i in range(2)]
    r_sg = [Res() for _ in range(2)]
    xr = [kb.sbuf("f_xr%d" % i, [128, TN], F32) for i in range(2)]
    r_xr = [Res() for _ in range(2)]
    c.alloc_w(44 * 128)
    for th in range(T // TN):
        t0 = th * TN
        rmsnorm_fm(c, src, lname, hT, r_hT, t0, TN, "fn%d_" % th)
        tiles = []
        for ft in range(44):
            tiles.append((wg, ft * 128, 128, D))
            tiles.append((wu, ft * 128, 128, D))
        for jo in range(16):
            tiles.append((wd, jo * 128, 128, DFF))
        ws = WStream(c, tiles)
        it = 0
        for ft in range(44):
            wgt, rg = ws.get(2 * ft)
            wut, ru = ws.get(2 * ft + 1)
            for g in range(TN // 512):
                bg = (it % 2) * 2
                bu = bg + 1
                sl = slice(g * 512, (g + 1) * 512)
                for kc in range(KC):
                    c.mm(c.banks[bg][:, :], wgt[:, kc, :], hT[:, kc, sl], kc == 0, kc == KC - 1, [rg, r_hT], [c.bres[bg]])
                for kc in range(KC):
                    c.mm(c.banks[bu][:, :], wut[:, kc, :], hT[:, kc, sl], kc == 0, kc == KC - 1, [ru, r_hT], [c.bres[bu]])
                s = it % 2
                c.act(sg[s][:], c.banks[bg][:, :], AF.Silu, [c.bres[bg]], [r_sg[s]])
                c.tt(actT[:, ft, sl], sg[s][:], c.banks[bu][:, :], ALU.mult, [r_sg[s], c.bres[bu]], [r_act])
                it += 1
        for jo in range(16):
            wdt, rd = ws.get(88 + jo)
            b = jo % 2
            kb.dma("sp", xr[b][:], src[jo * 128:(jo + 1) * 128, t0:t0 + TN], writes=[r_xr[b]])
            for g in range(TN // 512):
                bk = 4 + (it % 2)
                it += 1
                sl = slice(g * 512, (g + 1) * 512)
                for ft in range(44):
                    c.mm(c.banks[bk][:, :], wdt[:, ft, :], actT[:, ft, sl], ft == 0, ft == 43, [rd, r_act], [c.bres[bk]])
                c.tt(xr[b][:, sl], xr[b][:, sl], c.banks[bk][:, :], ALU.add, [c.bres[bk]], [r_xr[b]])
            kb.dma("sp", dst[jo * 128:(jo + 1) * 128, t0:t0 + TN], xr[b][:], reads=[r_xr[b]], writes=[r_dst], join=True)
    kb.barrier()
    kb.release(m)


def transpose_tok_to_fm(c, src_tok, r_src, ntile, dst_fn, bank0=6):
    ident = c.cb("ident")
    for q in range(0, ntile, 4):
        nj = min(4, ntile - q)
        bk = bank0 + ((q // 4) % 2)
        pb = c.banks[bk][:, :].bitcast(BF16)
        for j in range(nj):
            c.tr(pb[:, j * 128:(j + 1) * 128], src_tok[:, q + j, :], ident, [r_src, c.r_const], [c.bres[bk]])
        dst_fn(q, nj, pb[:, 0:nj * 128], c.bres[bk])


def l0_mixer(c, xT, x1T, r_x1T, w_in, w_out, cst, catd, dbg=None):
    kb = c.kb
    m = kb.mark()
    hT = kb.sbuf("a_hT", [128, KC, T], BF16)
    r_hT = Res()
    rmsnorm_fm(c, xT, "ln_mix0", hT, r_hT, 0, T, "an_")
    c.load_mscan(cst, "m64")
    r_catd = Res("catd")
    import os
    def run_pair(mk):
        m2 = kb.mark()
        gens = [mk([0, 1, 2, 3], 0), mk([4, 5, 6, 7], 4)]
        alive = [True, True]
        while any(alive):
            for gi_, g_ in enumerate(gens):
                if alive[gi_]:
                    try:
                        next(g_)
                    except StopIteration:
                        alive[gi_] = False
        kb.barrier()
        kb.release(m2)
    if not os.environ.get("K_SKIP_HGRN"):
        run_pair(lambda hs_, b0: hgrn_heads(c, hT, r_hT, catd, r_catd, w_in, c.new_wpool(16 * 128), hs_, b0))
    if not os.environ.get("K_SKIP_MOBA"):
        run_pair(lambda hs_, b0: moba_heads(c, hT, r_hT, catd, r_catd, w_in, c.new_wpool(16 * 128), hs_, b0))
    kb.barrier()
    kb.release(m)
    m = kb.mark()
    TN = 1024
    cath = kb.sbuf("a_cath", [128, 16, TN], BF16)
    r_cath = Res()
    xr = [kb.sbuf("a_xr%d" % i, [128, TN], F32) for i in range(2)]
    r_xr = [Res() for _ in range(2)]
    c.alloc_w(16 * 128)
    it = 0
    for th in range(2):
        t0 = th * TN
        csrc = catd[:, t0:t0 + TN].rearrange("(k p) t -> p k t", p=128)
        for k0 in range(0, 16, 4):
            kb.dma("sp", cath[:, k0:k0 + 4, :], csrc[:, k0:k0 + 4, :], reads=[r_catd], writes=[r_cath], join=(k0 > 0))
        if dbg is not None:
            r_dbg = Res()
            for hh in range(16):
                kb.dma("pool", dbg[hh * 128:(hh + 1) * 128, t0:t0 + TN], cath[:, hh, :], reads=[r_cath], writes=[r_dbg], join=True)
        ws = WStream(c, [(w_out, jo * 128, 128, 2048) for jo in range(16)])
        for jo in range(16):
            wt, rw = ws.get(jo)
            b = jo % 2
            kb.dma("sp", xr[b][:], xT[jo * 128:(jo + 1) * 128, t0:t0 + TN], writes=[r_xr[b]])
            for g in range(TN // 512):
                bk = it % 4
                it += 1
                sl = slice(g * 512, (g + 1) * 512)
                for kc in range(16):
                    c.mm(c.banks[bk][:, :], wt[:, kc, :], cath[:, kc, sl], kc == 0, kc == 15, [rw, r_cath], [c.bres[bk]])
                c.tt(xr[b][:, sl], xr[b][:, sl], c.banks[bk][:, :], ALU.add, [c.bres[bk]], [r_xr[b]])
            kb.dma("sp", x1T[jo * 128:(jo + 1) * 128, t0:t0 + TN], xr[b][:], reads=[r_xr[b]], writes=[r_x1T], join=True)
    kb.barrier()
    kb.release(m)


def hgrn_heads(c, hT, r_hT, catd, r_catd, w_in, wpool, heads, b0):
    kb = c.kb
    G = 512
    qA = kb.sbuf("h_qA", [128, T], BF16)
    qB = kb.sbuf("h_qB", [128, T], BF16)
    kT = kb.sbuf("h_kT", [128, T], BF16)
    ebl = kb.sbuf("h_ebl", [128, 32], F32)
    vtok = kb.sbuf("h_v", [128, 16, 128], BF16)
    sgt = kb.sbuf("h_sg", [128, 16, 128], BF16)
    oall = kb.sbuf("h_oall", [128, 16, 128], F32)
    ofin = kb.sbuf("h_ofin", [128, 16, 128], BF16)
    ss = kb.sbuf("h_ss", [128, 16], F32)
    lbv = kb.sbuf("h_lb", [128, 8], F32)
    oml = kb.sbuf("h_oml", [128, 8], F32)
    noml = kb.sbuf("h_noml", [128, 8], F32)
    r_qA, r_qB, r_kT, r_ebl, r_v, r_sg, r_oall, r_osq, r_ofin, r_ss, r_lb = [Res() for _ in range(11)]
    tmp = {}
    for nm in ("sig", "lf", "b", "enb", "qs"):
        tmp[nm] = [kb.sbuf("h_%s%d" % (nm, i), [128, G], F32) for i in range(1)] * 2
        tmp["r_" + nm] = [Res() for _ in range(1)] * 2
    S = [kb.sbuf("h_S%d" % i, [128, 128], F32) for i in range(2)]
    Sb = [kb.sbuf("h_Sb%d" % i, [128, 128], BF16) for i in range(2)]
    r_S = [Res() for _ in range(2)]
    r_Sb = [Res() for _ in range(2)]
    stmp = kb.sbuf("h_stmp", [128, 128], F32)
    r_stmp = Res()
    ATm = [kb.sbuf("h_ATm%d" % i, [128, 128], BF16) for i in range(2)]
    r_ATm = [Res() for _ in range(2)]
    ktok = [kb.sbuf("h_ktok%d" % i, [128, 128], BF16) for i in range(2)]
    r_ktok = [Res() for _ in range(2)]

    c.tt(lbv[:], c.v("lb0"), c.v("lb1"), ALU.subtract, [c.r_const], [r_lb])
    c.act(lbv[:], lbv[:], AF.Sigmoid, [r_lb], [r_lb])
    c.ts(oml[:], lbv[:], -1.0, 1.0, ALU.mult, ALU.add, [r_lb], [r_lb])
    c.ts(noml[:], oml[:], -1.0, None, ALU.mult, None, [r_lb], [r_lb])
    c.memset(qA[:], 0.0, [r_qA])
    c.memset(qB[:], 0.0, [r_qB])

    tiles = []
    for h in heads:
        for blk in range(4):
            tiles.append((w_in, blk * 1024 + h * 128, 128, D))
    ws = WStream(c, tiles, pool=wpool)
    ident = c.cb("ident")
    bdm = c.cb("bdmask")
    cst_ = [kb.sbuf("h_cst%d" % i, [128, 512], BF16) for i in range(2)]
    r_cst_ = [Res() for _ in range(2)]
    nput = [0]
    it = 0
    for hi_, h in enumerate(heads):
        wq, rq = ws.get(4 * hi_)
        wf, rf = ws.get(4 * hi_ + 1)
        lb_h = lbv[:, h:h + 1]
        oml_h = oml[:, h:h + 1]
        noml_h = noml[:, h:h + 1]
        for g in range(4):
            s = g % 2
            sl = slice(g * G, (g + 1) * G)
            bq, bf = b0, b0 + 1
            for kc in range(KC):
                c.mm(c.banks[bf][:, :], wf[:, kc, :], hT[:, kc, sl], kc == 0, kc == KC - 1, [rf, r_hT], [c.bres[bf]])
            for kc in range(KC):
                c.mm(c.banks[bq][:, :], wq[:, kc, :], hT[:, kc, sl], kc == 0, kc == KC - 1, [rq, r_hT], [c.bres[bq]])
            sig, lf, bb, enb, qs = (tmp[n][s] for n in ("sig", "lf", "b", "enb", "qs"))
            r_sig, r_lf, r_b, r_enb, r_qs = (tmp["r_" + n][s] for n in ("sig", "lf", "b", "enb", "qs"))
            c.act(sig[:], c.banks[bf][:, :], AF.Sigmoid, [c.bres[bf]], [r_sig])
            c.act(qs[:], c.banks[bq][:, :], AF.Silu, [c.bres[bq]], [r_qs])
            c.act(lf[:], sig[:], AF.Ln, [r_sig, r_lb], [r_lf], bias=lb_h, scale=oml_h)
            kb.op("dve", lambda e, o=bb[:], d0=c.mscan[:, 0:G], d1=lf[:]: e.tensor_tensor_scan(o, d0, d1, 0.0, ALU.mult, ALU.add),
                  reads=[r_lf, c.r_mscan], writes=[r_b])
            c.act(lf[:], bb[:], AF.Exp, [r_b], [r_lf])
            c.act(enb[:], bb[:], AF.Exp, [r_b], [r_enb], scale=-1.0)
            c.ts(sig[:], sig[:], noml_h, oml_h, ALU.mult, ALU.add, [r_sig, r_lb], [r_sig])
            c.tt(kT[:, sl], sig[:], enb[:], ALU.mult, [r_sig, r_enb], [r_kT])
            ev4 = lf[:].rearrange("p (a two c) -> p a two c", two=2, c=64)
            qs4 = qs[:].rearrange("p (a two c) -> p a two c", two=2, c=64)
            qA4 = qA[:, sl].rearrange("p (a two c) -> p a two c", two=2, c=64)
            qB4 = qB[:, sl].rearrange("p (a two c) -> p a two c", two=2, c=64)
            c.tt(qA4[:, :, 0, :], qs4[:, :, 0, :], ev4[:, :, 0, :], ALU.mult, [r_qs, r_lf], [r_qA])
            c.tt(qB4[:, :, 1, :], qs4[:, :, 1, :], ev4[:, :, 1, :], ALU.mult, [r_qs, r_lf], [r_qB])
            eb3 = lf[:].rearrange("p (a c) -> p a c", c=64)
            c.copy(ebl[:, g * 8:(g + 1) * 8], eb3[:, :, 63], [r_lf], [r_ebl])
            yield
        wi_, ri = ws.get(4 * hi_ + 2)
        wg_, rg = ws.get(4 * hi_ + 3)
        for j in range(16):
            bk = b0 + 2 + (j % 2)
            tsl = slice(j * 128, (j + 1) * 128)
            for kc in range(KC):
                c.mm(c.banks[bk][:, 0:128], hT[:, kc, tsl], wi_[:, kc, :], kc == 0, kc == KC - 1, [ri, r_hT], [c.bres[bk]])
            for kc in range(KC):
                c.mm(c.banks[bk][:, 128:256], hT[:, kc, tsl], wg_[:, kc, :], kc == 0, kc == KC - 1, [rg, r_hT], [c.bres[bk]])
            c.copy(vtok[:, j, :], c.banks[bk][:, 0:128], [c.bres[bk]], [r_v])
            c.act(sgt[:, j, :], c.banks[bk][:, 128:256], AF.Silu, [c.bres[bk]], [r_sg])
            yield
        c.memset(S[0][:], 0.0, [r_S[0]])
        c.memset(Sb[0][:], 0.0, [r_Sb[0]])
        cur = 0
        for j in range(16):
            a = j % 2
            p0 = j * 128
            bAT, bKT, bO, bKV = b0, b0 + 1, b0 + 2, b0 + 3
            c.mm(c.banks[bAT][:, 0:64], kT[:, p0:p0 + 128], qA[:, p0:p0 + 64], True, True, [r_kT, r_qA], [c.bres[bAT]])
            c.mm(c.banks[bAT][:, 64:128], kT[:, p0:p0 + 128], qB[:, p0 + 64:p0 + 128], True, True, [r_kT, r_qB], [c.bres[bAT]])
            c.tt(ATm[a][:], c.banks[bAT][:, 0:128], bdm, ALU.mult, [c.bres[bAT], c.r_const], [r_ATm[a]])
            pk = c.banks[bKT][:, :].bitcast(BF16)
            c.tr(pk[:, 0:128], kT[:, p0:p0 + 128], ident, [r_kT, c.r_const], [c.bres[bKT]])
            c.act(ktok[a][:], pk[:, 0:128], AF.Copy, [c.bres[bKT]], [r_ktok[a]])
            ob = c.banks[bO]
            c.mm(ob[:, 0:128], ATm[a][:], vtok[:, j, :], True, False, [r_ATm[a], r_v], [c.bres[bO]])
            c.mm(ob[:, 0:128], qA[:, p0:p0 + 128], Sb[cur][:], False, False, [r_qA, r_Sb[cur]], [c.bres[bO]])
            for half in range(2):
                nxt = 1 - cur
                kvb = c.banks[bKV]
                ps = slice(half * 64, half * 64 + 64)
                c.mm(kvb[:, half * 128:(half + 1) * 128], ktok[a][ps, :], vtok[ps, j, :], True, True,
                     [r_ktok[a], r_v], [c.bres[bKV]])
                c.tt(stmp[:], kvb[:, half * 128:(half + 1) * 128], S[cur][:], ALU.add, [c.bres[bKV], r_S[cur]], [r_stmp])
                ecol = ebl[:, 2 * j + half:2 * j + half + 1]
                c.ts(S[nxt][:], stmp[:], ecol, None, ALU.mult, None, [r_stmp, r_ebl], [r_S[nxt]])
                c.act(Sb[nxt][:], stmp[:], AF.Copy, [r_stmp, r_ebl], [r_Sb[nxt]], scale=ecol)
                cur = nxt
                if half == 0:
                    c.mm(ob[:, 0:128], qB[:, p0:p0 + 128], Sb[cur][:], False, True, [r_qB, r_Sb[cur]], [c.bres[bO]])
            c.act(oall[:, j, :], ob[:, 0:128], AF.Copy, [c.bres[bO]], [r_oall])
            yield
        c.tt(ofin[:], oall[:], oall[:], ALU.mult, [r_oall], [r_ofin])
        kb.op("dve", lambda e, o=ss[:], i=ofin[:]: e.reduce_sum(o, i, axis=AX.X), reads=[r_ofin], writes=[r_ss])
        c.act(ss[:], ss[:], AF.Sqrt, [r_ss], [r_ss], bias=EPS, scale=1.0 / 128)
        c.recip(ss[:], ss[:], [r_ss], [r_ss])
        c.tt(oall[:], oall[:], ss[:].unsqueeze(2).to_broadcast([128, 16, 128]), ALU.mult, [r_ss], [r_oall])
        c.tt(ofin[:], oall[:], sgt[:], ALU.mult, [r_oall, r_sg], [r_ofin])

        def put(q0, nj, pb, rb, h=h):
            k_ = nput[0] % 2
            nput[0] += 1
            c.act(cst_[k_][:, 0:nj * 128], pb, AF.Copy, [rb, c.r_const], [r_cst_[k_]], scale=c.v("hgn"))
            kb.dma("sp", catd[h * 128:(h + 1) * 128, q0 * 128:(q0 + nj) * 128], cst_[k_][:, 0:nj * 128],
                   reads=[r_cst_[k_]], writes=[r_catd], join=True)
        transpose_tok_to_fm(c, ofin, r_ofin, 16, put, bank0=b0)
        yield


def moba_heads(c, hT, r_hT, catd, r_catd, w_in, wpool, heads, b0):
    kb = c.kb
    G = 512
    qnT = kb.sbuf("m_qnT", [128, T], BF16)
    knT = kb.sbuf("m_knT", [128, T], BF16)
    qnf = kb.sbuf("m_qnf", [128, T], F32)
    knf = kb.sbuf("m_knf", [128, T], F32)
    ksum = kb.sbuf("m_ksum", [128, 8], F32)
    vext = kb.sbuf("m_vext", [128, 16, 132], BF16)
    negT = kb.sbuf("m_negT", [8, T], BF16)
    mofin = kb.sbuf("m_ofin", [128, 16, 128], BF16)
    r_qnT, r_knT, r_qnf, r_knf, r_ksum, r_vext, r_negT, r_mofin = [Res() for _ in range(8)]
    raw = [kb.sbuf("m_raw%d" % i, [128, G], F32) for i in range(2)]
    sqb = [kb.sbuf("m_sq%d" % i, [128, G], BF16) for i in range(2)]
    rsb = [kb.sbuf("m_rs%d" % i, [128, G], F32) for i in range(2)]
    r_raw = [Res() for _ in range(2)]
    r_sqb = [Res() for _ in range(2)]
    r_rsb = [Res() for _ in range(2)]
    gm = kb.sbuf("m_gm", [128, 8], F32)
    mx = kb.sbuf("m_mx", [128, 8], F32)
    negm = kb.sbuf("m_negm", [128, 8], F32)
    negs = kb.sbuf("m_negs", [8, 128], F32)
    r_gm, r_mx, r_negm, r_negs = [Res() for _ in range(4)]
    PT = [kb.sbuf("m_PT%d" % i, [128, 256], BF16) for i in range(3)]
    r_PT = [Res() for _ in range(3)]
    rinv = kb.sbuf("m_rinv", [128, 2], F32)
    r_rinv = Res()
    c.memset(vext[:], 1.0, [r_vext])
    ones = c.cb("ones")
    identb = c.cb("ident")
    tiles = []
    for h in heads:
        for blk in range(3):
            tiles.append((w_in, 4096 + blk * 1024 + h * 128, 128, D))
    ws = WStream(c, tiles, pool=wpool)
    cst_ = [kb.sbuf("m_cst%d" % i, [128, 512], BF16) for i in range(2)]
    r_cst_ = [Res() for _ in range(2)]
    nput = [0]
    scale = 128 ** -0.5
    it = 0
    for hi_, h in enumerate(heads):
        for (wi3, wn, dstb, r_dstb, dstf, r_dstf) in ((3 * hi_, "qn", qnT, r_qnT, qnf, r_qnf), (3 * hi_ + 1, "kn", knT, r_knT, knf, r_knf)):
            wt, rw = ws.get(wi3)
            for g in range(4):
                s = it % 2
                it += 1
                sl = slice(g * G, (g + 1) * G)
                bp, bs = b0, b0 + 1
                for kc in range(KC):
                    c.mm(c.banks[bp][:, :], wt[:, kc, :], hT[:, kc, sl], kc == 0, kc == KC - 1, [rw, r_hT], [c.bres[bp]])
                c.act(raw[s][:], c.banks[bp][:, :], AF.Copy, [c.bres[bp]], [r_raw[s]])
                c.act(sqb[s][:], c.banks[bp][:, :], AF.Square, [c.bres[bp]], [r_sqb[s]])
                c.mm(c.banks[bs][:, :], ones, sqb[s][:], True, True, [r_sqb[s], c.r_const], [c.bres[bs]])
                c.act(rsb[s][:], c.banks[bs][:, :], AF.Sqrt, [c.bres[bs]], [r_rsb[s]], bias=EPS, scale=1.0 / 128)
                c.recip(rsb[s][:], rsb[s][:], [r_rsb[s]], [r_rsb[s]])
                c.stt(dstf[:, sl], raw[s][:], c.v(wn), rsb[s][:], ALU.mult, ALU.mult, [r_raw[s], r_rsb[s], c.r_const], [r_dstf])
                c.copy(dstb[:, sl], dstf[:, sl], [r_dstf], [r_dstb])
                yield
        kb.op("dve", lambda e, o=ksum[:], i=knf[:].rearrange("p (n j) -> p n j", j=256): e.reduce_sum(o, i, axis=AX.X),
              reads=[r_knf], writes=[r_ksum])
        wv, rv = ws.get(3 * hi_ + 2)
        for j in range(16):
            bk = b0 + 2 + (j % 2)
            tsl = slice(j * 128, (j + 1) * 128)
            for kc in range(KC):
                c.mm(c.banks[bk][:, 0:128], hT[:, kc, tsl], wv[:, kc, :], kc == 0, kc == KC - 1, [rv, r_hT], [c.bres[bk]])
            c.copy(vext[:, j, 0:128], c.banks[bk][:, 0:128], [c.bres[bk]], [r_vext])
            yield
        for i in range(8, 16):
            jb = i // 2
            tsl = slice(i * 128, (i + 1) * 128)
            bk = b0 + 2
            c.mm(c.banks[bk][:, 0:8], qnf[:, tsl], ksum[:], True, True, [r_qnf, r_ksum], [c.bres[bk]])
            c.memset(gm[:], -1e30, [r_gm])
            c.copy(gm[:, 0:jb], c.banks[bk][:, 0:jb], [c.bres[bk]], [r_gm])
            kb.op("dve", lambda e, o=mx[:], i_=gm[:]: e.max(o, i_), reads=[r_gm], writes=[r_mx])
            c.memset(negm[:], 0.0, [r_negm])
            c.ts(negm[:, 0:jb], gm[:, 0:jb], mx[:, 2:3], NEG, ALU.is_lt, ALU.mult, [r_gm, r_mx], [r_negm])
            c.tr(c.banks[b0 + 3][0:8, 0:128], negm[:], c.identf[:], [r_negm, c.r_const], [c.bres[b0 + 3]])
            c.copy(negT[:, tsl], c.banks[b0 + 3][0:8, 0:128], [c.bres[b0 + 3]], [r_negT])
            yield
        ip = 0
        for jb in range(8):
            qsl = slice(jb * 256, (jb + 1) * 256)
            nkt = 2 * jb + 2
            bo = [b0, b0 + 1]
            for kt in range(nkt):
                n = kt // 2
                bst = b0 + 2 + (ip % 2)
                p = ip % 3
                ip += 1
                own = (n == jb)
                need_mask = (not own) and jb >= 4
                c.mm(c.banks[bst][:, 0:256], knT[:, kt * 128:(kt + 1) * 128], qnT[:, qsl], True, not (own or need_mask),
                     [r_knT, r_qnT], [c.bres[bst]])
                if need_mask:
                    o8, _ = CST["esel8"]
                    c.mm(c.banks[bst][:, 0:256], c.cbf[0:8, o8 + n * 128:o8 + (n + 1) * 128], negT[:, qsl], False, True,
                         [c.r_const, r_negT], [c.bres[bst]])
                if own:
                    cm = c.cb("cmaskA") if kt == 2 * jb else c.cb("cmaskB")
                    c.mm(c.banks[bst][:, 0:256], identb, cm, False, True, [c.r_const], [c.bres[bst]])
                c.act(PT[p][:], c.banks[bst][:, 0:256], AF.Exp, [c.bres[bst]], [r_PT[p]], scale=scale)
                for th in range(2):
                    if kt == 2 * jb + 1 and th == 0:
                        continue
                    last = (kt == 2 * jb) if th == 0 else (kt == 2 * jb + 1)
                    c.mm(c.banks[bo[th]][:, 0:129], PT[p][:, th * 128:(th + 1) * 128], vext[:, kt, 0:129], kt == 0, last,
                         [r_PT[p], r_vext], [c.bres[bo[th]]], signal=True)
                yield
            for th in range(2):
                i = 2 * jb + th
                c.recip(rinv[:, th:th + 1], c.banks[bo[th]][:, 128:129], [c.bres[bo[th]]], [r_rinv])
                c.ts(mofin[:, i, :], c.banks[bo[th]][:, 0:128], rinv[:, th:th + 1], None, ALU.mult, None,
                     [c.bres[bo[th]], r_rinv], [r_mofin])

        def put(q0, nj, pb, rb, h=h):
            k_ = nput[0] % 2
            nput[0] += 1
            c.act(cst_[k_][:, 0:nj * 128], pb, AF.Copy, [rb], [r_cst_[k_]])
            kb.dma("sp", catd[(8 + h) * 128:(9 + h) * 128, q0 * 128:(q0 + nj) * 128], cst_[k_][:, 0:nj * 128],
                   reads=[r_cst_[k_]], writes=[r_catd], join=True)
        transpose_tok_to_fm(c, mofin, r_mofin, 16, put, bank0=b0 + 2)
        yield


def build(stages=("l0", "ffn0", "l1", "ffn1"), debug=False):
    nc = bass.Bass("TRN2", target_bir_lowering=False)
    dbg = nc.dram_tensor("dbg", [2048, T], F32, kind="ExternalOutput").ap() if debug else None

    def din(name, shape):
        return nc.dram_tensor(name, list(shape), F32, kind="ExternalInput").ap()
    xT = din("xT", [D, T])
    cst = din("cst", [128, NCST])
    vec = din("vec", [128, NVEC])
    ssmn_rep = din("ssmn_rep", [128, 4096])
    w_in_even = din("w_in_even", [D, 7168])
    w_out_even = din("w_out_even", [2048, D])
    w_in_ssm = din("w_in_ssm", [D, 10304])
    w_out_ssm = din("w_out_ssm", [4096, D])
    wg = [din("w_gate%d" % l, [D, DFF]) for l in range(2)]
    wu = [din("w_up%d" % l, [D, DFF]) for l in range(2)]
    wd = [din("w_down%d" % l, [DFF, D]) for l in range(2)]
    names = {"l0": "x1T", "ffn0": "x2T", "l1": "x3T", "ffn1": "yT"}
    last = stages[-1]
    bufs = {}
    for st in ("l0", "ffn0", "l1", "ffn1"):
        kind = "ExternalOutput" if st == last else "Internal"
        bufs[st] = nc.dram_tensor(names[st], [D, T], F32, kind=kind).ap()
    gyT = nc.dram_tensor("gyT", [4096, T], BF16, kind="Internal").ap()
    catd = nc.dram_tensor("catd", [2048, T], BF16, kind="Internal").ap()
    acs_d = nc.dram_tensor("acs_d", [64, T], F32, kind="Internal").ap()
    c = Ctx(nc)
    kb = c.kb
    c.load_consts(cst, vec)
    r_out = {st: Res(names[st]) for st in bufs}
    src = xT
    first = stages[0]
    order = ["l0", "ffn0", "l1", "ffn1"]
    for st in order[order.index(first):order.index(last) + 1]:
        if st == "l0":
            l0_mixer(c, src, bufs[st], r_out[st], w_in_even, w_out_even, cst, catd, dbg)
        elif st == "ffn0":
            ffn_phase(c, src, bufs[st], r_out[st], "ln_ffn0", wg[0], wu[0], wd[0])
        elif st == "l1":
            from_l1 = globals().get("l1_mixer")
            from_l1(c, src, bufs[st], r_out[st], w_in_ssm, w_out_ssm, cst, gyT, ssmn_rep, acs_d)
        elif st == "ffn1":
            ffn_phase(c, src, bufs[st], r_out[st], "ln_ffn1", wg[1], wu[1], wd[1])
        src = bufs[st]
    kb.barrier()
    kb.finish()
    kb.close()
    return nc, names[last]


_CACHE = {}


def run_stages(inputs, stages, xT_all, ncores=8, debug=False):
    key = tuple(stages)
    if key not in _CACHE:
        _CACHE[key] = build(stages, debug)
    nc, oname = _CACHE[key]
    cst = _make_cst()
    vec = _make_vec(inputs)
    f = lambda a: np.ascontiguousarray(a, dtype=np.float32)
    shared = {
        "cst": cst, "vec": vec,
        "ssmn_rep": np.ascontiguousarray(np.broadcast_to(f(inputs["ssm_norm"][0]).reshape(1, 4096), (128, 4096))),
        "w_in_even": f(inputs["w_in_even"][0]), "w_out_even": f(inputs["w_out_even"][0]),
        "w_in_ssm": f(inputs["w_in_ssm"][0]), "w_out_ssm": f(inputs["w_out_ssm"][0]),
    }
    for l in range(2):
        shared["w_gate%d" % l] = f(inputs["w_gate"][l])
        shared["w_up%d" % l] = f(inputs["w_up"][l])
        shared["w_down%d" % l] = f(inputs["w_down"][l])
    in_maps = []
    for i in range(ncores):
        d = dict(shared)
        d["xT"] = np.ascontiguousarray(xT_all[i])
        in_maps.append(d)
    import os
    if os.environ.get("K_TRACE"):
        res = run_bass_kernel_spmd(nc, in_maps, core_ids=list(range(ncores)), trace=True)
        print("EXEC_TIME_NS", res.exec_time_ns)
    else:
        res = run_bass_kernel_spmd(nc, in_maps, core_ids=list(range(ncores)))
    if debug:
        return np.stack([r[oname] for r in res.results], 0), res.results[0]["dbg"]
    return np.stack([r[oname] for r in res.results], 0)


def kernel(**inputs):
    x = np.asarray(inputs["x"], dtype=np.float32)
    xT = np.ascontiguousarray(x.transpose(0, 2, 1))
    yT = run_stages(inputs, ("l0", "ffn0", "l1", "ffn1"), xT)
    return np.ascontiguousarray(yT.transpose(0, 2, 1)).astype(np.float32)


def l1_mixer(c, xT, x3T, r_x3T, w_in, w_out, cst, gyT, ssmn_rep, acs_d):
    kb = c.kb
    m = kb.mark()
    hT = kb.sbuf("b_hT", [128, KC, T], BF16)
    r_hT = Res()
    rmsnorm_fm(c, xT, "ln_mix1", hT, r_hT, 0, T, "bn_")
    c.load_mscan(cst, "m256")
    c.alloc_w(16 * 128)
    G = 512
    r_gyT = Res("gyT")
    identb = c.cb("ident")
    r_acsd = Res("acs_d")
    dt_tok = kb.sbuf("b_dttok", [128, 16, 64], F32)
    nacs_tok = kb.sbuf("b_nacstok", [128, 16, 64], F32)
    eacs_tok = kb.sbuf("b_eacstok", [128, 16, 64], F32)
    dec_tok = kb.sbuf("b_dectok", [128, 16, 64], F32)
    eal_b = kb.sbuf("b_ealb", [128, 8, 64], F32)
    m_dt = kb.mark()
    dtT = kb.sbuf("b_dtT", [64, T], F32)
    acsT = kb.sbuf("b_acsT", [64, T], F32)
    negA = kb.sbuf("b_negA", [64, 1], F32)
    diag = kb.sbuf("b_diag", [64, 64], F32)
    etmp = kb.sbuf("b_etmp", [64, G], F32)
    r_acsT, r_dtT, r_dttok, r_acstok, r_eacs, r_dec, r_eal, r_negA, r_diag, r_etmp = [Res() for _ in range(10)]
    c.act(negA[:], c.v("alog", rows=64), AF.Exp, [c.r_const], [r_negA])
    c.ts(negA[:], negA[:], -1.0, None, ALU.mult, None, [r_negA], [r_negA])
    wdt, rwdt = c.load_w(w_in, 10240, 64, D)
    for g in range(4):
        sl = slice(g * G, (g + 1) * G)
        bk = g % 2
        for kc in range(KC):
            c.mm(c.banks[bk][0:64, :], wdt[:, kc, :], hT[:, kc, sl], kc == 0, kc == KC - 1, [rwdt, r_hT], [c.bres[bk]])
        c.act(etmp[:], c.banks[bk][0:64, :], AF.Exp, [c.bres[bk], c.r_const], [r_etmp], bias=c.v("dtb", rows=64))
        c.act(dtT[:, sl], etmp[:], AF.Ln, [r_etmp], [r_dtT], bias=1.0)
        c.ts(etmp[:], dtT[:, sl], negA[:, 0:1], None, ALU.mult, None, [r_dtT, r_negA], [r_etmp])
        kb.op("dve", lambda e, o=acsT[:, sl], d0=c.mscan[0:64, 0:G], d1=etmp[:]: e.tensor_tensor_scan(o, d0, d1, 0.0, ALU.mult, ALU.add),
              reads=[r_etmp, c.r_mscan], writes=[r_acsT])
        kb.dma("sp", acs_d[:, sl], acsT[:, sl], reads=[r_acsT], writes=[r_acsd], join=True)
    for j in range(16):
        tsl = slice(j * 128, (j + 1) * 128)
        bk = 6 + (j % 2)
        c.tr(c.banks[bk][:, 0:64], dtT[0:64, tsl], c.identf[0:64, 0:64], [r_dtT, c.r_const], [c.bres[bk]])
        c.tr(c.banks[bk][:, 64:128], acsT[0:64, tsl], c.identf[0:64, 0:64], [r_acsT, c.r_const], [c.bres[bk]])
        c.copy(dt_tok[:, j, :], c.banks[bk][:, 0:64], [c.bres[bk]], [r_dttok])
        c.ts(nacs_tok[:, j, :], c.banks[bk][:, 64:128], -1.0, None, ALU.mult, None, [c.bres[bk]], [r_acstok])
    c.act(eacs_tok[:], nacs_tok[:], AF.Exp, [r_acstok], [r_eacs], scale=-1.0)
    for ci in range(8):
        col = ci * 256 + 255
        c.ts(diag[:], c.identf[0:64, 0:64], acsT[0:64, col:col + 1], None, ALU.mult, None, [c.r_const, r_acsT], [r_diag])
        bk = 4 + (ci % 2)
        c.mm(c.banks[bk][:, 0:64], c.onesf[0:64, :], diag[:], True, True, [c.r_const, r_diag], [c.bres[bk]])
        c.act(eal_b[:, ci, :], c.banks[bk][:, 0:64], AF.Exp, [c.bres[bk]], [r_eal])
        for i in range(2):
            c.tt(dec_tok[:, 2 * ci + i, :], c.banks[bk][:, 0:64], nacs_tok[:, 2 * ci + i, :], ALU.add,
                 [c.bres[bk], r_acstok], [r_dec])
    c.act(dec_tok[:], dec_tok[:], AF.Exp, [r_dec], [r_dec])
    kb.barrier()
    kb.release(m_dt)

    xs_tok = kb.sbuf("b_xs", [128, 16, 512], BF16)
    sz_tok = kb.sbuf("b_sz", [128, 16, 512], BF16)
    ssmn_b = kb.sbuf("b_ssmnb", [128, 512], F32)
    r_ssmnb = Res()
    B_tok = kb.sbuf("b_Btok", [128, 16, 128], BF16)
    BT = kb.sbuf("b_BT", [128, T], BF16)
    CT = kb.sbuf("b_CT", [128, T], BF16)
    stT = kb.sbuf("b_stT", [128, 512], F32)
    stTb = kb.sbuf("b_stTb", [128, 512], BF16)
    sttmp = kb.sbuf("b_sttmp", [128, 512], F32)
    r_xs, r_sz, r_Btok, r_BT, r_CT, r_stT, r_stTb, r_sttmp = [Res() for _ in range(8)]
    cmul = [c.cb("cmul0"), c.cb("cmul1")]

    tiles = []
    for gi in range(8):
        for ft in range(4):
            tiles.append((w_in, 4096 + gi * 512 + ft * 128, 128, D))
        tiles.append((w_in, 8192 + gi * 128, 128, D))
        tiles.append((w_in, 9216 + gi * 128, 128, D))
    ws = WStream(c, tiles)
    pit = 0
    for gi in range(8):
        hs = slice(gi * 8, (gi + 1) * 8)
        mA = kb.mark()
        wz_slab = kb.sbuf("b_wz", [128, 16, 512], BF16)
        featT = kb.sbuf("b_featT", [128, T], BF16)
        raw = kb.sbuf("b_raw", [128, T + 4], F32)
        acc = [kb.sbuf("b_acc%d" % i, [128, 512], F32) for i in range(2)]
        r_wz, r_rawpad = Res(), Res()
        r_featT = [Res() for _ in range(4)]
        r_raw = [Res() for _ in range(4)]
        r_acc = [Res() for _ in range(2)]
        c.memset(raw[:, 0:3], 0.0, [r_rawpad])
        zsrc = w_in[:, gi * 512:(gi + 1) * 512].rearrange("(k p) n -> p k n", p=128)
        for k0 in range(0, 16, 4):
            kb.dma("pool", wz_slab[:, k0:k0 + 4, :], zsrc[:, k0:k0 + 4, :], writes=[r_wz], join=(k0 > 0))

        kb.dma("sp", ssmn_b[:], ssmn_rep[:, gi * 512:(gi + 1) * 512], writes=[r_ssmnb])

        def emit_z(j):
            bk = 4 + (j % 2)
            for kc in range(KC):
                c.mm(c.banks[bk][:, :], hT[:, kc, j * 128:(j + 1) * 128], wz_slab[:, kc, :], kc == 0, kc == KC - 1,
                     [r_wz, r_hT], [c.bres[bk]])
            c.act(sz_tok[:, j, :], c.banks[bk][:, :], AF.Silu, [c.bres[bk]], [r_sz])

        def fm_tile(widx, ch_tile, dst, r_dst4):
            nonlocal pit
            wt, rw = ws.get(widx)
            cw = lambda k: c.v("convw", c0=ch_tile * 4 + k, n=1)
            for g in range(4):
                bk = pit % 2
                ab = pit % 2
                pit += 1
                sl = slice(g * G, (g + 1) * G)
                for kc in range(KC):
                    c.mm(c.banks[bk][:, :], wt[:, kc, :], hT[:, kc, sl], kc == 0, kc == KC - 1, [rw, r_hT], [c.bres[bk]])
                c.act(raw[:, 3 + g * G:3 + (g + 1) * G], c.banks[bk][:, :], AF.Copy, [c.bres[bk]], [r_raw[g]])
                rr = [r_raw[g], c.r_const] + ([r_raw[g - 1]] if g > 0 else [r_rawpad])
                c.ts(acc[ab][:], raw[:, g * G:(g + 1) * G], cw(0), c.v("convb", c0=ch_tile, n=1), ALU.mult, ALU.add, rr, [r_acc[ab]])
                for k in range(1, 4):
                    c.stt(acc[ab][:], raw[:, g * G + k:(g + 1) * G + k], cw(k), acc[ab][:], ALU.mult, ALU.add,
                          rr + [r_acc[ab]], [r_acc[ab]])
                c.act(dst[:, sl], acc[ab][:], AF.Silu, [r_acc[ab]], [r_dst4[g]])

        for ft in range(4):
            fm_tile(gi * 6 + ft, gi * 4 + ft, featT, r_featT)

            def putx(q0, nj, pb, rb, ft=ft):
                c.copy(xs_tok[:, q0:q0 + nj, ft * 128:(ft + 1) * 128], pb.rearrange("p (j f) -> p j f", f=128), [rb], [r_xs])
            transpose_fm_to_tok(c, featT, r_featT, putx)
        fm_tile(gi * 6 + 4, 32 + gi, BT, [r_BT] * 4)

        def putb(q0, nj, pb, rb):
            c.copy(B_tok[:, q0:q0 + nj, :], pb.rearrange("p (j f) -> p j f", f=128), [rb], [r_Btok])
        transpose_fm_to_tok(c, BT, [r_BT] * 4, putb)
        fm_tile(gi * 6 + 5, 40 + gi, CT, [r_CT] * 4)
        for j in range(16):
            emit_z(j)
        kb.barrier()
        kb.release(mA)
        mB = kb.mark()
        xD = [kb.sbuf("b_xD%d" % i, [128, 512], BF16) for i in range(4)]
        xdt = [kb.sbuf("b_xdt%d" % i, [128, 512], BF16) for i in range(4)]
        xdd = [kb.sbuf("b_xdd%d" % i, [128, 512], BF16) for i in range(4)]
        cbm = [kb.sbuf("b_cbm%d" % i, [128, 256], BF16) for i in range(4)]
        Lb = [kb.sbuf("b_L%d" % i, [128, 256], BF16) for i in range(4)]
        Gb = [kb.sbuf("b_G%d" % i, [128, 256], BF16) for i in range(4)]
        t1 = [kb.sbuf("b_t1%d" % i, [128, 512], F32) for i in range(4)]
        gyn = [kb.sbuf("b_gyn%d" % i, [128, 512], BF16) for i in range(2)]
        ssg = [kb.sbuf("b_ssg%d" % i, [128, 1], F32) for i in range(2)]
        gst = [kb.sbuf("b_gst%d" % i, [128, 4, 256], BF16) for i in range(2)]
        NAR = 6
        arow_sb = [kb.sbuf("b_arow%d" % i, [128, 256], F32) for i in range(NAR)]
        r_arow = [Res() for _ in range(NAR)]
        r_xD = [Res() for _ in range(4)]
        r_xdt = [Res() for _ in range(4)]
        r_xdd = [Res() for _ in range(4)]
        r_cbm = [Res() for _ in range(4)]
        r_L = [Res() for _ in range(4)]
        r_G = [Res() for _ in range(4)]
        r_t1 = [Res() for _ in range(4)]
        r_gyn = [Res() for _ in range(2)]
        r_ssg = [Res() for _ in range(2)]
        r_gst = [Res() for _ in range(2)]
        if gi == 0:
            print('MAMBA sbuf remaining (loop scope)', c.nc.sbuf_bytes_remaining)
        c.memset(stT[:], 0.0, [r_stT])
        c.memset(stTb[:], 0.0, [r_stTb])
        v3 = lambda ap: ap.rearrange("p (h d) -> p h d", d=64)

        def emit_arow(idx):
            ci_, hh_ = divmod(idx, 8)
            H_ = gi * 8 + hh_
            k_ = idx % NAR
            kb.dma("sp", arow_sb[k_][:], acs_d[H_:H_ + 1, ci_ * 256:(ci_ + 1) * 256].to_broadcast([128, 256]),
                   reads=[r_acsd], writes=[r_arow[k_]])

        def prologue(ci):
            pp = ci % 2
            csl = slice(ci * 256, (ci + 1) * 256)
            yb0 = 2 if pp else 4
            for i in range(2):
                j = 2 * ci + i
                b_ = pp * 2 + i
                c.tt(v3(xdt[b_][:]), v3(xs_tok[:, j, :]), dt_tok[:, j, hs].unsqueeze(2).to_broadcast([128, 8, 64]), ALU.mult,
                     [r_xs, r_dttok], [r_xdt[b_]])
                c.tt(v3(xdd[b_][:]), v3(xdt[b_][:]), dec_tok[:, j, hs].unsqueeze(2).to_broadcast([128, 8, 64]), ALU.mult,
                     [r_xdt[b_], r_dec], [r_xdd[b_]], eng="pool")
                c.tt(v3(xD[b_][:]), v3(xs_tok[:, j, :]), c.v("dskip")[:, hs].unsqueeze(2).to_broadcast([128, 8, 64]), ALU.mult,
                     [r_xs, c.r_const], [r_xD[b_]], eng="pool")
                c.mm(c.banks[0][:, i * 256:(i + 1) * 256], BT[:, ci * 256 + i * 128:ci * 256 + (i + 1) * 128], CT[:, csl], True, True,
                     [r_BT, r_CT], [c.bres[0]])
                c.tt(cbm[b_][:], c.banks[0][:, i * 256:(i + 1) * 256], cmul[i], ALU.mult, [c.bres[0], c.r_const], [r_cbm[b_]])
            for tt_ in range(2):
                j = 2 * ci + tt_
                b_ = pp * 2 + tt_
                c.mm(c.banks[1][:, :], CT[:, ci * 256 + tt_ * 128:ci * 256 + (tt_ + 1) * 128], stTb[:], True, True,
                     [r_CT, r_stTb], [c.bres[1]])
                c.tt(v3(t1[b_][:]), v3(c.banks[1][:, :]), eacs_tok[:, j, hs].unsqueeze(2).to_broadcast([128, 8, 64]), ALU.mult,
                     [c.bres[1], r_eacs], [r_t1[b_]])
                c.mm(c.banks[yb0 + tt_][:, :], identb, xD[b_][:], True, False, [c.r_const, r_xD[b_]], [c.bres[yb0 + tt_]], signal=True)

        def heads(ci):
            pp = ci % 2
            yb0 = 2 if pp else 4
            if ci == 0:
                for idx in range(NAR - 1):
                    emit_arow(idx)
            for hh in range(8):
                H = gi * 8 + hh
                idx = ci * 8 + hh
                ka = idx % NAR
                if idx + NAR - 1 < 64:
                    emit_arow(idx + NAR - 1)
                l0, l1 = (hh % 2) * 2, (hh % 2) * 2 + 1
                g0 = (2 * hh) % 4
                g1 = (2 * hh + 1) % 4
                c.act(Lb[l0][:], arow_sb[ka][:, 0:256], AF.Exp, [r_arow[ka], r_acstok], [r_L[l0]], bias=nacs_tok[:, 2 * ci, H:H + 1])
                c.stt(Gb[g0][:], Lb[l0][:], 1.0, cbm[pp * 2][:], ALU.min, ALU.mult, [r_L[l0], r_cbm[pp * 2]], [r_G[g0]])
                c.act(Lb[l1][:, 0:128], arow_sb[ka][:, 128:256], AF.Exp, [r_arow[ka], r_acstok], [r_L[l1]],
                      bias=nacs_tok[:, 2 * ci + 1, H:H + 1])
                c.stt(Gb[g1][:, 0:128], Lb[l1][:, 0:128], 1.0, cbm[pp * 2 + 1][:, 128:256], ALU.min, ALU.mult,
                      [r_L[l1], r_cbm[pp * 2 + 1]], [r_G[g1]])
                hsl = slice(hh * 64, (hh + 1) * 64)
                lasth = (hh == 7)
                x0, x1 = xdt[pp * 2], xdt[pp * 2 + 1]
                rx0, rx1 = r_xdt[pp * 2], r_xdt[pp * 2 + 1]
                c.mm(c.banks[yb0][:, hsl], Gb[g0][:, 0:128], x0[:, hsl], False, lasth, [r_G[g0], rx0], [c.bres[yb0]], signal=True)
                c.mm(c.banks[yb0 + 1][:, hsl], Gb[g0][:, 128:256], x0[:, hsl], False, False, [r_G[g0], rx0], [c.bres[yb0 + 1]])
                c.mm(c.banks[yb0 + 1][:, hsl], Gb[g1][:, 0:128], x1[:, hsl], False, lasth, [r_G[g1], rx1], [c.bres[yb0 + 1]], signal=True)

        def combine_a(ci, tt_):
            j = 2 * ci + tt_
            a = tt_
            ti = (ci % 2) * 2 + tt_
            yb = (2 if ci % 2 else 4) + tt_
            c.tt(t1[ti][:], t1[ti][:], c.banks[yb][:, :], ALU.add, [c.bres[yb]], [r_t1[ti]])
            c.tt(t1[ti][:], t1[ti][:], sz_tok[:, j, :], ALU.mult, [r_sz], [r_t1[ti]])
            c.act(gyn[a][:], t1[ti][:], AF.Square, [r_t1[ti]], [r_gyn[a], r_ssg[a]], accum_out=ssg[a][:])
            c.act(ssg[a][:], ssg[a][:], AF.Ln, [r_ssg[a]], [r_ssg[a]], bias=EPS, scale=1.0 / 512)
            c.act(ssg[a][:], ssg[a][:], AF.Exp, [r_ssg[a]], [r_ssg[a]], scale=-0.5)

        def combine_b(ci, tt_):
            a = tt_
            ti = (ci % 2) * 2 + tt_
            csl = slice(ci * 256, (ci + 1) * 256)
            c.stt(gyn[a][:], t1[ti][:], ssg[a][:, 0:1], ssmn_b[:], ALU.mult, ALU.mult, [r_t1[ti], r_ssg[a], r_ssmnb], [r_gyn[a]])
            bt = 7
            pb = c.banks[bt][:, :].bitcast(BF16)
            for ft in range(4):
                c.tr(pb[:, ft * 128:(ft + 1) * 128], gyn[a][:, ft * 128:(ft + 1) * 128], identb, [r_gyn[a], c.r_const], [c.bres[bt]])
            gs = ci % 2
            c.act(gst[gs][:, :, tt_ * 128:(tt_ + 1) * 128], pb[:, 0:512].rearrange("p (f t) -> p f t", t=128), AF.Copy,
                  [c.bres[bt]], [r_gst[gs]])
            if tt_ == 1:
                for ft in range(4):
                    r0 = gi * 512 + ft * 128
                    kb.dma("sp", gyT[r0:r0 + 128, csl], gst[gs][:, ft, :], reads=[r_gst[gs]], writes=[r_gyT], join=True)

        def state_update(ci):
            pp = ci % 2
            for i in range(2):
                c.mm(c.banks[6][:, :], B_tok[:, 2 * ci + i, :], xdd[pp * 2 + i][:], i == 0, i == 1,
                     [r_Btok, r_xdd[pp * 2 + i]], [c.bres[6]])
            c.tt(v3(sttmp[:]), v3(stT[:]), eal_b[:, ci, hs].unsqueeze(2).to_broadcast([128, 8, 64]), ALU.mult,
                 [r_stT, r_eal], [r_sttmp])
            c.tt(stT[:], sttmp[:], c.banks[6][:, :], ALU.add, [r_sttmp, c.bres[6]], [r_stT])
            c.act(stTb[:], stT[:], AF.Copy, [r_stT], [r_stTb])

        prologue(0)
        for ci in range(8):
            heads(ci)
            combine_a(ci, 0)
            combine_a(ci, 1)
            state_update(ci)
            if ci + 1 < 8:
                prologue(ci + 1)
            combine_b(ci, 0)
            combine_b(ci, 1)
        kb.barrier()
        kb.release(mB)
    kb.barrier()
    kb.release(m)
    m = kb.mark()
    TN = 1024
    gyh = kb.sbuf("b_gyh", [128, 32, TN], BF16)
    r_gyh = Res()
    xr = [kb.sbuf("b_xr%d" % i, [128, TN], F32) for i in range(2)]
    r_xr = [Res() for _ in range(2)]
    c.alloc_w(32 * 128)
    it = 0
    for th in range(2):
        t0 = th * TN
        gsrc = gyT[:, t0:t0 + TN].rearrange("(k p) t -> p k t", p=128)
        for k0 in range(0, 32, 8):
            kb.dma("sp", gyh[:, k0:k0 + 8, :], gsrc[:, k0:k0 + 8, :], reads=[r_gyT], writes=[r_gyh], join=(k0 > 0))
        ws = WStream(c, [(w_out, jo * 128, 128, 4096) for jo in range(16)])
        for jo in range(16):
            wt, rw = ws.get(jo)
            b = jo % 2
            kb.dma("sp", xr[b][:], xT[jo * 128:(jo + 1) * 128, t0:t0 + TN], writes=[r_xr[b]])
            for g in range(TN // 512):
                bk = it % 4
                it += 1
                sl = slice(g * 512, (g + 1) * 512)
                for kc in range(32):
                    c.mm(c.banks[bk][:, :], wt[:, kc, :], gyh[:, kc, sl], kc == 0, kc == 31, [rw, r_gyh], [c.bres[bk]])
                c.tt(xr[b][:, sl], xr[b][:, sl], c.banks[bk][:, :], ALU.add, [c.bres[bk]], [r_xr[b]])
            kb.dma("sp", x3T[jo * 128:(jo + 1) * 128, t0:t0 + TN], xr[b][:], reads=[r_xr[b]], writes=[r_x3T], join=True)
    kb.barrier()
    kb.release(m)


def transpose_fm_to_tok(c, srcT, r_src, dst_fn):
    ident = c.cb("ident")
    for q in range(0, 16, 4):
        bk = 6 + ((q // 4) % 2)
        pb = c.banks[bk][:, :].bitcast(BF16)
        for j in range(4):
            c.tr(pb[:, j * 128:(j + 1) * 128], srcT[:, (q + j) * 128:(q + j + 1) * 128], ident, [r_src[q // 4], c.r_const], [c.bres[bk]])
        dst_fn(q, 4, pb[:, 0:512], c.bres[bk])
```
